# Optimizing a Trainium2 kernel written in Bass

```python
import math, functools
import jax, jax.numpy as jnp
from jax import lax
import numpy as np

D_MODEL = 1024
BATCH = 8
SEQ = 2048
DEPTH = 1
DEC_BATCH = 32
DEC_SEQ = 32
PAST_LEN = 4096

CHUNK = 64
Q_BLOCK = 128
EPS = 1e-6
A_HEADS = 8
A_NOPE = 64
A_ROPE = 32
A_V = 64
A_Q_LORA = 256
A_KV_LORA = 128
ROPE_THETA = 10000.0
A_WIDTH = A_HEADS * A_V
B_HEADS = 8
B_KV_HEADS = 2
B_HEAD_DIM = 64
B_WIDTH = B_HEADS * B_HEAD_DIM
IDX_HEADS = 8
IDX_DIM = 64
TOP_K_MAX = 256
N_BUCKETS = 32
MAX_DISTANCE = 128

D_MIX = A_WIDTH + B_WIDTH
SPLITS = (A_Q_LORA, A_KV_LORA, A_ROPE, A_WIDTH,
          B_WIDTH, B_KV_HEADS * B_HEAD_DIM, B_KV_HEADS * B_HEAD_DIM,
          IDX_HEADS * IDX_DIM, IDX_DIM, IDX_HEADS, B_WIDTH)
IN_WIDTH = (A_Q_LORA + A_KV_LORA + A_ROPE + A_WIDTH + B_WIDTH + 2 * B_KV_HEADS * B_HEAD_DIM
            + IDX_HEADS * IDX_DIM + IDX_DIM + IDX_HEADS + B_WIDTH)

kernel_name = "hybrid_mla_dsa_streaming_step"


def rmsnorm(x, g):
    xf = x.astype(jnp.float32)
    y = xf * lax.rsqrt(jnp.mean(xf * xf, axis=-1, keepdims=True) + EPS)
    return (y * g.astype(jnp.float32)).astype(x.dtype)


def rope(x, pos):
    half = A_ROPE // 2
    inv = ROPE_THETA ** (-jnp.arange(half, dtype=jnp.float32) / half)
    ang = pos.astype(jnp.float32)[:, None] * inv[None, :]
    bshape = (pos.shape[0],) + (1,) * (x.ndim - 3) + (half,)
    cos = jnp.cos(ang).reshape(bshape)
    sin = jnp.sin(ang).reshape(bshape)
    xf = x.astype(jnp.float32)
    x1, x2 = xf[..., :half], xf[..., half:]
    return jnp.concatenate([x1 * cos - x2 * sin, x1 * sin + x2 * cos], axis=-1).astype(x.dtype)


def t5_bucket(rel):
    nb = N_BUCKETS // 2
    max_exact = nb // 2
    ret = (rel > 0).astype(jnp.int32) * nb
    n = jnp.abs(rel)
    nf = jnp.maximum(n, 1).astype(jnp.float32)
    large = max_exact + (jnp.log(nf / max_exact) / math.log(MAX_DISTANCE / max_exact)
                         * (nb - max_exact)).astype(jnp.int32)
    large = jnp.minimum(large, nb - 1)
    return ret + jnp.where(n < max_exact, n, large)


def chunk_admissible(pos_q, pos_k):
    return (pos_q[:, None] // CHUNK) >= (pos_k[None, :] // CHUNK)


def project(x, pos, norm_g, w_in, q_norm_g, kv_norm_g, w_uq, w_uk):
    b, t = x.shape[0], x.shape[1]
    h = rmsnorm(x, norm_g)
    z = jnp.einsum('btd,de->bte', h, w_in)
    offs = [int(o) for o in np.cumsum(SPLITS)[:-1]]
    (c_q, c_kv, k_pe, gate_a, q_b, k_b, v_b, q_idx, k_idx, w_idx, gate_b) = jnp.split(z, offs, axis=-1)
    cq = rmsnorm(c_q, q_norm_g)
    q = jnp.einsum('btc,che->bthe', cq, w_uq)
    q_nope, q_pe = q[..., :A_NOPE], rope(q[..., A_NOPE:], pos)
    q_lat = jnp.einsum('bthn,chn->bthc', q_nope, w_uk)
    ckv = rmsnorm(c_kv, kv_norm_g)
    kpe = rope(k_pe, pos)
    q_b = q_b.reshape(b, t, B_HEADS, B_HEAD_DIM)
    k_b = k_b.reshape(b, t, B_KV_HEADS, B_HEAD_DIM)
    v_b = v_b.reshape(b, t, B_KV_HEADS, B_HEAD_DIM)
    q_idx = q_idx.reshape(b, t, IDX_HEADS, IDX_DIM)
    w_idx = w_idx * (IDX_HEADS ** -0.5)
    return q_lat, q_pe, ckv, kpe, gate_a, q_b, k_b, v_b, q_idx, k_idx, w_idx, gate_b


def mla_attend(ckv, kpe, pos_k, q_lat, q_pe, pos_q):
    s = (jnp.einsum('bthc,bsc->bhts', q_lat, ckv) + jnp.einsum('bthr,bsr->bhts', q_pe, kpe)).astype(jnp.float32)
    s = s * ((A_NOPE + A_ROPE) ** -0.5)
    s = jnp.where(chunk_admissible(pos_q, pos_k)[None, None], s, -jnp.inf)
    p = jax.nn.softmax(s, axis=-1).astype(ckv.dtype)
    return jnp.einsum('bhts,bsc->bthc', p, ckv)


def dsa_attend(k, v, k_idx, pos_k, rel_bias, n_top, q, q_idx, w_idx, pos_q):
    b, tq = q.shape[0], q.shape[1]
    dots = jnp.einsum('bthd,bsd->bths', q_idx, k_idx).astype(jnp.float32) * (IDX_DIM ** -0.5)
    score = jnp.einsum('bths,bth->bts', jax.nn.relu(dots), w_idx.astype(jnp.float32))
    score = jnp.where(chunk_admissible(pos_q, pos_k)[None], score, -jnp.inf)
    _, sel = lax.top_k(score, n_top)
    gather = jax.vmap(lambda rows, ib: rows[ib])
    k_sel = gather(k, sel)
    v_sel = gather(v, sel)
    pos_sel = pos_k[sel]
    valid = (pos_sel // CHUNK) <= (pos_q[None, :, None] // CHUNK)
    bias = rel_bias[t5_bucket(pos_sel - pos_q[None, :, None])].astype(jnp.float32)
    g = B_HEADS // B_KV_HEADS
    bias = jnp.moveaxis(bias.reshape(b, tq, n_top, B_KV_HEADS, g), 2, -1)
    qg = q.reshape(b, tq, B_KV_HEADS, g, B_HEAD_DIM)
    logits = jnp.einsum('btkgd,btjkd->btkgj', qg, k_sel).astype(jnp.float32) * (B_HEAD_DIM ** -0.5) + bias
    logits = jnp.where(valid[:, :, None, None, :], logits, -jnp.inf)
    p = jax.nn.softmax(logits, axis=-1).astype(v.dtype)
    o = jnp.einsum('btkgj,btjkd->btkgd', p, v_sel)
    return o.reshape(b, tq, B_HEADS, B_HEAD_DIM)


def sweep(fn, q_args, pos_q):
    b, t = q_args[0].shape[0], q_args[0].shape[1]
    nb = t // Q_BLOCK
    blk = lambda a: jnp.moveaxis(a.reshape((b, nb, Q_BLOCK) + a.shape[2:]), 1, 0)
    xs = tuple(blk(a) for a in q_args) + (pos_q.reshape(nb, Q_BLOCK),)
    out = lax.map(lambda args: fn(*args), xs)
    return jnp.moveaxis(out, 0, 1).reshape((b, t) + out.shape[3:])


def combine(x, o_lat, o_b, gate_a, gate_b, w_uv, w_out):
    b, t = x.shape[0], x.shape[1]
    o_a = jnp.einsum('bthc,chv->bthv', o_lat, w_uv).reshape(b, t, A_WIDTH)
    mix = jnp.concatenate([o_a * jax.nn.silu(gate_a), o_b.reshape(b, t, B_WIDTH) * jax.nn.silu(gate_b)], axis=-1)
    return x + jnp.einsum('bte,ed->btd', mix, w_out)


def setup_inputs(seed: int = 0) -> dict:
    key = jax.random.key(seed)
    ks = jax.random.split(key, 20)
    f = jnp.float32
    nrm = lambda k, shape, scale: jax.random.normal(k, shape, f) * scale
    return {
        "x_prompt": nrm(ks[0], (BATCH, SEQ, D_MODEL), 1.0),
        "x_sample": nrm(ks[1], (DEC_BATCH, DEC_SEQ, D_MODEL), 1.0),
        "cache_mla_ckv": nrm(ks[2], (DEPTH, DEC_BATCH, PAST_LEN, A_KV_LORA), 1.0),
        "cache_mla_kpe": nrm(ks[3], (DEPTH, DEC_BATCH, PAST_LEN, A_ROPE), 1.0),
        "cache_dsa_k": nrm(ks[4], (DEPTH, DEC_BATCH, PAST_LEN, B_KV_HEADS, B_HEAD_DIM), 1.0),
        "cache_dsa_v": nrm(ks[5], (DEPTH, DEC_BATCH, PAST_LEN, B_KV_HEADS, B_HEAD_DIM), 1.0),
        "cache_dsa_kidx": nrm(ks[6], (DEPTH, DEC_BATCH, PAST_LEN, IDX_DIM), 1.0),
        "norm_g": 1.0 + nrm(ks[7], (DEPTH, D_MODEL), 0.01),
        "w_in": nrm(ks[8], (DEPTH, D_MODEL, IN_WIDTH), D_MODEL ** -0.5),
        "mla_q_norm_g": 1.0 + nrm(ks[9], (DEPTH, A_Q_LORA), 0.01),
        "mla_kv_norm_g": 1.0 + nrm(ks[10], (DEPTH, A_KV_LORA), 0.01),
        "mla_w_uq": nrm(ks[11], (DEPTH, A_Q_LORA, A_HEADS, A_NOPE + A_ROPE), A_Q_LORA ** -0.5),
        "mla_w_uk": nrm(ks[12], (DEPTH, A_KV_LORA, A_HEADS, A_NOPE), A_KV_LORA ** -0.5),
        "mla_w_uv": nrm(ks[13], (DEPTH, A_KV_LORA, A_HEADS, A_V), A_KV_LORA ** -0.5),
        "rel_bias": nrm(ks[14], (N_BUCKETS, B_HEADS), 0.1),
        "w_out": nrm(ks[15], (DEPTH, D_MIX, D_MODEL), D_MIX ** -0.5),
        "final_norm_g": 1.0 + nrm(ks[16], (D_MODEL,), 0.01),
    }


def reference(x_prompt, x_sample, cache_mla_ckv, cache_mla_kpe, cache_dsa_k, cache_dsa_v, cache_dsa_kidx,
              norm_g, w_in, mla_q_norm_g, mla_kv_norm_g, mla_w_uq, mla_w_uk, mla_w_uv, rel_bias, w_out,
              final_norm_g):
    t_p = x_prompt.shape[1]
    t_s = x_sample.shape[1]
    past = cache_mla_ckv.shape[2]
    pos_p = jnp.arange(t_p, dtype=jnp.int32)
    pos_s = past + jnp.arange(t_s, dtype=jnp.int32)
    pos_all = jnp.arange(past + t_s, dtype=jnp.int32)
    n_top_p = min(TOP_K_MAX, t_p // 4)
    n_top_s = min(TOP_K_MAX, (past + t_s) // 4)

    xp, xs = x_prompt, x_sample
    p_ckv, p_kpe, p_k, p_v, p_kidx = [], [], [], [], []
    s_ckv, s_kpe, s_k, s_v, s_kidx = [], [], [], [], []
    for l in range(DEPTH):
        lw = (norm_g[l], w_in[l], mla_q_norm_g[l], mla_kv_norm_g[l], mla_w_uq[l], mla_w_uk[l])
        (q_lat, q_pe, ckv, kpe, gate_a, q_b, k_b, v_b, q_idx, k_idx, w_idx, gate_b) = project(xp, pos_p, *lw)
        o_lat = sweep(functools.partial(mla_attend, ckv, kpe, pos_p), (q_lat, q_pe), pos_p)
        o_b = sweep(functools.partial(dsa_attend, k_b, v_b, k_idx, pos_p, rel_bias, n_top_p),
                    (q_b, q_idx, w_idx), pos_p)
        xp = combine(xp, o_lat, o_b, gate_a, gate_b, mla_w_uv[l], w_out[l])
        p_ckv.append(ckv); p_kpe.append(kpe); p_k.append(k_b); p_v.append(v_b); p_kidx.append(k_idx)
        (q_lat, q_pe, ckv, kpe, gate_a, q_b, k_b, v_b, q_idx, k_idx, w_idx, gate_b) = project(xs, pos_s, *lw)
        ckv_all = jnp.concatenate([cache_mla_ckv[l], ckv], axis=1)
        kpe_all = jnp.concatenate([cache_mla_kpe[l], kpe], axis=1)
        k_all = jnp.concatenate([cache_dsa_k[l], k_b], axis=1)
        v_all = jnp.concatenate([cache_dsa_v[l], v_b], axis=1)
        kidx_all = jnp.concatenate([cache_dsa_kidx[l], k_idx], axis=1)
        o_lat = mla_attend(ckv_all, kpe_all, pos_all, q_lat, q_pe, pos_s)
        o_b = dsa_attend(k_all, v_all, kidx_all, pos_all, rel_bias, n_top_s, q_b, q_idx, w_idx, pos_s)
        xs = combine(xs, o_lat, o_b, gate_a, gate_b, mla_w_uv[l], w_out[l])
        s_ckv.append(ckv); s_kpe.append(kpe); s_k.append(k_b); s_v.append(v_b); s_kidx.append(k_idx)

    y_prompt = rmsnorm(xp, final_norm_g)
    y_sample = rmsnorm(xs, final_norm_g)
    return (y_prompt, y_sample,
            jnp.stack(p_ckv), jnp.stack(p_kpe), jnp.stack(p_k), jnp.stack(p_v), jnp.stack(p_kidx),
            jnp.stack(s_ckv), jnp.stack(s_kpe), jnp.stack(s_k), jnp.stack(s_v), jnp.stack(s_kidx))
```

```python
import math
from contextlib import ExitStack

import numpy as np
import concourse.bass as bass
import concourse.mybir as mybir
from concourse.bass_utils import run_bass_kernel_spmd

F32 = mybir.dt.float32
BF16 = mybir.dt.bfloat16
ALU = mybir.AluOpType
AF = mybir.ActivationFunctionType

ENGS = ("tensor", "vector", "scalar", "gpsimd", "sync")
EPOCH = 12000
import os
LIMIT = int(os.environ.get('KLIMIT', '100000000'))

CQ, CKV, KPE, GA, QB, KB, VB, KIDX, WIDX, QIDX, GB = 0, 256, 384, 416, 928, 1440, 1568, 1696, 1760, 1768, 2280
GROUPS = [(0, 416), (416, 928), (928, 1440), (1440, 1768), (1768, 2280), (2280, 2792)]
MLA_SCALE = 96.0 ** -0.5
NEG = -30000.0
NITER = 24


class Buf:
    __slots__ = ("name", "w", "r", "const")

    def __init__(self, name):
        self.name = name
        self.w = None
        self.r = {}
        self.const = False


class Sched:
    def __init__(self, nc):
        self.nc = nc
        self.ops = {e: [] for e in ENGS}
        self.count = {e: 0 for e in ENGS}
        self.seen = {e: {} for e in ENGS}
        self.semkeys = []
        self.dmacount = {}

    def _deps(self, reads, writes):
        deps = {}

        def add(ev):
            if ev is None:
                return
            k, v = ev
            if deps.get(k, 0) < v:
                deps[k] = v
        for b in reads:
            add(b.w)
        for b in writes:
            add(b.w)
            for k, v in b.r.items():
                add((k, v))
        return deps

    def _waits(self, eng, deps):
        waits = []
        for k, v in deps.items():
            if eng == "tensor" and k[0] == "tensor":
                continue
            if self.seen[eng].get(k, 0) >= v:
                continue
            self.seen[eng][k] = v
            waits.append((k, v))
        return waits

    def _mark(self, ev, reads, writes):
        k, v = ev
        for b in reads:
            if not b.const and b.r.get(k, 0) < v:
                b.r[k] = v
        for b in writes:
            b.w = ev
            b.r = {}

    def op(self, eng, fn, reads=(), writes=()):
        self.nop = getattr(self, "nop", 0) + 1
        if os.environ.get('KTRACE'):
            import sys as _s
            print("OP", self.nop, eng, _s._getframe(2).f_lineno, _s._getframe(3).f_lineno)
        if self.nop > LIMIT:
            return
        deps = self._deps(reads, writes)
        waits = self._waits(eng, deps)
        self.count[eng] += 1
        n = self.count[eng]
        key = (eng, (n - 1) // EPOCH)
        if key not in self.semkeys:
            self.semkeys.append(key)
        val = (n - 1) % EPOCH + 1
        self.ops[eng].append((waits, fn, key, 1))
        self._mark((key, val), reads, writes)

    def dma(self, q, pairs, reads=(), writes=()):
        self.nop = getattr(self, "nop", 0) + 1
        if self.nop > LIMIT:
            return
        owner = (writes[0] if writes else reads[0])
        key = ("dma", owner.name)
        if key not in self.dmacount:
            self.dmacount[key] = 0
            self.semkeys.append(key)
        deps = self._deps(reads, writes)
        waits = self._waits(q, deps)
        first = True
        for (o, i) in pairs:
            self.dmacount[key] += 16
            self.ops[q].append((waits if first else [],
                                (lambda e, o=o, i=i: e.dma_start(out=o, in_=i)), key, 16))
            first = False
        self._mark((key, self.dmacount[key]), reads, writes)

    def final_waits(self, eng, bufs):
        deps = self._deps((), bufs)
        waits = self._waits(eng, deps)
        self.ops[eng].append((waits, None, None, 0))

    def build(self, st):
        nc = self.nc
        sems = {}
        for k in self.semkeys:
            sems[k] = st.enter_context(nc.semaphore("s_" + "_".join(str(x) for x in k)))
        block = st.enter_context(nc.Block())

        def mk(engname):
            def body(e):
                for (waits, fn, key, inc) in self.ops[engname]:
                    for (k, v) in waits:
                        e.wait_ge(sems[k], v)
                    if fn is not None:
                        fn(e).then_inc(sems[key], inc)
            return body

        for engname in ENGS:
            if self.ops[engname]:
                getattr(block, engname)(mk(engname))


def build_program(NT=16, SAMPLE=True):
    nc = bass.Bass("TRN2", target_bir_lowering=False)
    st = ExitStack()

    def din(name, shape):
        return nc.dram_tensor(name, shape, F32, kind="ExternalInput").ap()

    def dout(name, shape):
        return nc.dram_tensor(name, shape, F32, kind="ExternalOutput").ap()

    xp = din("xp", [2048, 1024]); xs = din("xs", [128, 1024])
    c_ckv = din("c_ckv", [4, 4096, 128]); c_kpe = din("c_kpe", [4, 4096, 32])
    c_k = din("c_k", [4, 4096, 128]); c_v = din("c_v", [4, 4096, 128]); c_kidx = din("c_kidx", [4, 4096, 64])
    d_normg = din("normg", [128, 8]); d_win = din("w_in", [1024, 2792]); d_qng = din("qng", [128, 2])
    d_kvng = din("kvng_bc", [128, 128]); d_wuq = din("w_uq", [256, 768]); d_wuk = din("w_uk", [128, 512])
    d_wuv = din("w_uv", [128, 512]); d_wout = din("w_out", [1024, 1024]); d_fng = din("fng_bc", [128, 1024])
    d_cfar = din("cfar_bc", [128, 8]); d_tb = din("tbias", [2, 128, 1024])
    d_cosp = din("cos_p", [2048, 16]); d_sinp = din("sin_p", [2048, 16])
    d_coss = din("cos_s", [128, 16]); d_sins = din("sin_s", [128, 16])
    y_p = dout("y_p", [2048, 1024]); y_s = dout("y_s", [128, 1024])
    o_p = {"ckv": dout("o_ckv_p", [2048, 128]), "kpe": dout("o_kpe_p", [2048, 32]), "k": dout("o_k_p", [2048, 128]),
           "v": dout("o_v_p", [2048, 128]), "kidx": dout("o_kidx_p", [2048, 64])}
    o_s = {"ckv": dout("o_ckv_s", [128, 128]), "kpe": dout("o_kpe_s", [128, 32]), "k": dout("o_k_s", [128, 128]),
           "v": dout("o_v_s", [128, 128]), "kidx": dout("o_kidx_s", [128, 64])}

    S = Sched(nc)
    bufs = {}

    def B(name):
        if name not in bufs:
            bufs[name] = Buf(name)
        return bufs[name]

    def sb(name, shape, dt):
        return st.enter_context(nc.sbuf_tensor("sb_" + name, shape, dt))

    w_in_b = sb("w_in_b", [128, 8, 2792], BF16)
    wcomb_b = sb("wcomb_b", [128, 2, 8, 128], BF16)
    wuqpe_b = sb("wuqpe_b", [128, 2, 256], BF16)
    wuv_b = sb("wuv_b", [128, 512], BF16)
    wout_b = sb("wout_b", [128, 8, 1024], BF16)
    ident_b = sb("ident_b", [128, 128], BF16)
    ident_f = sb("ident_f", [128, 128], F32)
    I4 = sb("I4", [128, 512], BF16)
    Rb = sb("Rb", [128, 4, 128], BF16)
    normg_t = sb("normg_t", [128, 8], F32)
    qng_t = sb("qng_t", [128, 2], F32)
    kvng_bc = sb("kvng_bc", [128, 128], F32)
    fng_bc = sb("fng_bc", [128, 1024], F32)
    cfs = sb("cfs", [128, 8], F32)
    cos_p = sb("cos_pt", [128, 16, 16], F32); sin_p = sb("sin_pt", [128, 16, 16], F32)
    cos_s = sb("cos_st", [128, 16], F32); sin_s = sb("sin_st", [128, 16], F32)
    Bt = sb("Bt", [128, 2, 8, 128], BF16)
    cst = sb("cst", [128, 16], F32)
    KT_ckv = sb("KT_ckv", [128, 4224], BF16)
    KT_kpe = sb("KT_kpe", [128, 4224], BF16)
    KT_kb = sb("KT_kb", [65, 2, 4224], BF16)
    KT_kidx = sb("KT_kidx", [128, 2048], BF16)
    KA_ckv = sb("KA_ckv", [128, 33, 129], BF16)
    KA_v = sb("KA_v", [128, 33, 2, 65], BF16)
    xt = sb("xt", [128, 1024], F32)
    xo = sb("xo", [128, 1024], F32)
    mix = sb("mix", [128, 1024], BF16)
    hT = sb("hT", [128, 8, 128], BF16)
    cq_b = sb("cq_b", [128, 256], BF16)
    cqT = sb("cqT", [128, 2, 128], BF16)
    qlatT = sb("qlatT", [128, 8, 128], BF16)
    rt = sb("rt", [128, 6, 8, 16], F32)
    qpe_b = sb("qpe_b", [128, 352], BF16)
    qpeT = sb("qpeT", [128, 8, 128], BF16)
    ckv_f = sb("ckv_f", [128, 128], F32)
    ckv_bt = sb("ckv_bt", [128, 128], BF16)
    kpe_f = sb("kpe_f", [128, 32], F32)
    kpe_bt = sb("kpe_bt", [128, 128], BF16)
    kv_f = sb("kv_f", [128, 328], F32)
    kb_bt = sb("kb_bt", [128, 256], BF16)
    v_bt = sb("v_bt", [128, 128], BF16)
    qbT = sb("qbT", [65, 8, 128], BF16)
    qb_bt = sb("qb_bt", [128, 576], BF16)
    qi_bt = sb("qi_bt", [128, 576], BF16)
    qiT = sb("qiT", [128, 8, 128], BF16)
    sga = sb("sga", [128, 512], BF16)
    sgb = sb("sgb", [128, 512], BF16)
    wabs = sb("wabs", [128, 8], F32)
    wsgn = sb("wsgn", [128, 8], F32)
    stt = sb("stt", [128, 16], F32)
    sc = sb("sc", [128, 4128], F32)
    mb = sb("mb", [128, 4128], BF16)
    rr = [sb("rr0", [128, 512], F32), sb("rr1", [128, 512], F32)]
    bis = sb("bis", [128, 8], F32)
    Pt = [sb("P0", [128, 512], BF16), sb("P1", [128, 512], BF16)]
    rc = sb("rc", [128, 16], F32)
    olat_b = sb("olat_b", [128, 8, 128], BF16)
    cache_f = sb("cache_f", [128, 2, 416], F32)
    cache_b = sb("cache_b", [128, 2, 352], BF16)
    kic = sb("kic", [128, 4, 512], BF16)
    kif = sb("kif", [128, 4, 64], F32)
    kib = sb("kib", [128, 5, 64], BF16)
    print('SBUF bytes remaining', nc.sbuf_bytes_remaining)
    PSX = [st.enter_context(nc.psum_tensor("psx%d" % j, [128, 1024], F32)) for j in range(4)]

    sc0 = sc[:, 0:2048]; sc1 = sc[:, 2048:4096]
    mb0 = mb[:, 0:2048]; mb1 = mb[:, 2048:4096]
    xt1 = KT_ckv[:, 2048:4096].bitcast(F32)
    xt2 = KT_kpe[:, 2048:4096].bitcast(F32)
    kaf = KA_ckv[:, 16:32, :].rearrange("p a b -> p (a b)")
    qlatT1 = kaf[:, 0:1024].rearrange("p (a b) -> p a b", a=8)
    qpeT1 = kaf[:, 1024:2048].rearrange("p (a b) -> p a b", a=8)
    kvf = KA_v[:, 16:33, :, :].rearrange("p a g d -> p (a g d)")
    olatT_p = kvf[:, 0:1024].rearrange("p (a b) -> p a b", a=8)
    mixT_p = kvf[:, 1024:2048].rearrange("p (a b) -> p a b", a=8)
    qbT1 = KT_kb[0:65, 0, 2048:3072].rearrange("p (a b) -> p a b", a=8)
    cff = cache_f[:].rearrange("p a b -> p (a b)").bitcast(BF16)
    hb_p = cff[:, 0:1024]
    sga1 = cff[:, 1024:1536]
    sgb1 = cache_b[:].rearrange("p a b -> p (a b)")[:, 0:512]
    qbT2 = KT_kb[0:65, 1, 2048:3072].rearrange("p (a b) -> p a b", a=8)
    sga2 = kif[:].rearrange("p a b -> p (a b)").bitcast(BF16)
    sgb2 = Rb[:].rearrange("p a b -> p (a b)")
    ALIAS_BUFS = ["sc1", "mb1", "mbj_d1", "mbj_a1", "xt1", "xt2", "qlatT1", "qpeT1", "kpe_pad1", "olatT", "mixT", "qbT1", "qbT_aug1",
                  "hb", "sga1", "sgb1", "qbT2", "qbT_aug2", "sga2", "sgb2", "kif", "Rb"]
    cur = {}
    OD_S = [(5 + h // 4, (h % 4) * 65) for h in range(8)]
    OD_P = [(2 + h // 4, (h % 4) * 65) for h in range(8)]
    CFG_S = dict(xt=(xt[:], "xt"), hb=(mix[:], "mix"), olatT=(hT[:], "hT"), mixT=(qlatT[:], "qlatT"),
                 qbT=(qbT[:], "qbT", "qbT_aug"), sga=(sga[:], "sga"), sgb=(sgb[:], "sgb"),
                 sc=(sc[:], "sc"), mb=(mb[:], "mb", "mbj_d", "mbj_a"), qlatT=(qlatT[:], "qlatT"), qpeT=(qpeT[:], "qpeT", "kpe_pad"), OD=OD_S)

    def cfg_p(t):
        p = t % 2
        return dict(xt=[(xt[:], "xt"), (xt1, "xt1"), (xt2, "xt2")][t % 3], hb=(hb_p, "hb"), olatT=(olatT_p, "olatT"), mixT=(mixT_p, "mixT"),
                    qbT=[(qbT[:], "qbT", "qbT_aug"), (qbT1, "qbT1", "qbT_aug1"), (qbT2, "qbT2", "qbT_aug2")][t % 3],
                    sga=[(sga[:], "sga"), (sga1, "sga1"), (sga2, "sga2")][t % 3], sgb=[(sgb[:], "sgb"), (sgb1, "sgb1"), (sgb2, "sgb2")][t % 3],
                    sc=[(sc0, "sc"), (sc1, "sc1")][p], mb=[(mb0, "mb", "mbj_d", "mbj_a"), (mb1, "mb1", "mbj_d1", "mbj_a1")][p],
                    qlatT=[(qlatT[:], "qlatT"), (qlatT1, "qlatT1")][p], qpeT=[(qpeT[:], "qpeT", "kpe_pad"), (qpeT1, "qpeT1", "kpe_pad1")][p],
                    OD=OD_P)

    def bank(j):
        return PSX[j // 2][:, (j % 2) * 512:(j % 2 + 1) * 512]

    def bankb(j):
        return bank(j).bitcast(BF16)

    def PB(j):
        return B("ps%d" % j)

    capl = [None]

    def _op(eng, fn, reads, writes):
        r = [B(x) for x in reads]; w = [B(x) for x in writes]
        if capl[0] is not None:
            capl[0].append(("op", eng, fn, r, w))
        else:
            S.op(eng, fn, r, w)

    def V(fn, reads, writes):
        _op("vector", fn, reads, writes)

    def A(fn, reads, writes):
        _op("scalar", fn, reads, writes)

    def G(fn, reads, writes):
        _op("gpsimd", fn, reads, writes)

    def T(fn, reads, writes):
        _op("tensor", fn, reads, writes)

    def D(pairs, reads, writes, q="sync"):
        r = [B(x) for x in reads]; w = [B(x) for x in writes]
        if capl[0] is not None:
            capl[0].append(("dma", pairs, r, w, q))
        else:
            S.dma(q, pairs, r, w)

    def capture(fn):
        capl[0] = []
        fn()
        l = capl[0]
        capl[0] = None
        return l

    def emit_merged(l1, l2):
        a = b = 0
        n1, n2 = len(l1), len(l2)
        while a < n1 or b < n2:
            if a < n1 and (b >= n2 or a * n2 <= b * n1):
                it = l1[a]; a += 1
            else:
                it = l2[b]; b += 1
            if it[0] == "op":
                S.op(it[1], it[2], it[3], it[4])
            else:
                S.dma(it[4], it[1], it[2], it[3])

    G(lambda e: e.memset(ident_b[:], 1.0), [], ["ident_b"])
    G(lambda e: e.affine_select(out=ident_b[:], in_=ident_b[:], pattern=[[-1, 128]], compare_op=ALU.is_equal,
                                fill=0.0, base=0, channel_multiplier=1), ["ident_b"], ["ident_b"])
    G(lambda e: e.memset(ident_f[:], 1.0), [], ["ident_f"])
    G(lambda e: e.affine_select(out=ident_f[:], in_=ident_f[:], pattern=[[-1, 128]], compare_op=ALU.is_equal,
                                fill=0.0, base=0, channel_multiplier=1), ["ident_f"], ["ident_f"])
    for j in range(4):
        V(lambda e, j=j: e.tensor_copy(out=I4[:, j * 128:(j + 1) * 128], in_=ident_b[:]), ["ident_b"], ["I4"])
    V(lambda e: e.tensor_tensor(out=rr[0][:, 0:32], in0=ident_f[:, 0:32], in1=ident_f[:, 32:64], op=ALU.add), ["ident_f"], ["rr0"])
    V(lambda e: e.tensor_tensor(out=rr[0][:, 0:32], in0=rr[0][:, 0:32], in1=ident_f[:, 64:96], op=ALU.add), ["ident_f", "rr0"], ["rr0"])
    V(lambda e: e.tensor_tensor(out=rr[0][:, 0:32], in0=rr[0][:, 0:32], in1=ident_f[:, 96:128], op=ALU.add), ["ident_f", "rr0"], ["rr0"])
    for b in range(4):
        V(lambda e, b=b: e.reduce_sum(out=rc[:, b:b + 1], in_=ident_f[:, 32 * b:32 * b + 32], axis=mybir.AxisListType.X), ["ident_f"], ["rc"])
        for h in range(4):
            V(lambda e, b=b, h=h: e.tensor_scalar(out=Rb[:, b, h * 32:(h + 1) * 32], in0=rr[0][:, 0:32], scalar1=rc[:, b:b + 1],
                                                  scalar2=None, op0=ALU.mult), ["rr0", "rc"], ["Rb"])
    V(lambda e: e.memset(cst[:, 0:1], -0.5), [], ["cst"])
    V(lambda e: e.memset(cst[:, 1:2], 1e-6), [], ["cst"])
    D([(normg_t[:], d_normg), (qng_t[:], d_qng), (kvng_bc[:], d_kvng), (fng_bc[:], d_fng), (cfs[:], d_cfar),
       (cos_s[:], d_coss), (sin_s[:], d_sins),
       (cos_p[:], d_cosp.rearrange("(t p) r -> p t r", p=128)), (sin_p[:], d_sinp.rearrange("(t p) r -> p t r", p=128))],
      [], ["smallc"])
    V(lambda e: e.tensor_copy(out=qbT[64:65, :, :], in_=cfs[64:65, :].unsqueeze(2).to_broadcast([1, 8, 128])), ["smallc"], ["qbT_aug"])
    wst = [(xt, "xt"), (xo, "xo"), (sc[:, 0:1024], "scst0"), (sc[:, 1024:2048], "scst1"), (sc[:, 2048:3072], "scst2"), (sc[:, 3072:4096], "scst3")]
    k = 0
    for kc in range(8):
        for (c0, c1) in [(0, 1024), (1024, 2048), (2048, 2792)]:
            t_, tn = wst[k % 6]; k += 1
            n = c1 - c0
            D([(t_[:, 0:n], d_win[kc * 128:(kc + 1) * 128, c0:c1])], [], [tn], q=("sync" if k % 2 else "scalar"))
            segs = []
            for (s0, s1, d0, sc_) in [(0, 928, 0, 1.0), (928, 1440, 928, 0.125), (1440, 1696, 1440, 1.0),
                                      (2208, 2280, 1696, 1.0), (1696, 2208, 1768, 0.125), (2280, 2792, 2280, 1.0)]:
                a0, a1 = max(s0, c0), min(s1, c1)
                if a0 < a1:
                    segs.append((a0, a1, d0 + (a0 - s0), sc_))
            for (a0, a1, dd, sc_) in segs:
                eng = V if (k % 2) else G
                eng(lambda e, t_=t_, a0=a0, a1=a1, dd=dd, sc_=sc_, kc=kc, c0=c0:
                    e.tensor_scalar(out=w_in_b[:, kc, dd:dd + (a1 - a0)], in0=t_[:, a0 - c0:a1 - c0],
                                    scalar1=normg_t[:, kc:kc + 1], scalar2=sc_, op0=ALU.mult, op1=ALU.mult),
                    [tn, "smallc"], ["w_in_b"])
    for ec in range(8):
        t_, tn = wst[k % 6]; k += 1
        D([(t_[:], d_wout[ec * 128:(ec + 1) * 128, :])], [], [tn], q=("sync" if k % 2 else "scalar"))
        (V if ec % 2 else G)(lambda e, t_=t_, ec=ec: e.tensor_copy(out=wout_b[:, ec, :], in_=t_[:]), [tn], ["wout_b"])
    D([(xt[:, 0:512], d_wuv)], [], ["xt"])
    V(lambda e: e.tensor_copy(out=wuv_b[:], in_=xt[:, 0:512]), ["xt"], ["wuv_b"])
    D([(xo[:, 0:768], d_wuq[0:128, :]), (xo[:, 768:1024], d_wuk[:, 0:256])], [], ["xo"])
    D([(xt[:, 0:768], d_wuq[128:256, :]), (xt[:, 768:1024], d_wuk[:, 256:512])], [], ["xt"])
    wq = [xo, xt]; wqn = ["xo", "xt"]
    for jc in range(2):
        V(lambda e, jc=jc: e.tensor_scalar(out=wq[jc][:, 0:768], in0=wq[jc][:, 0:768], scalar1=qng_t[:, jc:jc + 1], scalar2=None,
                                           op0=ALU.mult), [wqn[jc], "smallc"], [wqn[jc]])
        V(lambda e, jc=jc: e.tensor_copy(out=wuqpe_b[:, jc, :].rearrange("p (h r) -> p h r", h=8),
                                         in_=wq[jc][:, 0:768].rearrange("p (h e) -> p h e", h=8)[:, :, 64:96]),
          [wqn[jc]], ["wuqpe_b"])
    uqT = sc[0:64, 0:2048].rearrange("p (h j) -> p h j", h=8)
    ukT = sc[0:64, 2048:3072].rearrange("p (h c) -> p h c", h=8)
    for h in range(8):
        for jc in range(2):
            T(lambda e, h=h, jc=jc: e.matmul(bank(0)[0:64, 0:128], lhsT=wq[jc][:, h * 96:h * 96 + 64], rhs=ident_f[:], start=True, stop=True),
              [wqn[jc], "ident_f"], ["ps0"])
            V(lambda e, h=h, jc=jc: e.tensor_copy(out=uqT[:, h, jc * 128:(jc + 1) * 128], in_=bank(0)[0:64, 0:128]), ["ps0"], ["sc", "scst0", "scst1", "scst2", "scst3"])
        src, srcn, off = (xo, "xo", 768 + h * 64) if h < 4 else (xt, "xt", 768 + (h - 4) * 64)
        T(lambda e, src=src, off=off: e.matmul(bank(1)[0:64, 0:128], lhsT=src[:, off:off + 64], rhs=ident_f[:], start=True, stop=True),
          [srcn, "ident_f"], ["ps1"])
        V(lambda e, h=h: e.tensor_copy(out=ukT[:, h, :], in_=bank(1)[0:64, 0:128]), ["ps1"], ["sc", "scst0", "scst1", "scst2", "scst3"])
    for h in range(8):
        for jc in range(2):
            pj = 2 + (h * 2 + jc) % 2
            T(lambda e, h=h, jc=jc, pj=pj: e.matmul(bank(pj)[:, 0:128], lhsT=uqT[:, h, jc * 128:(jc + 1) * 128], rhs=ukT[:, h, :],
                                                   start=True, stop=True), ["sc"], ["ps%d" % pj])
            A(lambda e, h=h, jc=jc, pj=pj: e.activation(out=wcomb_b[:, jc, h, :], in_=bank(pj)[:, 0:128], func=AF.Copy),
              ["ps%d" % pj], ["wcomb_b"])

    V(lambda e: e.memset(KA_ckv[:, :, 128:129], 1.0), [], ["KA_ones"])
    G(lambda e: e.memset(KA_v[:, :, :, 64:65], 1.0), [], ["KA_ones"])
    V(lambda e: e.memset(KT_kb[64:65, :, :], 1.0), [], ["KT_ones"])
    V(lambda e: e.memset(qpe_b[:, 256:352], 0.0), [], ["tpad"])
    V(lambda e: e.memset(kpe_bt[:, 32:128], 0.0), [], ["tpad"])
    V(lambda e: e.memset(kb_bt[:, 192:256], 0.0), [], ["tpad"])
    V(lambda e: e.memset(qb_bt[:, 512:576], 0.0), [], ["tpad"])
    V(lambda e: e.memset(qi_bt[:, 512:576], 0.0), [], ["tpad"])
    G(lambda e: e.memset(cache_b[:, :, 288:352], 0.0), [], ["tpad"])
    V(lambda e: e.memset(kib[:, 4, :], 0.0), [], ["tpad"])
    G(lambda e: e.memset(KT_kidx[64:128, :], 0.0), [], ["idx_pad"])
    V(lambda e: e.memset(qiT[64:128, :, :], 0.0), [], ["idx_pad"])
    G(lambda e: e.memset(kic[64:128, :, :], 0.0), [], ["idx_pad"])
    V(lambda e: e.memset(KT_kpe[32:64, :], 0.0), [], ["kpe_pad"])
    G(lambda e: e.memset(KT_kpe[64:128, :], 0.0), [], ["kpe_pad"])
    V(lambda e: e.memset(qpeT[32:64, :, :], 0.0), [], ["kpe_pad"])
    G(lambda e: e.memset(qpeT[64:128, :, :], 0.0), [], ["kpe_pad"])
    for typ in range(2):
        D([(xo[:], d_tb[typ])], [], ["xo"])
        for h in range(8):
            V(lambda e, typ=typ, h=h: e.tensor_scalar(out=Bt[:, typ, h, :], in0=xo[:, h * 128:(h + 1) * 128], scalar1=cfs[:, h:h + 1],
                                                      scalar2=None, op0=ALU.subtract), ["xo", "smallc"], ["Bt"])

    def rope(src, cos, sin, nh, outb, outf, rd, wr):
        cb = cos.unsqueeze(1).to_broadcast([128, nh, 16]); sbb = sin.unsqueeze(1).to_broadcast([128, nh, 16])
        x1 = src[:, :, 0:16]; x2 = src[:, :, 16:32]
        t = [rt[:, i, 0:nh, :] for i in range(6)]
        V(lambda e: e.tensor_tensor(out=t[0], in0=x1, in1=cb, op=ALU.mult), rd, ["rt"])
        V(lambda e: e.tensor_tensor(out=t[1], in0=x2, in1=sbb, op=ALU.mult), rd, ["rt"])
        V(lambda e: e.tensor_tensor(out=t[2], in0=x1, in1=sbb, op=ALU.mult), rd, ["rt"])
        V(lambda e: e.tensor_tensor(out=t[3], in0=x2, in1=cb, op=ALU.mult), rd, ["rt"])
        tgt = outf if outf is not None else outb
        V(lambda e: e.tensor_tensor(out=tgt[:, :, 0:16], in0=t[0], in1=t[1], op=ALU.subtract), ["rt"], wr)
        V(lambda e: e.tensor_tensor(out=tgt[:, :, 16:32], in0=t[2], in1=t[3], op=ALU.add), ["rt"], wr)
        if outf is not None:
            V(lambda e: e.tensor_copy(out=outb, in_=outf), wr, wr + ["rope_b"])

    def rstd_from(ssq_ap, out_ap, n, rd, wr):
        V(lambda e: e.tensor_scalar(out=out_ap, in0=ssq_ap, scalar1=1.0 / n, scalar2=1e-6, op0=ALU.mult, op1=ALU.add), rd, wr)
        G(lambda e: e.tensor_tensor(out=out_ap, in0=out_ap, in1=cst[:, 0:1], op=ALU.pow), wr + ["cst"], wr)

    def transposes(items, evac_out, evac_reads_extra=()):
        pass

    def proj_tile(xsrc, cos, sin, odst, row0, blk, sample):
        xt_, xtn = cur["xt"]; hb_, hbn = cur["hb"]
        qbT_, qbTn, _ = cur["qbT"]; sga_, sgan = cur["sga"]; sgb_, sgbn = cur["sgb"]
        D([(xt_, xsrc)], [], [xtn])
        A(lambda e: e.activation(out=hb_, in_=xt_, func=AF.Square, accum_out=stt[:, 0:1]), [xtn], [hbn, "stt0"])
        rstd_from(stt[:, 0:1], stt[:, 1:2], 1024.0, ["stt0"], ["stt1"])
        V(lambda e: e.tensor_scalar(out=hb_, in0=xt_, scalar1=stt[:, 1:2], scalar2=None, op0=ALU.mult), [xtn, "stt1"], [hbn])
        hv = bankb(7).rearrange("p (a b) -> p a b", a=8)
        for kc in range(8):
            T(lambda e, kc=kc: e.transpose(hv[:, kc, :], hb_[:, kc * 128:(kc + 1) * 128], ident_b[:]), [hbn, "ident_b"], ["ps7"])
        A(lambda e: e.activation(out=hT[:].rearrange("p a b -> p (a b)"), in_=bankb(7), func=AF.Copy), ["ps7"], ["hT"])
        def group(gi, pj):
            c0, c1 = GROUPS[gi]
            for kc in range(8):
                T(lambda e, kc=kc, c0=c0, c1=c1, pj=pj: e.matmul(bank(pj)[:, 0:c1 - c0], lhsT=hT[:, kc, :], rhs=w_in_b[:, kc, c0:c1],
                                                               start=(kc == 0), stop=(kc == 7)), ["hT", "w_in_b"], ["ps%d" % pj])
        group(0, 5)
        pa = bank(5)
        A(lambda e: e.activation(out=hb_[:, 0:256], in_=pa[:, 0:256], func=AF.Square, accum_out=stt[:, 2:3]), ["ps5"], [hbn, "stt2"])
        A(lambda e: e.activation(out=hb_[:, 256:384], in_=pa[:, 256:384], func=AF.Square, accum_out=stt[:, 3:4]), ["ps5"], [hbn, "stt3"])
        rstd_from(stt[:, 2:3], stt[:, 4:5], 256.0, ["stt2"], ["stt4"])
        rstd_from(stt[:, 3:4], stt[:, 5:6], 128.0, ["stt3"], ["stt5"])
        V(lambda e: e.tensor_scalar(out=cq_b[:], in0=pa[:, 0:256], scalar1=stt[:, 4:5], scalar2=None, op0=ALU.mult), ["ps5", "stt4"], ["cq_b"])
        V(lambda e: e.scalar_tensor_tensor(out=ckv_f[:], in0=pa[:, 256:384], scalar=stt[:, 5:6], in1=kvng_bc[:], op0=ALU.mult, op1=ALU.mult),
          ["ps5", "stt5", "smallc"], ["ckv_f"])
        rope(pa[:, 384:416].rearrange("p (h r) -> p h r", h=1), cos, sin, 1, kpe_bt[:, 0:32].rearrange("p (h r) -> p h r", h=1),
             kpe_f[:].rearrange("p (h r) -> p h r", h=1), ["ps5", "smallc"], ["kpe_f"])
        D([(odst["ckv"][row0:row0 + 128, :], ckv_f[:])], ["ckv_f"], [], q="gpsimd")
        D([(odst["kpe"][row0:row0 + 128, :], kpe_f[:])], ["kpe_f"], [], q="gpsimd")
        G(lambda e: e.tensor_copy(out=ckv_bt[:], in_=ckv_f[:]), ["ckv_f"], ["ckv_bt"])
        if not sample:
            G(lambda e: e.tensor_copy(out=KA_ckv[:, blk, 0:128], in_=ckv_f[:]), ["ckv_f"], ["KA_ckv%d" % blk])
        group(1, 6)
        A(lambda e: e.activation(out=rr[0][:], in_=bank(6), func=AF.Tanh, scale=0.5), ["ps6"], ["rr0"])
        V(lambda e: e.scalar_tensor_tensor(out=sga_, in0=rr[0][:], scalar=1.0, in1=bank(6), op0=ALU.add, op1=ALU.mult), ["rr0", "ps6"], [sgan])
        group(2, 5)
        V(lambda e: e.tensor_copy(out=qb_bt[:, 0:512], in_=bank(5)), ["ps5"], ["qb_bt"])
        group(3, 6)
        A(lambda e: e.activation(out=kv_f[:], in_=bank(6)[:, 0:328], func=AF.Copy), ["ps6"], ["kv_f"])
        D([(odst["k"][row0:row0 + 128, :], kv_f[:, 0:128])], ["kv_f"], [], q="gpsimd")
        D([(odst["v"][row0:row0 + 128, :], kv_f[:, 128:256])], ["kv_f"], [], q="gpsimd")
        D([(odst["kidx"][row0:row0 + 128, :], kv_f[:, 256:320])], ["kv_f"], [], q="gpsimd")
        G(lambda e: e.tensor_copy(out=kb_bt[:, 0:128], in_=kv_f[:, 0:128]), ["kv_f"], ["kb_bt"])
        G(lambda e: e.tensor_copy(out=kb_bt[:, 128:192], in_=kv_f[:, 256:320]), ["kv_f"], ["kb_bt"])
        if sample:
            G(lambda e: e.tensor_copy(out=v_bt[:], in_=kv_f[:, 128:256]), ["kv_f"], ["v_bt"])
        else:
            G(lambda e: e.tensor_copy(out=KA_v[:, blk, :, 0:64], in_=kv_f[:, 128:256].rearrange("p (g d) -> p g d", g=2)),
              ["kv_f"], ["KA_v%d" % blk])
        A(lambda e: e.activation(out=wabs[:], in_=kv_f[:, 320:328], func=AF.Abs, scale=8.0 ** -0.5), ["kv_f"], ["wabs"])
        V(lambda e: e.tensor_scalar(out=wsgn[:], in0=kv_f[:, 320:328], scalar1=0.0, scalar2=2.0, op0=ALU.is_ge, op1=ALU.mult),
          ["kv_f"], ["wsgn"])
        V(lambda e: e.tensor_scalar(out=wsgn[:], in0=wsgn[:], scalar1=-1.0, scalar2=None, op0=ALU.add), ["wsgn"], ["wsgn"])
        group(4, 5)
        V(lambda e: e.tensor_copy(out=qi_bt[:, 0:512], in_=bank(5)), ["ps5"], ["qi_bt"])
        group(5, 6)
        A(lambda e: e.activation(out=rr[1][:], in_=bank(6), func=AF.Tanh, scale=0.5), ["ps6"], ["rr1"])
        V(lambda e: e.scalar_tensor_tensor(out=sgb_, in0=rr[1][:], scalar=1.0, in1=bank(6), op0=ALU.add, op1=ALU.mult), ["rr1", "ps6"], [sgbn])
        pv7 = bankb(7)
        for jc in range(2):
            T(lambda e, jc=jc: e.transpose(pv7[:, jc * 128:(jc + 1) * 128], cq_b[:, jc * 128:(jc + 1) * 128], ident_b[:]), ["cq_b", "ident_b"], ["ps7"])
        T(lambda e: e.transpose(pv7[:, 256:384], ckv_bt[:], ident_b[:]), ["ckv_bt", "ident_b"], ["ps7"])
        T(lambda e: e.transpose(pv7[:, 384:512], kpe_bt[:], ident_b[:]), ["rope_b", "kpe_f", "ident_b", "tpad"], ["ps7"])
        for g in range(2):
            T(lambda e, g=g: e.transpose(pv7[:, 512 + g * 128:640 + g * 128], kb_bt[:, g * 64:g * 64 + 128], ident_b[:]), ["kb_bt", "ident_b", "tpad"], ["ps7"])
        T(lambda e: e.transpose(pv7[:, 768:896], kb_bt[:, 128:256], ident_b[:]), ["kb_bt", "ident_b", "tpad"], ["ps7"])
        V(lambda e: e.tensor_copy(out=cqT[:].rearrange("p a b -> p (a b)"), in_=pv7[:, 0:256]), ["ps7"], ["cqT"])
        if sample:
            kdst = dict(ckv=xo[:, 0:128].bitcast(BF16)[:, 0:128], kpe=xo[0:32, 128:256].bitcast(BF16)[:, 0:128],
                        kb=[xo[0:64, 256:384].bitcast(BF16)[:, 0:128], xo[0:64, 384:512].bitcast(BF16)[:, 0:128]],
                        kidx=xo[0:64, 512:640].bitcast(BF16)[:, 0:128])
            kw = ["xo"]
        else:
            kdst = dict(ckv=KT_ckv[:, blk * 128:(blk + 1) * 128], kpe=KT_kpe[0:32, blk * 128:(blk + 1) * 128],
                        kb=[KT_kb[0:64, g, blk * 128:(blk + 1) * 128] for g in range(2)],
                        kidx=KT_kidx[0:64, blk * 128:(blk + 1) * 128])
            kw = ["KT%d" % blk]
        V(lambda e: e.tensor_copy(out=kdst["ckv"], in_=pv7[:, 256:384]), ["ps7"], kw)
        V(lambda e: e.tensor_copy(out=kdst["kpe"], in_=pv7[0:32, 384:512]), ["ps7"], kw)
        for g in range(2):
            V(lambda e, g=g: e.tensor_copy(out=kdst["kb"][g], in_=pv7[0:64, 512 + g * 128:640 + g * 128]), ["ps7"], kw)
        V(lambda e: e.tensor_copy(out=kdst["kidx"], in_=pv7[0:64, 768:896]), ["ps7"], kw)
        qlatT_, qlatTn = cur["qlatT"]; qpeT_, qpeTn, _ = cur["qpeT"]
        for h in range(8):
            pj = 5 + h // 4
            for jc in range(2):
                T(lambda e, h=h, jc=jc, pj=pj: e.matmul(bank(pj)[:, (h % 4) * 128:(h % 4 + 1) * 128], lhsT=wcomb_b[:, jc, h, :], rhs=cqT[:, jc, :],
                                                       start=(jc == 0), stop=(jc == 1)), ["wcomb_b", "cqT"], ["ps%d" % pj])
        A(lambda e: e.activation(out=qlatT_[:, 0:4, :].rearrange("p a b -> p (a b)"), in_=bank(5), func=AF.Copy), ["ps5"], [qlatTn])
        V(lambda e: e.tensor_copy(out=qlatT_[:, 4:8, :].rearrange("p a b -> p (a b)"), in_=bank(6)), ["ps6"], [qlatTn])
        for jc in range(2):
            T(lambda e, jc=jc: e.matmul(bank(7)[:, 0:256], lhsT=cqT[:, jc, :], rhs=wuqpe_b[:, jc, :], start=(jc == 0), stop=(jc == 1)),
              ["cqT", "wuqpe_b"], ["ps7"])
        rope(bank(7)[:, 0:256].rearrange("p (h r) -> p h r", h=8), cos, sin, 8, qpe_b[:, 0:256].rearrange("p (h r) -> p h r", h=8), None, ["ps7", "smallc"], ["qpe_b"])
        pv6 = bankb(5)
        for h in range(8):
            T(lambda e, h=h: e.transpose(pv6[:, h * 128:(h + 1) * 128], qpe_b[:, h * 32:h * 32 + 128], ident_b[:]), ["qpe_b", "ident_b", "tpad"], ["ps5"])
        V(lambda e: e.tensor_copy(out=qpeT_[0:32, :, :].rearrange("p a b -> p (a b)"), in_=pv6[0:32, :]), ["ps5"], [qpeTn])
        pv5 = bankb(6)
        for h in range(8):
            T(lambda e, h=h: e.transpose(pv5[:, h * 128:(h + 1) * 128], qb_bt[:, h * 64:h * 64 + 128], ident_b[:]), ["qb_bt", "ident_b", "tpad"], ["ps6"])
        A(lambda e: e.activation(out=qbT_[0:64, :, :].rearrange("p a b -> p (a b)"), in_=pv5[0:64, :], func=AF.Copy), ["ps6"], [qbTn])
        for h in range(8):
            T(lambda e, h=h: e.transpose(pv7[:, h * 128:(h + 1) * 128], qi_bt[:, h * 64:h * 64 + 128], ident_b[:]), ["qi_bt", "ident_b", "tpad"], ["ps7"])
        V(lambda e: e.tensor_copy(out=qiT[0:64, :, :].rearrange("p a b -> p (a b)"), in_=pv7[0:64, :]), ["ps7"], ["qiT"])

    state = {"r": 0, "p": 0, "sps": 0}

    def idx_epilogue(pj, h, c0, n):
        j = state["r"]; state["r"] ^= 1
        sc_, scn = cur["sc"]
        A(lambda e: e.activation(out=rr[j][:, 0:n], in_=bank(pj)[:, 0:n], func=AF.Relu, scale=wabs[:, h:h + 1]), ["ps%d" % pj, "wabs"], ["rr%d" % j])
        if h == 0:
            V(lambda e: e.tensor_scalar(out=sc_[:, c0:c0 + n], in0=rr[j][:, 0:n], scalar1=wsgn[:, 0:1], scalar2=None, op0=ALU.mult),
              ["rr%d" % j, "wsgn"], [scn])
        else:
            V(lambda e: e.scalar_tensor_tensor(out=sc_[:, c0:c0 + n], in0=rr[j][:, 0:n], scalar=wsgn[:, h:h + 1], in1=sc_[:, c0:c0 + n],
                                               op0=ALU.mult, op1=ALU.add), ["rr%d" % j, "wsgn", scn], [scn])

    def bisect(N, do):
        sc_, scn = cur["sc"]; mb_, mbn, mjd, mja = cur["mb"]
        if not do:
            V(lambda e: e.memset(bis[:, 3:4], -1e9), [], ["bis_thr"])
        else:
            M = int(round(0.55 * N / 32.0)) * 32
            Nd = N - M
            V(lambda e: e.memset(bis[:, 0:1], 0.0), [], ["bis_t"])
            g = 32.0
            for it in range(NITER):
                last = (it == NITER - 1)
                V(lambda e: e.tensor_scalar(out=mb_[:, 0:Nd], in0=sc_[:, 0:Nd], scalar1=bis[:, 0:1], scalar2=None, op0=ALU.is_ge, op1=ALU.add,
                                            accum_out=bis[:, 1:2]), [scn, "bis_t"], [mjd, "bis_c"])
                A(lambda e: e.activation(out=mb_[:, Nd:N], in_=sc_[:, Nd:N], func=AF.Sign, scale=-1.0, bias=bis[:, 0:1], accum_out=bis[:, 4:5]),
                  [scn, "bis_t"], [mja, "bis_s"])
                V(lambda e: e.scalar_tensor_tensor(out=bis[:, 5:6], in0=bis[:, 1:2], scalar=2.0, in1=bis[:, 4:5], op0=ALU.mult, op1=ALU.subtract),
                  ["bis_c", "bis_s"], ["bis_u"])
                cthr = 511.0 - M
                if not last:
                    hh = g / 2
                    V(lambda e, hh=hh: e.tensor_scalar(out=bis[:, 2:3], in0=bis[:, 5:6], scalar1=cthr, scalar2=2 * hh, op0=ALU.is_ge, op1=ALU.mult),
                      ["bis_u"], ["bis_d"])
                    V(lambda e, hh=hh: e.scalar_tensor_tensor(out=bis[:, 0:1], in0=bis[:, 2:3], scalar=-hh, in1=bis[:, 0:1], op0=ALU.add, op1=ALU.add),
                      ["bis_d", "bis_t"], ["bis_t"])
                    g = hh
                else:
                    V(lambda e, g=g: e.tensor_scalar(out=bis[:, 2:3], in0=bis[:, 5:6], scalar1=cthr, scalar2=g, op0=ALU.is_ge, op1=ALU.mult),
                      ["bis_u"], ["bis_d"])
                    V(lambda e, g=g: e.scalar_tensor_tensor(out=bis[:, 3:4], in0=bis[:, 2:3], scalar=-g, in1=bis[:, 0:1], op0=ALU.add, op1=ALU.add),
                      ["bis_d", "bis_t"], ["bis_thr"])
                yield
        V(lambda e: e.tensor_scalar(out=mb_[:, 0:N], in0=sc_[:, 0:N], scalar1=bis[:, 3:4], scalar2=NEG, op0=ALU.is_lt, op1=ALU.mult),
          [scn, "bis_thr"], [mbn, mjd, mja])
        yield

    def interleave(g1, n1, g2, n2):
        a = b = 0
        d1 = d2 = False
        while not (d1 and d2):
            if not d1 and (d2 or a * n2 <= b * n1):
                try:
                    next(g1); a += 1
                except StopIteration:
                    d1 = True
            elif not d2:
                try:
                    next(g2); b += 1
                except StopIteration:
                    d2 = True

    OM = [(2 + h // 3, (h % 3) * 129) for h in range(8)]

    def attend_block(q0, nq, kcol, nk, kblk, first, tp, mla_mask, dsa_mask, dsa_bias, Rsel, kreads, which="both"):
        units = []
        qlatT_, qlatTn = cur["qlatT"]; qpeT_, qpeTn, qpadn = cur["qpeT"]
        mb_, mbn = cur["mb"][0], cur["mb"][1]
        OD = cur["OD"]
        hgs = [(0, 4), (4, 8)] if nq == 128 else [(0, 8)]
        if which == "dsa":
            hgs = []
        for (h0, h1) in hgs:
            nh = h1 - h0
            pj = state["sps"]; state["sps"] ^= 1
            pi = state["p"]; state["p"] ^= 1
            n = nh * nq
            sv = bank(pj)[0:nk, 0:n]
            pt = Pt[pi]

            def qk(sv=sv, h0=h0, h1=h1, pj=pj, pi=pi, pt=pt, n=n, nh=nh):
                T(lambda e: e.matmul(sv, lhsT=KT_ckv[:, kcol:kcol + nk], rhs=qlatT_[:, h0:h1, q0:q0 + nq], start=True, stop=False),
                  kreads + [qlatTn], ["ps%d" % pj])
                T(lambda e: e.matmul(sv, lhsT=KT_kpe[:, kcol:kcol + nk], rhs=qpeT_[:, h0:h1, q0:q0 + nq], start=False, stop=True),
                  kreads + [qpeTn, qpadn, "kpe_pad"], ["ps%d" % pj])
                A(lambda e: e.activation(out=pt[0:nk, 0:n], in_=sv, func=AF.Exp, scale=MLA_SCALE), ["ps%d" % pj], ["P%d" % pi])
                if mla_mask:
                    G(lambda e: e.memset(pt[64:128, 0:nh * 128].rearrange("p (h q) -> p h q", h=nh)[:, :, 0:64], 0.0), ["P%d" % pi], ["P%d" % pi])

            def pv(h0=h0, h1=h1, pi=pi, pt=pt):
                for h in range(h0, h1):
                    bj, co = OM[h]
                    st_flag = first and (h % 3 == 0)
                    T(lambda e, h=h, bj=bj, co=co, st_flag=st_flag: e.matmul(
                        bank(bj)[q0:q0 + nq, co:co + 129], lhsT=pt[0:nk, (h - h0) * nq:(h - h0 + 1) * nq], rhs=KA_ckv[0:nk, kblk, :],
                        start=st_flag, stop=False, tile_position=tp, skip_group_check=True),
                      ["P%d" % pi, "KA_ones"] + kreads, ["ps%d" % bj])
            units.append((qk, pv))
        for g in (range(2) if which != "mla" else []):
            pj = state["sps"]; state["sps"] ^= 1
            pi = state["p"]; state["p"] ^= 1
            n = 4 * nq
            sv = bank(pj)[0:nk, 0:n]
            pt = Pt[pi]
            more = dsa_mask or (dsa_bias is not None)

            qbT_, qbTn, qbTan = cur["qbT"]

            def qk(sv=sv, g=g, pj=pj, pi=pi, pt=pt, n=n, more=more, qbT_=qbT_, qbTn=qbTn, qbTan=qbTan):
                T(lambda e: e.matmul(sv, lhsT=KT_kb[:, g, kcol:kcol + nk], rhs=qbT_[:, 4 * g:4 * g + 4, q0:q0 + nq], start=True, stop=not more),
                  kreads + [qbTn, qbTan, "KT_ones"], ["ps%d" % pj])
                if dsa_mask:
                    rhs_m = I4[:, 0:n] if Rsel is None else Rb[:, Rsel, :]
                    T(lambda e: e.matmul(sv, lhsT=mb_[:, kcol:kcol + nk], rhs=rhs_m, start=False, stop=(dsa_bias is None)),
                      [mbn, "I4", "Rb"], ["ps%d" % pj])
                if dsa_bias is not None:
                    T(lambda e: e.matmul(sv, lhsT=ident_b[0:nk, 0:nk], rhs=Bt[0:nk, dsa_bias, 4 * g:4 * g + 4, 0:nq], start=False, stop=True),
                      ["Bt", "ident_b"], ["ps%d" % pj])
                A(lambda e: e.activation(out=pt[0:nk, 0:n], in_=sv, func=AF.Exp), ["ps%d" % pj], ["P%d" % pi])

            def pv(g=g, pi=pi, pt=pt):
                for hl in range(4):
                    h = 4 * g + hl
                    bj, co = OD[h]
                    st_flag = first and (hl == 0)
                    T(lambda e, hl=hl, bj=bj, co=co, st_flag=st_flag: e.matmul(
                        bank(bj)[q0:q0 + nq, co:co + 65], lhsT=pt[0:nk, hl * nq:(hl + 1) * nq], rhs=KA_v[0:nk, kblk, g, :],
                        start=st_flag, stop=False, tile_position=tp, skip_group_check=True),
                      ["P%d" % pi, "KA_ones"] + kreads, ["ps%d" % bj])
            units.append((qk, pv))
        return units

    def run_units(units):
        if not units:
            return
        units[0][0]()
        for n in range(len(units)):
            if n + 1 < len(units):
                units[n + 1][0]()
            units[n][1]()
            yield

    def mla_finish():
        for j, nh in [(0, 3), (1, 3), (2, 2)]:
            V(lambda e, j=j, nh=nh: e.reciprocal(out=rc[:, 3 * j:3 * j + nh], in_=bank(2 + j)[:, 0:nh * 129].rearrange("p (h c) -> p h c", h=nh)[:, :, 128]),
              ["ps%d" % (2 + j)], ["rc"])
        for h in range(8):
            bj, co = OM[h]
            V(lambda e, h=h, bj=bj, co=co: e.tensor_scalar(out=olat_b[:, h, :], in0=bank(bj)[:, co:co + 128], scalar1=rc[:, h:h + 1], scalar2=None, op0=ALU.mult),
              ["ps%d" % bj, "rc"], ["olat_b"])

    def combine(xdst, row0, sample=False):
        xt_, xtn = cur["xt"]; olatT, olatTn = cur["olatT"]; mixT, mixTn = cur["mixT"]
        sga_, sgan = cur["sga"]; sgb_, sgbn = cur["sgb"]
        OD = cur["OD"]; odb = OD[0][0]
        pv7 = bankb(0).rearrange("p (a b) -> p a b", a=8)
        if sample:
            V(lambda e: e.scalar_tensor_tensor(out=mix[:, 512:1024], in0=bank(5), scalar=0.5, in1=sgb_, op0=ALU.mult, op1=ALU.mult), ["ps5", sgbn], ["mix"])
        else:
            for j in range(2):
                V(lambda e, j=j: e.reciprocal(out=rc[:, 8 + 4 * j:12 + 4 * j], in_=bank(odb + j)[:, 0:260].rearrange("p (h c) -> p h c", h=4)[:, :, 64]),
                  ["ps%d" % (odb + j)], ["rcd"])
            V(lambda e: e.tensor_scalar(out=rc[:, 8:16], in0=rc[:, 8:16], scalar1=0.5, scalar2=None, op0=ALU.mult), ["rcd"], ["rcd"])
            for h in range(8):
                bj, co = OD[h]
                V(lambda e, h=h, bj=bj, co=co: e.scalar_tensor_tensor(out=mix[:, 512 + h * 64:576 + h * 64], in0=bank(bj)[:, co:co + 64], scalar=rc[:, 8 + h:9 + h],
                                                                      in1=sgb_[:, h * 64:(h + 1) * 64], op0=ALU.mult, op1=ALU.mult),
                  ["ps%d" % bj, "rcd", sgbn], ["mix"])
            pv7 = bankb(0).rearrange("p (a b) -> p a b", a=8)
            for h in range(8):
                T(lambda e, h=h: e.transpose(pv7[:, h, :], olat_b[:, h, :], ident_b[:]), ["olat_b", "ident_b"], ["ps0"])
            A(lambda e: e.activation(out=olatT[:].rearrange("p a b -> p (a b)"), in_=bankb(0), func=AF.Copy), ["ps0"], [olatTn])
        for h in range(8):
            T(lambda e, h=h: e.matmul(bank(1)[:, h * 64:(h + 1) * 64], lhsT=olatT[:, h, :], rhs=wuv_b[:, h * 64:(h + 1) * 64], start=True, stop=True),
              [olatTn, "wuv_b"], ["ps1"])
        V(lambda e: e.scalar_tensor_tensor(out=mix[:, 0:512], in0=bank(1), scalar=0.5, in1=sga_, op0=ALU.mult, op1=ALU.mult), ["ps1", sgan], ["mix"])
        for ec in range(8):
            T(lambda e, ec=ec: e.transpose(pv7[:, ec, :], mix[:, ec * 128:(ec + 1) * 128], ident_b[:]), ["mix", "ident_b"], ["ps0"])
        A(lambda e: e.activation(out=mixT[:].rearrange("p a b -> p (a b)"), in_=bankb(0), func=AF.Copy), ["ps0"], [mixTn])
        for nn in range(2):
            for ec in range(8):
                T(lambda e, nn=nn, ec=ec: e.matmul(bank(nn), lhsT=mixT[:, ec, :], rhs=wout_b[:, ec, nn * 512:(nn + 1) * 512], start=(ec == 0), stop=(ec == 7)),
                  [mixTn, "wout_b"], ["ps%d" % nn])
        V(lambda e: e.tensor_tensor(out=xo[:], in0=PSX[0][:], in1=xt_, op=ALU.add), ["ps0", "ps1", xtn], ["xo"])
        A(lambda e: e.activation(out=mix[:], in_=xo[:], func=AF.Square, accum_out=stt[:, 6:7]), ["xo"], ["mix", "stt6"])
        rstd_from(stt[:, 6:7], stt[:, 7:8], 1024.0, ["stt6"], ["stt7"])
        V(lambda e: e.scalar_tensor_tensor(out=xo[:], in0=xo[:], scalar=stt[:, 7:8], in1=fng_bc[:], op0=ALU.mult, op1=ALU.mult),
          ["xo", "stt7", "smallc"], ["xo"])
        D([(xdst[row0:row0 + 128, :], xo[:])], ["xo"], [], q="gpsimd")

    if SAMPLE:
        cur.update(CFG_S)
        proj_tile(xs, cos_s[:], sin_s[:], o_s, 0, None, True)
        nckv = xo[:, 0:128].bitcast(BF16)[:, 0:128]; nkpe = xo[0:32, 128:256].bitcast(BF16)[:, 0:128]
        nkb = [xo[0:64, 256:384].bitcast(BF16)[:, 0:128], xo[0:64, 384:512].bitcast(BF16)[:, 0:128]]
        nkidx = xo[0:64, 512:640].bitcast(BF16)[:, 0:128]
        kifs = [(kif[:], "kif"), (cache_f[:, 0, 0:256].rearrange("p (a b) -> p a b", a=4), "cache_f0")]
        kibs = [(kib[:], "kib"), (cache_b[:, 0, 0:320].rearrange("p (a b) -> p a b", a=5), "cache_b0")]
        kics = [(kic[:], "kic"), (KT_ckv[:, 0:2048].rearrange("p (a b) -> p a b", a=4), "kic2")]
        V(lambda e: e.memset(KT_ckv[64:128, 0:2048], 0.0), [], ["kic2"])

        def load_a(c, b):
            sl = (c * 4 + b) % 2
            kf, kfn = kifs[sl]; kb_, kbn = kibs[sl]
            D([(kf, c_kidx[b, c * 512:(c + 1) * 512, :].rearrange("(k p) d -> p k d", p=128))], [], [kfn])
            G(lambda e: e.tensor_copy(out=kb_[:, 0:4, :], in_=kf), [kfn], [kbn])

        def load_b(c, b):
            sl = (c * 4 + b) % 2
            kb_, kbn = kibs[sl]; kc_, kcn = kics[c % 2]
            pvx = bankb(6 + sl); pn = "ps%d" % (6 + sl)
            for kk in range(4):
                T(lambda e, kk=kk: e.transpose(pvx[:, kk * 128:(kk + 1) * 128], kb_[:, kk:kk + 2, :].rearrange("p a b -> p (a b)"), ident_b[:]),
                  [kbn, "ident_b", "tpad"], [pn])
            V(lambda e: e.tensor_copy(out=kc_[0:64, b, :], in_=pvx[0:64, 0:512]), [pn], [kcn])

        def compute(c, heads):
            n = 512 if c < 8 else 32
            kc_, kcn = kics[c % 2]
            for h in heads:
                pj = h % 2
                for b in range(4):
                    T(lambda e, h=h, b=b, pj=pj, n=n: e.matmul(bank(pj)[32 * b:32 * b + 32, 0:n], lhsT=qiT[:, h, 32 * b:32 * b + 32], rhs=kc_[:, b, 0:n],
                                                             start=True, stop=True, tile_position=(0, 32 * b)), ["qiT", kcn, "idx_pad"], ["ps%d" % pj])
                idx_epilogue(pj, h, c * 512, n)

        for b in range(4):
            load_a(0, b)
            load_b(0, b)
        for c in range(9):
            nxt = c + 1
            if nxt < 8:
                load_a(nxt, 0); load_a(nxt, 1)
                compute(c, [0, 1])
                load_b(nxt, 0); load_a(nxt, 2)
                compute(c, [2, 3])
                load_b(nxt, 1); load_a(nxt, 3)
                compute(c, [4, 5])
                load_b(nxt, 2)
                compute(c, [6, 7])
                load_b(nxt, 3)
            else:
                if nxt == 8:
                    kc_, kcn = kics[0]
                    for b in range(4):
                        V(lambda e, b=b, kc_=kc_: e.tensor_copy(out=kc_[0:64, b, 0:32], in_=nkidx[:, 32 * b:32 * b + 32]), ["xo"], [kcn])
                compute(c, list(range(8)))
        V(lambda e: e.memset(bis[:, 6:7], 0.0), [], ["kic2", "cache_f0", "cache_b0"] + ["KT%d" % k for k in range(16)])
        pend = [None]

        def push(u):
            u[0]()
            if pend[0] is not None:
                pend[0][1]()
            pend[0] = u

        def load_block(b, blk):
            kk = blk % 2
            k0 = blk * 128
            cf = "cache_f%d" % kk; cb = "cache_b%d" % kk; pb = "ps%d" % (6 + kk)
            D([(cache_f[:, kk, 0:128], c_ckv[b, k0:k0 + 128, :]), (cache_f[:, kk, 128:160], c_kpe[b, k0:k0 + 128, :]),
               (cache_f[:, kk, 160:288], c_k[b, k0:k0 + 128, :]), (cache_f[:, kk, 288:416], c_v[b, k0:k0 + 128, :])], [], [cf])
            G(lambda e: e.tensor_copy(out=cache_b[:, kk, 0:288], in_=cache_f[:, kk, 0:288]), [cf], [cb])
            A(lambda e: e.activation(out=KA_ckv[:, blk, 0:128], in_=cache_f[:, kk, 0:128], func=AF.Copy), [cf], ["KA_ckv%d" % blk])
            V(lambda e: e.tensor_copy(out=KA_v[:, blk, :, 0:64], in_=cache_f[:, kk, 288:416].rearrange("p (g d) -> p g d", g=2)),
              [cf], ["KA_v%d" % blk])
            pv7 = bankb(6 + kk)
            o = 0
            T(lambda e: e.transpose(pv7[:, o:o + 128], cache_b[:, kk, 0:128], ident_b[:]), [cb, "ident_b"], [pb])
            T(lambda e: e.transpose(pv7[:, o + 128:o + 256], cache_b[:, kk, 128:256], ident_b[:]), [cb, "ident_b", "tpad"], [pb])
            for g in range(2):
                T(lambda e, g=g: e.transpose(pv7[:, o + 256 + g * 128:o + 384 + g * 128], cache_b[:, kk, 160 + g * 64:288 + g * 64], ident_b[:]),
                  [cb, "ident_b", "tpad"], [pb])
            cs = slice(blk * 128, (blk + 1) * 128)
            V(lambda e: e.tensor_copy(out=KT_ckv[:, cs], in_=pv7[:, o:o + 128]), [pb], ["KT%d" % blk])
            V(lambda e: e.tensor_copy(out=KT_kpe[0:32, cs], in_=pv7[0:32, o + 128:o + 256]), [pb], ["KT%d" % blk])
            V(lambda e: e.tensor_copy(out=KT_kb[0:64, :, cs], in_=pv7[0:64, o + 256:o + 512].rearrange("p (g k) -> p g k", g=2)),
              [pb], ["KT%d" % blk])

        def new_keys(b):
            cs = slice(4096, 4128)
            V(lambda e: e.tensor_copy(out=KT_ckv[:, cs], in_=nckv[:, 32 * b:32 * b + 32]), ["xo"], ["KT32"])
            V(lambda e: e.tensor_copy(out=KT_kpe[0:32, cs], in_=nkpe[:, 32 * b:32 * b + 32]), ["xo"], ["KT32"])
            for g in range(2):
                V(lambda e, g=g: e.tensor_copy(out=KT_kb[0:64, g, cs], in_=nkb[g][:, 32 * b:32 * b + 32]), ["xo"], ["KT32"])
            D([(KA_ckv[0:32, 32, 0:128], ckv_bt[32 * b:32 * b + 32, :])], ["ckv_bt"], ["KA_ckv32"])
            D([(KA_v[0:32, 32, :, 0:64], v_bt[32 * b:32 * b + 32, :].rearrange("p (g d) -> p g d", g=2))], ["v_bt"], ["KA_v32"])

        on_b = olat_b[:, 0:2, :]
        obn = olat_b[:, 2, :].rearrange("p (g d) -> p g d", g=2)
        olatT_S = CFG_S["olatT"][0]

        def att(b, blk, which="both"):
            nk = 128 if blk < 32 else 32
            bias = 1 if blk == 31 else (0 if blk == 32 else None)
            kcol = blk * 128
            q0 = 32 * b
            ob = 2 + (b % 2)
            kreads = ["KT%d" % blk, "KA_ckv%d" % blk, "KA_v%d" % blk]
            first = (blk == 0)
            if which != "dsa":
                pj = state["sps"]; state["sps"] ^= 1
                pi = state["p"]; state["p"] ^= 1
                sv = bank(pj)[0:nk, 0:256]
                pt = Pt[pi]

                def qk(sv=sv, pj=pj, pi=pi, pt=pt):
                    T(lambda e: e.matmul(sv, lhsT=KT_ckv[:, kcol:kcol + nk], rhs=qlatT[:, :, q0:q0 + 32], start=True, stop=False),
                      kreads + ["qlatT"], ["ps%d" % pj])
                    T(lambda e: e.matmul(sv, lhsT=KT_kpe[:, kcol:kcol + nk], rhs=qpeT[:, :, q0:q0 + 32], start=False, stop=True),
                      kreads + ["qpeT", "kpe_pad"], ["ps%d" % pj])
                    A(lambda e: e.activation(out=pt[0:nk, 0:256], in_=sv, func=AF.Exp, scale=MLA_SCALE), ["ps%d" % pj], ["P%d" % pi])

                def pv(pi=pi, pt=pt):
                    for hg in range(2):
                        T(lambda e, hg=hg: e.matmul(bank(ob)[:, hg * 129:(hg + 1) * 129], lhsT=pt[0:nk, hg * 128:(hg + 1) * 128], rhs=KA_ckv[0:nk, blk, :],
                                                    start=(first and hg == 0), stop=False, skip_group_check=True),
                          ["P%d" % pi, "KA_ones"] + kreads, ["ps%d" % ob])
                push((qk, pv))
            for g in (range(2) if which != "mla" else []):
                pj = state["sps"]; state["sps"] ^= 1
                pi = state["p"]; state["p"] ^= 1
                sv = bank(pj)[0:nk, 0:128]
                pt = Pt[pi]

                def qk(sv=sv, g=g, pj=pj, pi=pi, pt=pt):
                    T(lambda e: e.matmul(sv, lhsT=KT_kb[:, g, kcol:kcol + nk], rhs=qbT[:, 4 * g:4 * g + 4, q0:q0 + 32], start=True, stop=False),
                      kreads + ["qbT", "qbT_aug", "KT_ones"], ["ps%d" % pj])
                    T(lambda e: e.matmul(sv, lhsT=mb[:, kcol:kcol + nk], rhs=Rb[:, b, :], start=False, stop=(bias is None)),
                      ["mb", "Rb"], ["ps%d" % pj])
                    if bias is not None:
                        T(lambda e: e.matmul(sv, lhsT=ident_b[0:nk, 0:nk], rhs=Bt[0:nk, bias, 4 * g:4 * g + 4, 0:32], start=False, stop=True),
                          ["Bt", "ident_b"], ["ps%d" % pj])
                    A(lambda e: e.activation(out=pt[0:nk, 0:128], in_=sv, func=AF.Exp), ["ps%d" % pj], ["P%d" % pi])

                def pv(g=g, pi=pi, pt=pt):
                    T(lambda e: e.matmul(bank(ob)[:, 258 + g * 65:258 + (g + 1) * 65], lhsT=pt[0:nk, 0:128], rhs=KA_v[0:nk, blk, g, :],
                                        start=False, stop=False, skip_group_check=True),
                      ["P%d" % pi, "KA_ones"] + kreads, ["ps%d" % ob])
                push((qk, pv))

        def finalize(b):
            ob = 2 + (b % 2)
            obn_ = "ps%d" % ob
            V(lambda e: e.reciprocal(out=rc[:, 0:2], in_=bank(ob)[:, 0:258].rearrange("p (h c) -> p h c", h=2)[:, :, 128]), [obn_], ["rc"])
            V(lambda e: e.reciprocal(out=rc[:, 2:4], in_=bank(ob)[:, 258:388].rearrange("p (h c) -> p h c", h=2)[:, :, 64]), [obn_], ["rc"])
            for hg in range(2):
                V(lambda e, hg=hg: e.tensor_scalar(out=on_b[:, hg, :], in0=bank(ob)[:, hg * 129:hg * 129 + 128], scalar1=rc[:, hg:hg + 1], scalar2=None, op0=ALU.mult),
                  [obn_, "rc"], ["olat_b"])
            for g in range(2):
                V(lambda e, g=g: e.tensor_scalar(out=obn[:, g, :], in0=bank(ob)[:, 258 + g * 65:258 + g * 65 + 64], scalar1=rc[:, 2 + g:3 + g], scalar2=None, op0=ALU.mult),
                  [obn_, "rc"], ["olat_b"])
            pv4 = bankb(4)
            for hg in range(2):
                T(lambda e, hg=hg: e.transpose(pv4[:, hg * 128:(hg + 1) * 128], on_b[:, hg, :], ident_b[:]), ["olat_b", "ident_b"], ["ps4"])
            for hg in range(2):
                V(lambda e, hg=hg: e.tensor_copy(out=olatT_S[:, 4 * hg:4 * hg + 4, 32 * b:32 * b + 32],
                                                 in_=pv4[:, hg * 128:(hg + 1) * 128].rearrange("p (h q) -> p h q", h=4)), ["ps4"], ["hT"])
            dst = bank(5)[32 * b:32 * b + 32, :].rearrange("p (g h d) -> p g h d", g=2, h=4)
            for hl in range(4):
                for g in range(2):
                    T(lambda e, hl=hl, g=g: e.matmul(dst[:, g, hl, :], lhsT=ident_b[:, hl * 32:(hl + 1) * 32], rhs=obn[:, g, :], start=True, stop=True,
                                                     tile_position=(0, 32 * b), skip_group_check=True), ["olat_b", "ident_b"], ["ps5"])

        def flush():
            if pend[0] is not None:
                pend[0][1]()
                pend[0] = None

        LAG = 3

        def do_load(b, blk):
            if blk < 32:
                load_block(b, blk)
            else:
                new_keys(b)

        def b0_mla():
            for blk in range(33):
                do_load(0, blk)
                if blk >= LAG:
                    att(0, blk - LAG, "mla")
            for blk in range(33 - LAG, 33):
                att(0, blk, "mla")
            flush()

        def sample_bisect():
            for _ in bisect(4128, True):
                pass
        emit_merged(capture(sample_bisect), capture(b0_mla))
        loads = [(b, blk) for b in range(1, 4) for blk in range(33)]
        order = []
        for blk in range(33):
            order.append(("dsa0", 0, blk))
            if blk >= 1:
                order.append(("load",) + loads[blk - 1])
            if blk >= 1 + LAG:
                order.append(("att",) + loads[blk - 1 - LAG])
        for n in range(32, len(loads)):
            order.append(("load",) + loads[n])
            order.append(("att",) + loads[n - LAG])
        for n in range(len(loads) - LAG, len(loads)):
            order.append(("att",) + loads[n])
        for it in order:
            if it[0] == "load":
                do_load(it[1], it[2])
            elif it[0] == "dsa0":
                att(0, it[2], "dsa")
                if it[2] == 32:
                    flush()
                    finalize(0)
            else:
                att(it[1], it[2])
                if it[2] == 32:
                    flush()
                    finalize(it[1])
        combine(y_s, 0, sample=True)

    V(lambda e: e.memset(bis[:, 7:8], 0.0), [],
      ["sc", "mb", "mbj_d", "mbj_a", "cache_f0", "cache_f1", "cache_b0", "cache_b1"] + ALIAS_BUFS +
      ["KT%d" % k for k in range(16, 33)] + ["KA_ckv%d" % k for k in range(16, 33)] + ["KA_v%d" % k for k in range(16, 33)])
    V(lambda e: e.tensor_copy(out=qbT1[64:65, :, :], in_=cfs[64:65, :].unsqueeze(2).to_broadcast([1, 8, 128])), ["smallc"], ["qbT_aug1"])
    V(lambda e: e.tensor_copy(out=qbT2[64:65, :, :], in_=cfs[64:65, :].unsqueeze(2).to_broadcast([1, 8, 128])), ["smallc"], ["qbT_aug2"])
    V(lambda e: e.memset(qpeT1[32:64, :, :], 0.0), [], ["kpe_pad1"])
    V(lambda e: e.memset(qpeT1[64:128, :, :], 0.0), [], ["kpe_pad1"])

    def stage_a(i):
        cur.update(cfg_p(i))
        proj_tile(xp[i * 128:(i + 1) * 128, :], cos_p[:, i, :], sin_p[:, i, :], o_p, i * 128, i, False)
        sc_, scn = cur["sc"]
        N = 128 * (i + 1)
        for c in range((N + 511) // 512):
            n = min(512, N - c * 512)
            for h in range(8):
                pj = 5 + h % 2
                T(lambda e, h=h, pj=pj, c=c, n=n: e.matmul(bank(pj)[:, 0:n], lhsT=qiT[:, h, :], rhs=KT_kidx[:, c * 512:c * 512 + n], start=True, stop=True),
                  ["qiT", "idx_pad"] + ["KT%d" % bb for bb in range(c * 4, min(c * 4 + 4, i + 1))], ["ps%d" % pj])
                idx_epilogue(pj, h, c * 512, n)
        V(lambda e, i=i: e.memset(sc_[0:64, i * 128 + 64:i * 128 + 128], -1e30), [scn], [scn])

    def do_bisect(i):
        cur.update(cfg_p(i))
        for _ in bisect(128 * (i + 1), i >= 2):
            pass

    def do_mla(i):
        cur.update(cfg_p(i))
        mu = []
        for kb in range(i + 1):
            mu += attend_block(0, 128, kb * 128, 128, kb, kb == 0, None, kb == i, False, None, None,
                               ["KT%d" % kb, "KA_ckv%d" % kb, "KA_v%d" % kb], which="mla")
        for _ in run_units(mu):
            pass
        mla_finish()

    def do_dsa_combine(i):
        cur.update(cfg_p(i))
        du = []
        for kb in range(i + 1):
            diag = (kb == i)
            need_mask = (i >= 2) or diag
            bias = 0 if diag else (1 if kb == i - 1 else None)
            du += attend_block(0, 128, kb * 128, 128, kb, kb == 0, None, diag, need_mask, bias, None,
                               ["KT%d" % kb, "KA_ckv%d" % kb, "KA_v%d" % kb], which="dsa")
        for _ in run_units(du):
            pass
        combine(y_p, i * 128)

    def emit_merged_k(lists):
        lists = [l for l in lists if l]
        pos = [0] * len(lists)
        while True:
            best = None
            for k, l in enumerate(lists):
                if pos[k] < len(l):
                    f = pos[k] / float(len(l))
                    if best is None or f < best[0]:
                        best = (f, k)
            if best is None:
                break
            k = best[1]
            it = lists[k][pos[k]]; pos[k] += 1
            if it[0] == "op":
                S.op(it[1], it[2], it[3], it[4])
            else:
                S.dma(it[4], it[1], it[2], it[3])

    if NT:
        stage_a(0)
        do_bisect(0)
        do_mla(0)
        if NT > 1:
            stage_a(1)
    for j in range(NT):
        def main_chain(j=j):
            do_dsa_combine(j)
            if j + 1 < NT:
                do_mla(j + 1)
        l_main = capture(main_chain)
        l_bis = capture(lambda: do_bisect(j + 1)) if j + 1 < NT else []
        l_proj = capture(lambda: stage_a(j + 2)) if j + 2 < NT else []
        emit_merged_k([l_main, l_bis, l_proj])

    outs = [B(x) for x in ["xo", "ckv_f", "kpe_f", "kv_f"]]
    S.final_waits("sync", outs)
    print('total ops', S.nop, {e: S.count[e] for e in ENGS})
    S.build(st)
    st.close()
    return nc


def _t5_bucket(rel):
    rel = np.asarray(rel, np.int64)
    nb, me = 16, 8
    ret = (rel > 0).astype(np.int64) * nb
    n = np.abs(rel)
    nf = np.maximum(n, 1).astype(np.float32)
    large = me + (np.log(nf / np.float32(me)) / np.float32(math.log(128 / 8)) * np.float32(nb - me)).astype(np.int32)
    large = np.minimum(large, nb - 1)
    return ret + np.where(n < me, n, large)


def _rope_tables(pos):
    half = 16
    inv = (np.float32(10000.0) ** (-np.arange(half, dtype=np.float32) / np.float32(half))).astype(np.float32)
    ang = pos.astype(np.float32)[:, None] * inv[None, :]
    return np.cos(ang).astype(np.float32), np.sin(ang).astype(np.float32)


_CACHE = {}


def make_in_maps(x_prompt, x_sample, cache_mla_ckv, cache_mla_kpe, cache_dsa_k, cache_dsa_v, cache_dsa_kidx,
                 norm_g, w_in, mla_q_norm_g, mla_kv_norm_g, mla_w_uq, mla_w_uk, mla_w_uv, rel_bias, w_out, final_norm_g):
    f = lambda a: np.ascontiguousarray(np.asarray(a, dtype=np.float32))
    x_prompt, x_sample = f(x_prompt), f(x_sample)
    rel_bias = f(rel_bias)
    cos_p, sin_p = _rope_tables(np.arange(2048))
    cos_s, sin_s = _rope_tables(4096 + (np.arange(128) % 32))
    kl = np.arange(128)[:, None]; ql = np.arange(128)[None, :]
    tb = np.stack([rel_bias[_t5_bucket(kl - ql)], rel_bias[_t5_bucket(kl - 128 - ql)]])
    tb = f(tb.transpose(0, 1, 3, 2).reshape(2, 128, 1024))
    shared = {
        "normg": f(f(norm_g).reshape(8, 128).T), "w_in": f(w_in[0]), "qng": f(f(mla_q_norm_g).reshape(2, 128).T),
        "kvng_bc": f(np.broadcast_to(f(mla_kv_norm_g).reshape(1, 128), (128, 128))),
        "w_uq": f(f(mla_w_uq)[0].reshape(256, 768)), "w_uk": f(f(mla_w_uk)[0].reshape(128, 512)),
        "w_uv": f(f(mla_w_uv)[0].reshape(128, 512)), "w_out": f(f(w_out)[0]),
        "fng_bc": f(np.broadcast_to(f(final_norm_g).reshape(1, 1024), (128, 1024))),
        "cfar_bc": f(np.broadcast_to(rel_bias[15:16, :], (128, 8))), "tbias": tb,
        "cos_p": cos_p, "sin_p": sin_p, "cos_s": cos_s, "sin_s": sin_s,
    }
    in_maps = []
    for c in range(8):
        m = dict(shared)
        m["xp"] = x_prompt[c]
        m["xs"] = x_sample[4 * c:4 * c + 4].reshape(128, 1024)
        m["c_ckv"] = f(cache_mla_ckv[0, 4 * c:4 * c + 4]); m["c_kpe"] = f(cache_mla_kpe[0, 4 * c:4 * c + 4])
        m["c_k"] = f(cache_dsa_k[0, 4 * c:4 * c + 4]).reshape(4, 4096, 128)
        m["c_v"] = f(cache_dsa_v[0, 4 * c:4 * c + 4]).reshape(4, 4096, 128)
        m["c_kidx"] = f(cache_dsa_kidx[0, 4 * c:4 * c + 4])
        in_maps.append(m)
    return in_maps


def kernel(**inputs):
    if "nc" not in _CACHE:
        _CACHE["nc"] = build_program()
    nc = _CACHE["nc"]
    in_maps = make_in_maps(**inputs)
    res = run_bass_kernel_spmd(nc, in_maps, core_ids=list(range(8)))
    R = res.results
    cat = lambda k: np.stack([R[c][k] for c in range(8)])
    y_p = cat("y_p")
    y_s = cat("y_s").reshape(32, 32, 1024)
    outs = [y_p, y_s]
    for k, shp in [("o_ckv_p", (1, 8, 2048, 128)), ("o_kpe_p", (1, 8, 2048, 32)), ("o_k_p", (1, 8, 2048, 2, 64)),
                   ("o_v_p", (1, 8, 2048, 2, 64)), ("o_kidx_p", (1, 8, 2048, 64))]:
        outs.append(cat(k).reshape(shp))
    for k, shp in [("o_ckv_s", (1, 32, 32, 128)), ("o_kpe_s", (1, 32, 32, 32)), ("o_k_s", (1, 32, 32, 2, 64)),
                   ("o_v_s", (1, 32, 32, 2, 64)), ("o_kidx_s", (1, 32, 32, 64))]:
        outs.append(cat(k).reshape(shp))
    return tuple(np.ascontiguousarray(o.astype(np.float32)) for o in outs)
```

```python
import math
from contextlib import ExitStack

import numpy as np
import concourse.bass as bass
import concourse.mybir as mybir
from concourse.bass_utils import run_bass_kernel_spmd

F32 = mybir.dt.float32
BF16 = mybir.dt.bfloat16
ALU = mybir.AluOpType
AF = mybir.ActivationFunctionType

ENGS = ("tensor", "vector", "scalar", "gpsimd", "sync")
EPOCH = 12000
import os
LIMIT = int(os.environ.get('KLIMIT', '100000000'))

CQ, CKV, KPE, GA, QB, KB, VB, KIDX, WIDX, QIDX, GB = 0, 256, 384, 416, 928, 1440, 1568, 1696, 1760, 1768, 2280
GROUPS = [(0, 416), (416, 928), (928, 1440), (1440, 1768), (1768, 2280), (2280, 2792)]
MLA_SCALE = 96.0 ** -0.5
NEG = -30000.0
NITER = 24


class Buf:
    __slots__ = ("name", "w", "r", "const")

    def __init__(self, name):
        self.name = name
        self.w = None
        self.r = {}
        self.const = False


class Sched:
    def __init__(self, nc):
        self.nc = nc
        self.ops = {e: [] for e in ENGS}
        self.count = {e: 0 for e in ENGS}
        self.seen = {e: {} for e in ENGS}
        self.semkeys = []
        self.dmacount = {}

    def _deps(self, reads, writes, eng=None):
        deps = {}

        def add(ev, raw):
            if ev is None:
                return
            k, v = ev
            if not raw and eng is not None and k[0] == eng:
                return
            if deps.get(k, 0) < v:
                deps[k] = v
        for b in reads:
            add(b.w, True)
        for b in writes:
            add(b.w, False)
            for k, v in b.r.items():
                add((k, v), False)
        return deps

    def _waits(self, eng, deps):
        waits = []
        for k, v in deps.items():
            if eng == "tensor" and k[0] == "tensor":
                continue
            if self.seen[eng].get(k, 0) >= v:
                continue
            self.seen[eng][k] = v
            waits.append((k, v))
        return waits

    def _mark(self, ev, reads, writes):
        k, v = ev
        for b in reads:
            if not b.const and b.r.get(k, 0) < v:
                b.r[k] = v
        for b in writes:
            b.w = ev
            b.r = {}

    def op(self, eng, fn, reads=(), writes=()):
        self.nop = getattr(self, "nop", 0) + 1
        if os.environ.get('KTRACE'):
            import sys as _s
            print("OP", self.nop, eng, _s._getframe(2).f_lineno, _s._getframe(3).f_lineno)
        if self.nop > LIMIT:
            return
        deps = self._deps(reads, writes, eng)
        waits = self._waits(eng, deps)
        self.count[eng] += 1
        n = self.count[eng]
        key = (eng, (n - 1) // EPOCH)
        if key not in self.semkeys:
            self.semkeys.append(key)
        val = (n - 1) % EPOCH + 1
        self.ops[eng].append((waits, fn, key, 1))
        self._mark((key, val), reads, writes)

    def dma(self, q, pairs, reads=(), writes=()):
        self.nop = getattr(self, "nop", 0) + 1
        if self.nop > LIMIT:
            return
        owner = (writes[0] if writes else reads[0])
        key = ("dma", owner.name)
        if key not in self.dmacount:
            self.dmacount[key] = 0
            self.semkeys.append(key)
        deps = self._deps(reads, writes)
        waits = self._waits(q, deps)
        first = True
        for (o, i) in pairs:
            self.dmacount[key] += 16
            self.ops[q].append((waits if first else [],
                                (lambda e, o=o, i=i: e.dma_start(out=o, in_=i)), key, 16))
            first = False
        self._mark((key, self.dmacount[key]), reads, writes)

    def final_waits(self, eng, bufs):
        deps = self._deps((), bufs)
        waits = self._waits(eng, deps)
        self.ops[eng].append((waits, None, None, 0))

    def build(self, st):
        nc = self.nc
        sems = {}
        for k in self.semkeys:
            sems[k] = st.enter_context(nc.semaphore("s_" + "_".join(str(x) for x in k)))
        block = st.enter_context(nc.Block())

        def mk(engname):
            def body(e):
                for (waits, fn, key, inc) in self.ops[engname]:
                    for (k, v) in waits:
                        e.wait_ge(sems[k], v)
                    if fn is not None:
                        fn(e).then_inc(sems[key], inc)
            return body

        for engname in ENGS:
            if self.ops[engname]:
                getattr(block, engname)(mk(engname))


def build_program(NT=16, SAMPLE=True):
    nc = bass.Bass("TRN2", target_bir_lowering=False)
    st = ExitStack()

    def din(name, shape):
        return nc.dram_tensor(name, shape, F32, kind="ExternalInput").ap()

    def dout(name, shape):
        return nc.dram_tensor(name, shape, F32, kind="ExternalOutput").ap()

    xp = din("xp", [2048, 1024]); xs = din("xs", [128, 1024])
    c_ckv = din("c_ckv", [4, 4096, 128]); c_kpe = din("c_kpe", [4, 4096, 32])
    c_k = din("c_k", [4, 4096, 128]); c_v = din("c_v", [4, 4096, 128]); c_kidx = din("c_kidx", [4, 4096, 64])
    d_normg = din("normg", [128, 8]); d_win = din("w_in", [1024, 2792]); d_qng = din("qng", [128, 2])
    d_kvng = din("kvng_bc", [128, 128]); d_wuq = din("w_uq", [256, 768]); d_wuk = din("w_uk", [128, 512])
    d_wuv = din("w_uv", [128, 512]); d_wout = din("w_out", [1024, 1024]); d_fng = din("fng_bc", [128, 1024])
    d_cfar = din("cfar_bc", [128, 8]); d_tb = din("tbias", [2, 128, 1024])
    d_cosp = din("cos_p", [2048, 16]); d_sinp = din("sin_p", [2048, 16])
    d_coss = din("cos_s", [128, 16]); d_sins = din("sin_s", [128, 16])
    y_p = dout("y_p", [2048, 1024]); y_s = dout("y_s", [128, 1024])
    o_p = {"ckv": dout("o_ckv_p", [2048, 128]), "kpe": dout("o_kpe_p", [2048, 32]), "k": dout("o_k_p", [2048, 128]),
           "v": dout("o_v_p", [2048, 128]), "kidx": dout("o_kidx_p", [2048, 64])}
    o_s = {"ckv": dout("o_ckv_s", [128, 128]), "kpe": dout("o_kpe_s", [128, 32]), "k": dout("o_k_s", [128, 128]),
           "v": dout("o_v_s", [128, 128]), "kidx": dout("o_kidx_s", [128, 64])}

    S = Sched(nc)
    bufs = {}

    def B(name):
        if name not in bufs:
            bufs[name] = Buf(name)
        return bufs[name]

    def sb(name, shape, dt):
        return st.enter_context(nc.sbuf_tensor("sb_" + name, shape, dt))

    w_in_b = sb("w_in_b", [128, 8, 2792], BF16)
    wcomb_b = sb("wcomb_b", [128, 2, 8, 128], BF16)
    wuqpe_b = sb("wuqpe_b", [128, 2, 256], BF16)
    wuv_b = sb("wuv_b", [128, 512], BF16)
    wout_b = sb("wout_b", [128, 8, 1024], BF16)
    ident_b = sb("ident_b", [128, 128], BF16)
    ident_f = sb("ident_f", [128, 128], F32)
    I4 = sb("I4", [128, 512], BF16)
    Rb = sb("Rb", [128, 4, 128], BF16)
    normg_t = sb("normg_t", [128, 8], F32)
    qng_t = sb("qng_t", [128, 2], F32)
    kvng_bc = sb("kvng_bc", [128, 128], F32)
    fng_bc = sb("fng_bc", [128, 1024], F32)
    cfs = sb("cfs", [128, 8], F32)
    cos_p = sb("cos_pt", [128, 16, 16], F32); sin_p = sb("sin_pt", [128, 16, 16], F32)
    cos_s = sb("cos_st", [128, 16], F32); sin_s = sb("sin_st", [128, 16], F32)
    Bt = sb("Bt", [128, 2, 8, 128], BF16)
    cst = sb("cst", [128, 16], F32)
    KT_ckv = sb("KT_ckv", [128, 4224], BF16)
    KT_kpe = sb("KT_kpe", [128, 4224], BF16)
    KT_kb = sb("KT_kb", [65, 2, 4224], BF16)
    KT_kidx = sb("KT_kidx", [128, 2048], BF16)
    KA_ckv = sb("KA_ckv", [128, 33, 129], BF16)
    KA_v = sb("KA_v", [128, 33, 2, 65], BF16)
    xt = sb("xt", [128, 1024], F32)
    xo = sb("xo", [128, 1024], F32)
    mix = sb("mix", [128, 1024], BF16)
    hT = sb("hT", [128, 8, 128], BF16)
    cq_b = sb("cq_b", [128, 256], BF16)
    cqT = sb("cqT", [128, 2, 128], BF16)
    qlatT = sb("qlatT", [128, 8, 128], BF16)
    rt = sb("rt", [128, 6, 8, 16], F32)
    qpe_b = sb("qpe_b", [128, 352], BF16)
    qpeT = sb("qpeT", [128, 8, 128], BF16)
    ckv_f = sb("ckv_f", [128, 128], F32)
    ckv_bt = sb("ckv_bt", [128, 128], BF16)
    kpe_f = sb("kpe_f", [128, 32], F32)
    kpe_bt = sb("kpe_bt", [128, 128], BF16)
    kv_f = sb("kv_f", [128, 328], F32)
    kb_bt = sb("kb_bt", [128, 256], BF16)
    v_bt = sb("v_bt", [128, 128], BF16)
    qbT = sb("qbT", [65, 8, 128], BF16)
    qb_bt = sb("qb_bt", [128, 576], BF16)
    qi_bt = sb("qi_bt", [128, 576], BF16)
    qiT = sb("qiT", [128, 8, 128], BF16)
    sga = sb("sga", [128, 512], BF16)
    sgb = sb("sgb", [128, 512], BF16)
    wabs = sb("wabs", [128, 8], F32)
    wsgn = sb("wsgn", [128, 8], F32)
    stt = sb("stt", [128, 16], F32)
    sc = sb("sc", [128, 4128], F32)
    mb = sb("mb", [128, 4128], BF16)
    rr = [sb("rr0", [128, 512], F32), sb("rr1", [128, 512], F32)]
    bis = sb("bis", [128, 8], F32)
    Pt = [sb("P0", [128, 512], BF16), sb("P1", [128, 512], BF16)]
    rc = sb("rc", [128, 16], F32)
    olat_b = sb("olat_b", [128, 8, 128], BF16)
    cache_f = sb("cache_f", [128, 2, 416], F32)
    cache_b = sb("cache_b", [128, 2, 352], BF16)
    kic = sb("kic", [128, 4, 512], BF16)
    kif = sb("kif", [128, 4, 64], F32)
    kib = sb("kib", [128, 5, 64], BF16)
    print('SBUF bytes remaining', nc.sbuf_bytes_remaining)
    PSX = [st.enter_context(nc.psum_tensor("psx%d" % j, [128, 1024], F32)) for j in range(4)]

    sc0 = sc[:, 0:2048]; sc1 = sc[:, 2048:4096]
    mb0 = mb[:, 0:2048]; mb1 = mb[:, 2048:4096]
    xt1 = KT_ckv[:, 2048:4096].bitcast(F32)
    xt2 = KT_kpe[:, 2048:4096].bitcast(F32)
    kaf = KA_ckv[:, 16:32, :].rearrange("p a b -> p (a b)")
    qlatT1 = kaf[:, 0:1024].rearrange("p (a b) -> p a b", a=8)
    qpeT1 = kaf[:, 1024:2048].rearrange("p (a b) -> p a b", a=8)
    kvf = KA_v[:, 16:33, :, :].rearrange("p a g d -> p (a g d)")
    olatT_p = kvf[:, 0:1024].rearrange("p (a b) -> p a b", a=8)
    mixT_p = kvf[:, 1024:2048].rearrange("p (a b) -> p a b", a=8)
    qbT1 = KT_kb[0:65, 0, 2048:3072].rearrange("p (a b) -> p a b", a=8)
    cff = cache_f[:].rearrange("p a b -> p (a b)").bitcast(BF16)
    hb_p = cff[:, 0:1024]
    sga1 = cff[:, 1024:1536]
    sgb1 = cache_b[:].rearrange("p a b -> p (a b)")[:, 0:512]
    qbT2 = KT_kb[0:65, 1, 2048:3072].rearrange("p (a b) -> p a b", a=8)
    sga2 = kif[:].rearrange("p a b -> p (a b)").bitcast(BF16)
    sgb2 = Rb[:].rearrange("p a b -> p (a b)")
    ALIAS_BUFS = ["sc1", "mb1", "mbj_d1", "mbj_a1", "xt1", "xt2", "qlatT1", "qpeT1", "kpe_pad1", "olatT", "mixT", "qbT1", "qbT_aug1",
                  "hb", "sga1", "sgb1", "qbT2", "qbT_aug2", "sga2", "sgb2", "kif", "Rb"]
    cur = {}
    OD_S = [(5 + h // 4, (h % 4) * 65) for h in range(8)]
    OD_P = [(2 + h // 4, (h % 4) * 65) for h in range(8)]
    CFG_S = dict(xt=(xt[:], "xt"), hb=(mix[:], "mix"), olatT=(hT[:], "hT"), mixT=(qlatT[:], "qlatT"),
                 qbT=(qbT[:], "qbT", "qbT_aug"), sga=(sga[:], "sga"), sgb=(sgb[:], "sgb"),
                 sc=(sc[:], "sc"), mb=(mb[:], "mb", "mbj_d", "mbj_a"), qlatT=(qlatT[:], "qlatT"), qpeT=(qpeT[:], "qpeT", "kpe_pad"), OD=OD_S)

    def cfg_p(t):
        p = t % 2
        return dict(xt=[(xt[:], "xt"), (xt1, "xt1"), (xt2, "xt2")][t % 3], hb=(hb_p, "hb"), olatT=(olatT_p, "olatT"), mixT=(mixT_p, "mixT"),
                    qbT=[(qbT[:], "qbT", "qbT_aug"), (qbT1, "qbT1", "qbT_aug1"), (qbT2, "qbT2", "qbT_aug2")][t % 3],
                    sga=[(sga[:], "sga"), (sga1, "sga1"), (sga2, "sga2")][t % 3], sgb=[(sgb[:], "sgb"), (sgb1, "sgb1"), (sgb2, "sgb2")][t % 3],
                    sc=[(sc0, "sc"), (sc1, "sc1")][p], mb=[(mb0, "mb", "mbj_d", "mbj_a"), (mb1, "mb1", "mbj_d1", "mbj_a1")][p],
                    qlatT=[(qlatT[:], "qlatT"), (qlatT1, "qlatT1")][p], qpeT=[(qpeT[:], "qpeT", "kpe_pad"), (qpeT1, "qpeT1", "kpe_pad1")][p],
                    OD=OD_P)

    def bank(j):
        return PSX[j // 2][:, (j % 2) * 512:(j % 2 + 1) * 512]

    def bankb(j):
        return bank(j).bitcast(BF16)

    def PB(j):
        return B("ps%d" % j)

    capl = [None]

    def _op(eng, fn, reads, writes):
        r = [B(x) for x in reads]; w = [B(x) for x in writes]
        if capl[0] is not None:
            capl[0].append(("op", eng, fn, r, w))
        else:
            S.op(eng, fn, r, w)

    def V(fn, reads, writes):
        _op("vector", fn, reads, writes)

    def A(fn, reads, writes):
        _op("scalar", fn, reads, writes)

    def G(fn, reads, writes):
        _op("gpsimd", fn, reads, writes)

    def T(fn, reads, writes):
        _op("tensor", fn, reads, writes)

    def D(pairs, reads, writes, q="sync"):
        r = [B(x) for x in reads]; w = [B(x) for x in writes]
        if capl[0] is not None:
            capl[0].append(("dma", pairs, r, w, q))
        else:
            S.dma(q, pairs, r, w)

    def capture(fn):
        capl[0] = []
        fn()
        l = capl[0]
        capl[0] = None
        return l

    def emit_merged(l1, l2):
        a = b = 0
        n1, n2 = len(l1), len(l2)
        while a < n1 or b < n2:
            if a < n1 and (b >= n2 or a * n2 <= b * n1):
                it = l1[a]; a += 1
            else:
                it = l2[b]; b += 1
            if it[0] == "op":
                S.op(it[1], it[2], it[3], it[4])
            else:
                S.dma(it[4], it[1], it[2], it[3])

    G(lambda e: e.memset(ident_b[:], 1.0), [], ["ident_b"])
    G(lambda e: e.affine_select(out=ident_b[:], in_=ident_b[:], pattern=[[-1, 128]], compare_op=ALU.is_equal,
                                fill=0.0, base=0, channel_multiplier=1), ["ident_b"], ["ident_b"])
    G(lambda e: e.memset(ident_f[:], 1.0), [], ["ident_f"])
    G(lambda e: e.affine_select(out=ident_f[:], in_=ident_f[:], pattern=[[-1, 128]], compare_op=ALU.is_equal,
                                fill=0.0, base=0, channel_multiplier=1), ["ident_f"], ["ident_f"])
    for j in range(4):
        V(lambda e, j=j: e.tensor_copy(out=I4[:, j * 128:(j + 1) * 128], in_=ident_b[:]), ["ident_b"], ["I4"])
    V(lambda e: e.tensor_tensor(out=rr[0][:, 0:32], in0=ident_f[:, 0:32], in1=ident_f[:, 32:64], op=ALU.add), ["ident_f"], ["rr0"])
    V(lambda e: e.tensor_tensor(out=rr[0][:, 0:32], in0=rr[0][:, 0:32], in1=ident_f[:, 64:96], op=ALU.add), ["ident_f", "rr0"], ["rr0"])
    V(lambda e: e.tensor_tensor(out=rr[0][:, 0:32], in0=rr[0][:, 0:32], in1=ident_f[:, 96:128], op=ALU.add), ["ident_f", "rr0"], ["rr0"])
    for b in range(4):
        V(lambda e, b=b: e.reduce_sum(out=rc[:, b:b + 1], in_=ident_f[:, 32 * b:32 * b + 32], axis=mybir.AxisListType.X), ["ident_f"], ["rc"])
        for h in range(4):
            V(lambda e, b=b, h=h: e.tensor_scalar(out=Rb[:, b, h * 32:(h + 1) * 32], in0=rr[0][:, 0:32], scalar1=rc[:, b:b + 1],
                                                  scalar2=None, op0=ALU.mult), ["rr0", "rc"], ["Rb"])
    V(lambda e: e.memset(cst[:, 0:1], -0.5), [], ["cst"])
    V(lambda e: e.memset(cst[:, 1:2], 1e-6), [], ["cst"])
    D([(normg_t[:], d_normg), (qng_t[:], d_qng), (kvng_bc[:], d_kvng), (fng_bc[:], d_fng), (cfs[:], d_cfar),
       (cos_s[:], d_coss), (sin_s[:], d_sins),
       (cos_p[:], d_cosp.rearrange("(t p) r -> p t r", p=128)), (sin_p[:], d_sinp.rearrange("(t p) r -> p t r", p=128))],
      [], ["smallc"])
    V(lambda e: e.tensor_copy(out=qbT[64:65, :, :], in_=cfs[64:65, :].unsqueeze(2).to_broadcast([1, 8, 128])), ["smallc"], ["qbT_aug"])
    wst = [(xt, "xt"), (xo, "xo"), (sc[:, 0:1024], "scst0"), (sc[:, 1024:2048], "scst1"), (sc[:, 2048:3072], "scst2"), (sc[:, 3072:4096], "scst3")]
    k = 0
    for kc in range(8):
        for (c0, c1) in [(0, 1024), (1024, 2048), (2048, 2792)]:
            t_, tn = wst[k % 6]; k += 1
            n = c1 - c0
            D([(t_[:, 0:n], d_win[kc * 128:(kc + 1) * 128, c0:c1])], [], [tn])
            segs = []
            for (s0, s1, d0, sc_) in [(0, 928, 0, 1.0), (928, 1440, 928, 0.125), (1440, 1696, 1440, 1.0),
                                      (2208, 2280, 1696, 1.0), (1696, 2208, 1768, 0.125), (2280, 2792, 2280, 1.0)]:
                a0, a1 = max(s0, c0), min(s1, c1)
                if a0 < a1:
                    segs.append((a0, a1, d0 + (a0 - s0), sc_))
            for (a0, a1, dd, sc_) in segs:
                eng = V if (k % 2) else G
                eng(lambda e, t_=t_, a0=a0, a1=a1, dd=dd, sc_=sc_, kc=kc, c0=c0:
                    e.tensor_scalar(out=w_in_b[:, kc, dd:dd + (a1 - a0)], in0=t_[:, a0 - c0:a1 - c0],
                                    scalar1=normg_t[:, kc:kc + 1], scalar2=sc_, op0=ALU.mult, op1=ALU.mult),
                    [tn, "smallc"], ["w_in_b"])
    for ec in range(8):
        t_, tn = wst[k % 6]; k += 1
        D([(t_[:], d_wout[ec * 128:(ec + 1) * 128, :])], [], [tn])
        (V if ec % 2 else G)(lambda e, t_=t_, ec=ec: e.tensor_copy(out=wout_b[:, ec, :], in_=t_[:]), [tn], ["wout_b"])
    D([(xt[:, 0:512], d_wuv)], [], ["xt"])
    V(lambda e: e.tensor_copy(out=wuv_b[:], in_=xt[:, 0:512]), ["xt"], ["wuv_b"])
    D([(xo[:, 0:768], d_wuq[0:128, :]), (xo[:, 768:1024], d_wuk[:, 0:256])], [], ["xo"])
    D([(xt[:, 0:768], d_wuq[128:256, :]), (xt[:, 768:1024], d_wuk[:, 256:512])], [], ["xt"])
    wq = [xo, xt]; wqn = ["xo", "xt"]
    for jc in range(2):
        V(lambda e, jc=jc: e.tensor_scalar(out=wq[jc][:, 0:768], in0=wq[jc][:, 0:768], scalar1=qng_t[:, jc:jc + 1], scalar2=None,
                                           op0=ALU.mult), [wqn[jc], "smallc"], [wqn[jc]])
        V(lambda e, jc=jc: e.tensor_copy(out=wuqpe_b[:, jc, :].rearrange("p (h r) -> p h r", h=8),
                                         in_=wq[jc][:, 0:768].rearrange("p (h e) -> p h e", h=8)[:, :, 64:96]),
          [wqn[jc]], ["wuqpe_b"])
    uqT = sc[0:64, 0:2048].rearrange("p (h j) -> p h j", h=8)
    ukT = sc[0:64, 2048:3072].rearrange("p (h c) -> p h c", h=8)
    for h in range(8):
        for jc in range(2):
            T(lambda e, h=h, jc=jc: e.matmul(bank(0)[0:64, 0:128], lhsT=wq[jc][:, h * 96:h * 96 + 64], rhs=ident_f[:], start=True, stop=True),
              [wqn[jc], "ident_f"], ["ps0"])
            V(lambda e, h=h, jc=jc: e.tensor_copy(out=uqT[:, h, jc * 128:(jc + 1) * 128], in_=bank(0)[0:64, 0:128]), ["ps0"], ["sc", "scst0", "scst1", "scst2", "scst3"])
        src, srcn, off = (xo, "xo", 768 + h * 64) if h < 4 else (xt, "xt", 768 + (h - 4) * 64)
        T(lambda e, src=src, off=off: e.matmul(bank(1)[0:64, 0:128], lhsT=src[:, off:off + 64], rhs=ident_f[:], start=True, stop=True),
          [srcn, "ident_f"], ["ps1"])
        V(lambda e, h=h: e.tensor_copy(out=ukT[:, h, :], in_=bank(1)[0:64, 0:128]), ["ps1"], ["sc", "scst0", "scst1", "scst2", "scst3"])
    for h in range(8):
        for jc in range(2):
            pj = 2 + (h * 2 + jc) % 2
            T(lambda e, h=h, jc=jc, pj=pj: e.matmul(bank(pj)[:, 0:128], lhsT=uqT[:, h, jc * 128:(jc + 1) * 128], rhs=ukT[:, h, :],
                                                   start=True, stop=True), ["sc"], ["ps%d" % pj])
            A(lambda e, h=h, jc=jc, pj=pj: e.activation(out=wcomb_b[:, jc, h, :], in_=bank(pj)[:, 0:128], func=AF.Copy),
              ["ps%d" % pj], ["wcomb_b"])

    V(lambda e: e.memset(KA_ckv[:, :, 128:129], 1.0), [], ["KA_ones"])
    G(lambda e: e.memset(KA_v[:, :, :, 64:65], 1.0), [], ["KA_ones"])
    V(lambda e: e.memset(KT_kb[64:65, :, :], 1.0), [], ["KT_ones"])
    V(lambda e: e.memset(qpe_b[:, 256:352], 0.0), [], ["tpad"])
    V(lambda e: e.memset(kpe_bt[:, 32:128], 0.0), [], ["tpad"])
    V(lambda e: e.memset(kb_bt[:, 192:256], 0.0), [], ["tpad"])
    V(lambda e: e.memset(qb_bt[:, 512:576], 0.0), [], ["tpad"])
    V(lambda e: e.memset(qi_bt[:, 512:576], 0.0), [], ["tpad"])
    G(lambda e: e.memset(cache_b[:, :, 288:352], 0.0), [], ["tpad"])
    V(lambda e: e.memset(kib[:, 4, :], 0.0), [], ["tpad"])
    G(lambda e: e.memset(KT_kidx[64:128, :], 0.0), [], ["idx_pad"])
    V(lambda e: e.memset(qiT[64:128, :, :], 0.0), [], ["idx_pad"])
    G(lambda e: e.memset(kic[64:128, :, :], 0.0), [], ["idx_pad"])
    V(lambda e: e.memset(KT_kpe[32:64, :], 0.0), [], ["kpe_pad"])
    G(lambda e: e.memset(KT_kpe[64:128, :], 0.0), [], ["kpe_pad"])
    V(lambda e: e.memset(qpeT[32:64, :, :], 0.0), [], ["kpe_pad"])
    G(lambda e: e.memset(qpeT[64:128, :, :], 0.0), [], ["kpe_pad"])
    for typ in range(2):
        D([(xo[:], d_tb[typ])], [], ["xo"])
        for h in range(8):
            V(lambda e, typ=typ, h=h: e.tensor_scalar(out=Bt[:, typ, h, :], in0=xo[:, h * 128:(h + 1) * 128], scalar1=cfs[:, h:h + 1],
                                                      scalar2=None, op0=ALU.subtract), ["xo", "smallc"], ["Bt"])

    def rope(src, cos, sin, nh, outb, outf, rd, wr):
        cb = cos.unsqueeze(1).to_broadcast([128, nh, 16]); sbb = sin.unsqueeze(1).to_broadcast([128, nh, 16])
        x1 = src[:, :, 0:16]; x2 = src[:, :, 16:32]
        t = [rt[:, i, 0:nh, :] for i in range(6)]
        V(lambda e: e.tensor_tensor(out=t[0], in0=x1, in1=cb, op=ALU.mult), rd, ["rt"])
        V(lambda e: e.tensor_tensor(out=t[1], in0=x2, in1=sbb, op=ALU.mult), rd, ["rt"])
        V(lambda e: e.tensor_tensor(out=t[2], in0=x1, in1=sbb, op=ALU.mult), rd, ["rt"])
        V(lambda e: e.tensor_tensor(out=t[3], in0=x2, in1=cb, op=ALU.mult), rd, ["rt"])
        tgt = outf if outf is not None else outb
        V(lambda e: e.tensor_tensor(out=tgt[:, :, 0:16], in0=t[0], in1=t[1], op=ALU.subtract), ["rt"], wr)
        V(lambda e: e.tensor_tensor(out=tgt[:, :, 16:32], in0=t[2], in1=t[3], op=ALU.add), ["rt"], wr)
        if outf is not None:
            V(lambda e: e.tensor_copy(out=outb, in_=outf), wr, wr + ["rope_b"])

    def rstd_from(ssq_ap, out_ap, n, rd, wr):
        V(lambda e: e.tensor_scalar(out=out_ap, in0=ssq_ap, scalar1=1.0 / n, scalar2=1e-6, op0=ALU.mult, op1=ALU.add), rd, wr)
        G(lambda e: e.tensor_tensor(out=out_ap, in0=out_ap, in1=cst[:, 0:1], op=ALU.pow), wr + ["cst"], wr)

    def transposes(items, evac_out, evac_reads_extra=()):
        pass

    def proj_tile(xsrc, cos, sin, odst, row0, blk, sample):
        xt_, xtn = cur["xt"]; hb_, hbn = cur["hb"]
        qbT_, qbTn, _ = cur["qbT"]; sga_, sgan = cur["sga"]; sgb_, sgbn = cur["sgb"]
        D([(xt_, xsrc)], [], [xtn])
        A(lambda e: e.activation(out=hb_, in_=xt_, func=AF.Square, accum_out=stt[:, 0:1]), [xtn], [hbn, "stt0"])
        rstd_from(stt[:, 0:1], stt[:, 1:2], 1024.0, ["stt0"], ["stt1"])
        V(lambda e: e.tensor_scalar(out=hb_, in0=xt_, scalar1=stt[:, 1:2], scalar2=None, op0=ALU.mult), [xtn, "stt1"], [hbn])
        hv = bankb(7).rearrange("p (a b) -> p a b", a=8)
        for kc in range(8):
            T(lambda e, kc=kc: e.transpose(hv[:, kc, :], hb_[:, kc * 128:(kc + 1) * 128], ident_b[:]), [hbn, "ident_b"], ["ps7"])
        A(lambda e: e.activation(out=hT[:].rearrange("p a b -> p (a b)"), in_=bankb(7), func=AF.Copy), ["ps7"], ["hT"])
        def group(gi, pj):
            c0, c1 = GROUPS[gi]
            for kc in range(8):
                T(lambda e, kc=kc, c0=c0, c1=c1, pj=pj: e.matmul(bank(pj)[:, 0:c1 - c0], lhsT=hT[:, kc, :], rhs=w_in_b[:, kc, c0:c1],
                                                               start=(kc == 0), stop=(kc == 7)), ["hT", "w_in_b"], ["ps%d" % pj])
        group(0, 5)
        pa = bank(5)
        A(lambda e: e.activation(out=hb_[:, 0:256], in_=pa[:, 0:256], func=AF.Square, accum_out=stt[:, 2:3]), ["ps5"], [hbn, "stt2"])
        A(lambda e: e.activation(out=hb_[:, 256:384], in_=pa[:, 256:384], func=AF.Square, accum_out=stt[:, 3:4]), ["ps5"], [hbn, "stt3"])
        rstd_from(stt[:, 2:3], stt[:, 4:5], 256.0, ["stt2"], ["stt4"])
        rstd_from(stt[:, 3:4], stt[:, 5:6], 128.0, ["stt3"], ["stt5"])
        V(lambda e: e.tensor_scalar(out=cq_b[:], in0=pa[:, 0:256], scalar1=stt[:, 4:5], scalar2=None, op0=ALU.mult), ["ps5", "stt4"], ["cq_b"])
        V(lambda e: e.scalar_tensor_tensor(out=ckv_f[:], in0=pa[:, 256:384], scalar=stt[:, 5:6], in1=kvng_bc[:], op0=ALU.mult, op1=ALU.mult),
          ["ps5", "stt5", "smallc"], ["ckv_f"])
        rope(pa[:, 384:416].rearrange("p (h r) -> p h r", h=1), cos, sin, 1, kpe_bt[:, 0:32].rearrange("p (h r) -> p h r", h=1),
             kpe_f[:].rearrange("p (h r) -> p h r", h=1), ["ps5", "smallc"], ["kpe_f"])
        D([(odst["ckv"][row0:row0 + 128, :], ckv_f[:])], ["ckv_f"], [], q="gpsimd")
        D([(odst["kpe"][row0:row0 + 128, :], kpe_f[:])], ["kpe_f"], [], q="gpsimd")
        G(lambda e: e.tensor_copy(out=ckv_bt[:], in_=ckv_f[:]), ["ckv_f"], ["ckv_bt"])
        if not sample:
            G(lambda e: e.tensor_copy(out=KA_ckv[:, blk, 0:128], in_=ckv_f[:]), ["ckv_f"], ["KA_ckv%d" % blk])
        group(1, 6)
        A(lambda e: e.activation(out=rr[0][:], in_=bank(6), func=AF.Tanh, scale=0.5), ["ps6"], ["rr0"])
        V(lambda e: e.scalar_tensor_tensor(out=sga_, in0=rr[0][:], scalar=1.0, in1=bank(6), op0=ALU.add, op1=ALU.mult), ["rr0", "ps6"], [sgan])
        group(2, 5)
        V(lambda e: e.tensor_copy(out=qb_bt[:, 0:512], in_=bank(5)), ["ps5"], ["qb_bt"])
        group(3, 6)
        A(lambda e: e.activation(out=kv_f[:], in_=bank(6)[:, 0:328], func=AF.Copy), ["ps6"], ["kv_f"])
        D([(odst["k"][row0:row0 + 128, :], kv_f[:, 0:128])], ["kv_f"], [], q="gpsimd")
        D([(odst["v"][row0:row0 + 128, :], kv_f[:, 128:256])], ["kv_f"], [], q="gpsimd")
        D([(odst["kidx"][row0:row0 + 128, :], kv_f[:, 256:320])], ["kv_f"], [], q="gpsimd")
        G(lambda e: e.tensor_copy(out=kb_bt[:, 0:128], in_=kv_f[:, 0:128]), ["kv_f"], ["kb_bt"])
        G(lambda e: e.tensor_copy(out=kb_bt[:, 128:192], in_=kv_f[:, 256:320]), ["kv_f"], ["kb_bt"])
        if sample:
            G(lambda e: e.tensor_copy(out=v_bt[:], in_=kv_f[:, 128:256]), ["kv_f"], ["v_bt"])
        else:
            G(lambda e: e.tensor_copy(out=KA_v[:, blk, :, 0:64], in_=kv_f[:, 128:256].rearrange("p (g d) -> p g d", g=2)),
              ["kv_f"], ["KA_v%d" % blk])
        A(lambda e: e.activation(out=wabs[:], in_=kv_f[:, 320:328], func=AF.Abs, scale=8.0 ** -0.5), ["kv_f"], ["wabs"])
        V(lambda e: e.tensor_scalar(out=wsgn[:], in0=kv_f[:, 320:328], scalar1=0.0, scalar2=2.0, op0=ALU.is_ge, op1=ALU.mult),
          ["kv_f"], ["wsgn"])
        V(lambda e: e.tensor_scalar(out=wsgn[:], in0=wsgn[:], scalar1=-1.0, scalar2=None, op0=ALU.add), ["wsgn"], ["wsgn"])
        group(4, 5)
        V(lambda e: e.tensor_copy(out=qi_bt[:, 0:512], in_=bank(5)), ["ps5"], ["qi_bt"])
        group(5, 6)
        A(lambda e: e.activation(out=rr[1][:], in_=bank(6), func=AF.Tanh, scale=0.5), ["ps6"], ["rr1"])
        V(lambda e: e.scalar_tensor_tensor(out=sgb_, in0=rr[1][:], scalar=1.0, in1=bank(6), op0=ALU.add, op1=ALU.mult), ["rr1", "ps6"], [sgbn])
        pv7 = bankb(7)
        for jc in range(2):
            T(lambda e, jc=jc: e.transpose(pv7[:, jc * 128:(jc + 1) * 128], cq_b[:, jc * 128:(jc + 1) * 128], ident_b[:]), ["cq_b", "ident_b"], ["ps7"])
        T(lambda e: e.transpose(pv7[:, 256:384], ckv_bt[:], ident_b[:]), ["ckv_bt", "ident_b"], ["ps7"])
        T(lambda e: e.transpose(pv7[:, 384:512], kpe_bt[:], ident_b[:]), ["rope_b", "kpe_f", "ident_b", "tpad"], ["ps7"])
        for g in range(2):
            T(lambda e, g=g: e.transpose(pv7[:, 512 + g * 128:640 + g * 128], kb_bt[:, g * 64:g * 64 + 128], ident_b[:]), ["kb_bt", "ident_b", "tpad"], ["ps7"])
        T(lambda e: e.transpose(pv7[:, 768:896], kb_bt[:, 128:256], ident_b[:]), ["kb_bt", "ident_b", "tpad"], ["ps7"])
        V(lambda e: e.tensor_copy(out=cqT[:].rearrange("p a b -> p (a b)"), in_=pv7[:, 0:256]), ["ps7"], ["cqT"])
        if sample:
            kdst = dict(ckv=xo[:, 0:128].bitcast(BF16)[:, 0:128], kpe=xo[0:32, 128:256].bitcast(BF16)[:, 0:128],
                        kb=[xo[0:64, 256:384].bitcast(BF16)[:, 0:128], xo[0:64, 384:512].bitcast(BF16)[:, 0:128]],
                        kidx=xo[0:64, 512:640].bitcast(BF16)[:, 0:128])
            kw = ["xo"]
        else:
            kdst = dict(ckv=KT_ckv[:, blk * 128:(blk + 1) * 128], kpe=KT_kpe[0:32, blk * 128:(blk + 1) * 128],
                        kb=[KT_kb[0:64, g, blk * 128:(blk + 1) * 128] for g in range(2)],
                        kidx=KT_kidx[0:64, blk * 128:(blk + 1) * 128])
            kw = ["KT%d" % blk]
        V(lambda e: e.tensor_copy(out=kdst["ckv"], in_=pv7[:, 256:384]), ["ps7"], kw)
        V(lambda e: e.tensor_copy(out=kdst["kpe"], in_=pv7[0:32, 384:512]), ["ps7"], kw)
        for g in range(2):
            V(lambda e, g=g: e.tensor_copy(out=kdst["kb"][g], in_=pv7[0:64, 512 + g * 128:640 + g * 128]), ["ps7"], kw)
        V(lambda e: e.tensor_copy(out=kdst["kidx"], in_=pv7[0:64, 768:896]), ["ps7"], kw)
        qlatT_, qlatTn = cur["qlatT"]; qpeT_, qpeTn, _ = cur["qpeT"]
        for h in range(8):
            pj = 5 + h // 4
            for jc in range(2):
                T(lambda e, h=h, jc=jc, pj=pj: e.matmul(bank(pj)[:, (h % 4) * 128:(h % 4 + 1) * 128], lhsT=wcomb_b[:, jc, h, :], rhs=cqT[:, jc, :],
                                                       start=(jc == 0), stop=(jc == 1)), ["wcomb_b", "cqT"], ["ps%d" % pj])
        A(lambda e: e.activation(out=qlatT_[:, 0:4, :].rearrange("p a b -> p (a b)"), in_=bank(5), func=AF.Copy), ["ps5"], [qlatTn])
        V(lambda e: e.tensor_copy(out=qlatT_[:, 4:8, :].rearrange("p a b -> p (a b)"), in_=bank(6)), ["ps6"], [qlatTn])
        for jc in range(2):
            T(lambda e, jc=jc: e.matmul(bank(7)[:, 0:256], lhsT=cqT[:, jc, :], rhs=wuqpe_b[:, jc, :], start=(jc == 0), stop=(jc == 1)),
              ["cqT", "wuqpe_b"], ["ps7"])
        rope(bank(7)[:, 0:256].rearrange("p (h r) -> p h r", h=8), cos, sin, 8, qpe_b[:, 0:256].rearrange("p (h r) -> p h r", h=8), None, ["ps7", "smallc"], ["qpe_b"])
        pv6 = bankb(5)
        for h in range(8):
            T(lambda e, h=h: e.transpose(pv6[:, h * 128:(h + 1) * 128], qpe_b[:, h * 32:h * 32 + 128], ident_b[:]), ["qpe_b", "ident_b", "tpad"], ["ps5"])
        V(lambda e: e.tensor_copy(out=qpeT_[0:32, :, :].rearrange("p a b -> p (a b)"), in_=pv6[0:32, :]), ["ps5"], [qpeTn])
        pv5 = bankb(6)
        for h in range(8):
            T(lambda e, h=h: e.transpose(pv5[:, h * 128:(h + 1) * 128], qb_bt[:, h * 64:h * 64 + 128], ident_b[:]), ["qb_bt", "ident_b", "tpad"], ["ps6"])
        A(lambda e: e.activation(out=qbT_[0:64, :, :].rearrange("p a b -> p (a b)"), in_=pv5[0:64, :], func=AF.Copy), ["ps6"], [qbTn])
        for h in range(8):
            T(lambda e, h=h: e.transpose(pv7[:, h * 128:(h + 1) * 128], qi_bt[:, h * 64:h * 64 + 128], ident_b[:]), ["qi_bt", "ident_b", "tpad"], ["ps7"])
        V(lambda e: e.tensor_copy(out=qiT[0:64, :, :].rearrange("p a b -> p (a b)"), in_=pv7[0:64, :]), ["ps7"], ["qiT"])

    state = {"r": 0, "p": 0, "sps": 0}

    def idx_epilogue(pj, h, c0, n):
        j = state["r"]; state["r"] ^= 1
        sc_, scn = cur["sc"]
        A(lambda e: e.activation(out=rr[j][:, 0:n], in_=bank(pj)[:, 0:n], func=AF.Relu, scale=wabs[:, h:h + 1]), ["ps%d" % pj, "wabs"], ["rr%d" % j])
        if h == 0:
            V(lambda e: e.tensor_scalar(out=sc_[:, c0:c0 + n], in0=rr[j][:, 0:n], scalar1=wsgn[:, 0:1], scalar2=None, op0=ALU.mult),
              ["rr%d" % j, "wsgn"], [scn])
        else:
            V(lambda e: e.scalar_tensor_tensor(out=sc_[:, c0:c0 + n], in0=rr[j][:, 0:n], scalar=wsgn[:, h:h + 1], in1=sc_[:, c0:c0 + n],
                                               op0=ALU.mult, op1=ALU.add), ["rr%d" % j, "wsgn", scn], [scn])

    def bisect(N, do):
        sc_, scn = cur["sc"]; mb_, mbn, mjd, mja = cur["mb"]
        if not do:
            V(lambda e: e.memset(bis[:, 3:4], -1e9), [], ["bis_thr"])
        else:
            M = int(round(0.55 * N / 32.0)) * 32
            Nd = N - M
            V(lambda e: e.memset(bis[:, 0:1], 0.0), [], ["bis_t"])
            g = 32.0
            for it in range(NITER):
                last = (it == NITER - 1)
                V(lambda e: e.tensor_scalar(out=mb_[:, 0:Nd], in0=sc_[:, 0:Nd], scalar1=bis[:, 0:1], scalar2=None, op0=ALU.is_ge, op1=ALU.add,
                                            accum_out=bis[:, 1:2]), [scn, "bis_t"], [mjd, "bis_c"])
                A(lambda e: e.activation(out=mb_[:, Nd:N], in_=sc_[:, Nd:N], func=AF.Sign, scale=-1.0, bias=bis[:, 0:1], accum_out=bis[:, 4:5]),
                  [scn, "bis_t"], [mja, "bis_s"])
                V(lambda e: e.scalar_tensor_tensor(out=bis[:, 5:6], in0=bis[:, 1:2], scalar=2.0, in1=bis[:, 4:5], op0=ALU.mult, op1=ALU.subtract),
                  ["bis_c", "bis_s"], ["bis_u"])
                cthr = 511.0 - M
                if not last:
                    hh = g / 2
                    V(lambda e, hh=hh: e.tensor_scalar(out=bis[:, 2:3], in0=bis[:, 5:6], scalar1=cthr, scalar2=2 * hh, op0=ALU.is_ge, op1=ALU.mult),
                      ["bis_u"], ["bis_d"])
                    V(lambda e, hh=hh: e.scalar_tensor_tensor(out=bis[:, 0:1], in0=bis[:, 2:3], scalar=-hh, in1=bis[:, 0:1], op0=ALU.add, op1=ALU.add),
                      ["bis_d", "bis_t"], ["bis_t"])
                    g = hh
                else:
                    V(lambda e, g=g: e.tensor_scalar(out=bis[:, 2:3], in0=bis[:, 5:6], scalar1=cthr, scalar2=g, op0=ALU.is_ge, op1=ALU.mult),
                      ["bis_u"], ["bis_d"])
                    V(lambda e, g=g: e.scalar_tensor_tensor(out=bis[:, 3:4], in0=bis[:, 2:3], scalar=-g, in1=bis[:, 0:1], op0=ALU.add, op1=ALU.add),
                      ["bis_d", "bis_t"], ["bis_thr"])
                yield
        V(lambda e: e.tensor_scalar(out=mb_[:, 0:N], in0=sc_[:, 0:N], scalar1=bis[:, 3:4], scalar2=NEG, op0=ALU.is_lt, op1=ALU.mult),
          [scn, "bis_thr"], [mbn, mjd, mja])
        yield

    def interleave(g1, n1, g2, n2):
        a = b = 0
        d1 = d2 = False
        while not (d1 and d2):
            if not d1 and (d2 or a * n2 <= b * n1):
                try:
                    next(g1); a += 1
                except StopIteration:
                    d1 = True
            elif not d2:
                try:
                    next(g2); b += 1
                except StopIteration:
                    d2 = True

    OM = [(2 + h // 3, (h % 3) * 129) for h in range(8)]

    def attend_block(q0, nq, kcol, nk, kblk, first, tp, mla_mask, dsa_mask, dsa_bias, Rsel, kreads, which="both"):
        units = []
        qlatT_, qlatTn = cur["qlatT"]; qpeT_, qpeTn, qpadn = cur["qpeT"]
        mb_, mbn = cur["mb"][0], cur["mb"][1]
        OD = cur["OD"]
        hgs = [(0, 4), (4, 8)] if nq == 128 else [(0, 8)]
        if which == "dsa":
            hgs = []
        for (h0, h1) in hgs:
            nh = h1 - h0
            pj = state["sps"]; state["sps"] ^= 1
            pi = state["p"]; state["p"] ^= 1
            n = nh * nq
            sv = bank(pj)[0:nk, 0:n]
            pt = Pt[pi]

            def qk(sv=sv, h0=h0, h1=h1, pj=pj, pi=pi, pt=pt, n=n, nh=nh):
                T(lambda e: e.matmul(sv, lhsT=KT_ckv[:, kcol:kcol + nk], rhs=qlatT_[:, h0:h1, q0:q0 + nq], start=True, stop=False),
                  kreads + [qlatTn], ["ps%d" % pj])
                T(lambda e: e.matmul(sv, lhsT=KT_kpe[:, kcol:kcol + nk], rhs=qpeT_[:, h0:h1, q0:q0 + nq], start=False, stop=True),
                  kreads + [qpeTn, qpadn, "kpe_pad"], ["ps%d" % pj])
                A(lambda e: e.activation(out=pt[0:nk, 0:n], in_=sv, func=AF.Exp, scale=MLA_SCALE), ["ps%d" % pj], ["P%d" % pi])
                if mla_mask:
                    G(lambda e: e.memset(pt[64:128, 0:nh * 128].rearrange("p (h q) -> p h q", h=nh)[:, :, 0:64], 0.0), ["P%d" % pi], ["P%d" % pi])

            def pv(h0=h0, h1=h1, pi=pi, pt=pt):
                for h in range(h0, h1):
                    bj, co = OM[h]
                    st_flag = first and (h % 3 == 0)
                    T(lambda e, h=h, bj=bj, co=co, st_flag=st_flag: e.matmul(
                        bank(bj)[q0:q0 + nq, co:co + 129], lhsT=pt[0:nk, (h - h0) * nq:(h - h0 + 1) * nq], rhs=KA_ckv[0:nk, kblk, :],
                        start=st_flag, stop=False, tile_position=tp, skip_group_check=True),
                      ["P%d" % pi, "KA_ones"] + kreads, ["ps%d" % bj])
            units.append((qk, pv))
        for g in (range(2) if which != "mla" else []):
            pj = state["sps"]; state["sps"] ^= 1
            pi = state["p"]; state["p"] ^= 1
            n = 4 * nq
            sv = bank(pj)[0:nk, 0:n]
            pt = Pt[pi]
            more = dsa_mask or (dsa_bias is not None)

            qbT_, qbTn, qbTan = cur["qbT"]

            def qk(sv=sv, g=g, pj=pj, pi=pi, pt=pt, n=n, more=more, qbT_=qbT_, qbTn=qbTn, qbTan=qbTan):
                T(lambda e: e.matmul(sv, lhsT=KT_kb[:, g, kcol:kcol + nk], rhs=qbT_[:, 4 * g:4 * g + 4, q0:q0 + nq], start=True, stop=not more),
                  kreads + [qbTn, qbTan, "KT_ones"], ["ps%d" % pj])
                if dsa_mask:
                    rhs_m = I4[:, 0:n] if Rsel is None else Rb[:, Rsel, :]
                    T(lambda e: e.matmul(sv, lhsT=mb_[:, kcol:kcol + nk], rhs=rhs_m, start=False, stop=(dsa_bias is None)),
                      [mbn, "I4", "Rb"], ["ps%d" % pj])
                if dsa_bias is not None:
                    T(lambda e: e.matmul(sv, lhsT=ident_b[0:nk, 0:nk], rhs=Bt[0:nk, dsa_bias, 4 * g:4 * g + 4, 0:nq], start=False, stop=True),
                      ["Bt", "ident_b"], ["ps%d" % pj])
                A(lambda e: e.activation(out=pt[0:nk, 0:n], in_=sv, func=AF.Exp), ["ps%d" % pj], ["P%d" % pi])

            def pv(g=g, pi=pi, pt=pt):
                for hl in range(4):
                    h = 4 * g + hl
                    bj, co = OD[h]
                    st_flag = first and (hl == 0)
                    T(lambda e, hl=hl, bj=bj, co=co, st_flag=st_flag: e.matmul(
                        bank(bj)[q0:q0 + nq, co:co + 65], lhsT=pt[0:nk, hl * nq:(hl + 1) * nq], rhs=KA_v[0:nk, kblk, g, :],
                        start=st_flag, stop=False, tile_position=tp, skip_group_check=True),
                      ["P%d" % pi, "KA_ones"] + kreads, ["ps%d" % bj])
            units.append((qk, pv))
        return units

    def run_units(units):
        if not units:
            return
        units[0][0]()
        for n in range(len(units)):
            if n + 1 < len(units):
                units[n + 1][0]()
            units[n][1]()
            yield

    def mla_finish():
        for j, nh in [(0, 3), (1, 3), (2, 2)]:
            V(lambda e, j=j, nh=nh: e.reciprocal(out=rc[:, 3 * j:3 * j + nh], in_=bank(2 + j)[:, 0:nh * 129].rearrange("p (h c) -> p h c", h=nh)[:, :, 128]),
              ["ps%d" % (2 + j)], ["rc"])
        for h in range(8):
            bj, co = OM[h]
            V(lambda e, h=h, bj=bj, co=co: e.tensor_scalar(out=olat_b[:, h, :], in0=bank(bj)[:, co:co + 128], scalar1=rc[:, h:h + 1], scalar2=None, op0=ALU.mult),
              ["ps%d" % bj, "rc"], ["olat_b"])

    def combine(xdst, row0, sample=False):
        xt_, xtn = cur["xt"]; olatT, olatTn = cur["olatT"]; mixT, mixTn = cur["mixT"]
        sga_, sgan = cur["sga"]; sgb_, sgbn = cur["sgb"]
        OD = cur["OD"]; odb = OD[0][0]
        pv7 = bankb(0).rearrange("p (a b) -> p a b", a=8)
        if sample:
            V(lambda e: e.scalar_tensor_tensor(out=mix[:, 512:1024], in0=bank(5), scalar=0.5, in1=sgb_, op0=ALU.mult, op1=ALU.mult), ["ps5", sgbn], ["mix"])
        else:
            for j in range(2):
                V(lambda e, j=j: e.reciprocal(out=rc[:, 8 + 4 * j:12 + 4 * j], in_=bank(odb + j)[:, 0:260].rearrange("p (h c) -> p h c", h=4)[:, :, 64]),
                  ["ps%d" % (odb + j)], ["rcd"])
            V(lambda e: e.tensor_scalar(out=rc[:, 8:16], in0=rc[:, 8:16], scalar1=0.5, scalar2=None, op0=ALU.mult), ["rcd"], ["rcd"])
            for h in range(8):
                bj, co = OD[h]
                V(lambda e, h=h, bj=bj, co=co: e.scalar_tensor_tensor(out=mix[:, 512 + h * 64:576 + h * 64], in0=bank(bj)[:, co:co + 64], scalar=rc[:, 8 + h:9 + h],
                                                                      in1=sgb_[:, h * 64:(h + 1) * 64], op0=ALU.mult, op1=ALU.mult),
                  ["ps%d" % bj, "rcd", sgbn], ["mix"])
            pv7 = bankb(0).rearrange("p (a b) -> p a b", a=8)
            for h in range(8):
                T(lambda e, h=h: e.transpose(pv7[:, h, :], olat_b[:, h, :], ident_b[:]), ["olat_b", "ident_b"], ["ps0"])
            A(lambda e: e.activation(out=olatT[:].rearrange("p a b -> p (a b)"), in_=bankb(0), func=AF.Copy), ["ps0"], [olatTn])
        for h in range(8):
            T(lambda e, h=h: e.matmul(bank(1)[:, h * 64:(h + 1) * 64], lhsT=olatT[:, h, :], rhs=wuv_b[:, h * 64:(h + 1) * 64], start=True, stop=True),
              [olatTn, "wuv_b"], ["ps1"])
        V(lambda e: e.scalar_tensor_tensor(out=mix[:, 0:512], in0=bank(1), scalar=0.5, in1=sga_, op0=ALU.mult, op1=ALU.mult), ["ps1", sgan], ["mix"])
        for ec in range(8):
            T(lambda e, ec=ec: e.transpose(pv7[:, ec, :], mix[:, ec * 128:(ec + 1) * 128], ident_b[:]), ["mix", "ident_b"], ["ps0"])
        A(lambda e: e.activation(out=mixT[:].rearrange("p a b -> p (a b)"), in_=bankb(0), func=AF.Copy), ["ps0"], [mixTn])
        for nn in range(2):
            for ec in range(8):
                T(lambda e, nn=nn, ec=ec: e.matmul(bank(nn), lhsT=mixT[:, ec, :], rhs=wout_b[:, ec, nn * 512:(nn + 1) * 512], start=(ec == 0), stop=(ec == 7)),
                  [mixTn, "wout_b"], ["ps%d" % nn])
        V(lambda e: e.tensor_tensor(out=xo[:], in0=PSX[0][:], in1=xt_, op=ALU.add), ["ps0", "ps1", xtn], ["xo"])
        A(lambda e: e.activation(out=mix[:], in_=xo[:], func=AF.Square, accum_out=stt[:, 6:7]), ["xo"], ["mix", "stt6"])
        rstd_from(stt[:, 6:7], stt[:, 7:8], 1024.0, ["stt6"], ["stt7"])
        V(lambda e: e.scalar_tensor_tensor(out=xo[:], in0=xo[:], scalar=stt[:, 7:8], in1=fng_bc[:], op0=ALU.mult, op1=ALU.mult),
          ["xo", "stt7", "smallc"], ["xo"])
        D([(xdst[row0:row0 + 128, :], xo[:])], ["xo"], [], q="gpsimd")

    if SAMPLE:
        cur.update(CFG_S)
        proj_tile(xs, cos_s[:], sin_s[:], o_s, 0, None, True)
        nckv = xo[:, 0:128].bitcast(BF16)[:, 0:128]; nkpe = xo[0:32, 128:256].bitcast(BF16)[:, 0:128]
        nkb = [xo[0:64, 256:384].bitcast(BF16)[:, 0:128], xo[0:64, 384:512].bitcast(BF16)[:, 0:128]]
        nkidx = xo[0:64, 512:640].bitcast(BF16)[:, 0:128]
        kifs = [(kif[:], "kif"), (cache_f[:, 0, 0:256].rearrange("p (a b) -> p a b", a=4), "cache_f0")]
        kibs = [(kib[:], "kib"), (cache_b[:, 0, 0:320].rearrange("p (a b) -> p a b", a=5), "cache_b0")]
        kics = [(kic[:], "kic"), (KT_ckv[:, 0:2048].rearrange("p (a b) -> p a b", a=4), "kic2")]
        V(lambda e: e.memset(KT_ckv[64:128, 0:2048], 0.0), [], ["kic2"])

        def load_a(c, b):
            sl = (c * 4 + b) % 2
            kf, kfn = kifs[sl]; kb_, kbn = kibs[sl]
            D([(kf, c_kidx[b, c * 512:(c + 1) * 512, :].rearrange("(k p) d -> p k d", p=128))], [], [kfn])
            G(lambda e: e.tensor_copy(out=kb_[:, 0:4, :], in_=kf), [kfn], [kbn])

        def load_b(c, b):
            sl = (c * 4 + b) % 2
            kb_, kbn = kibs[sl]; kc_, kcn = kics[c % 2]
            pvx = bankb(6 + sl); pn = "ps%d" % (6 + sl)
            for kk in range(4):
                T(lambda e, kk=kk: e.transpose(pvx[:, kk * 128:(kk + 1) * 128], kb_[:, kk:kk + 2, :].rearrange("p a b -> p (a b)"), ident_b[:]),
                  [kbn, "ident_b", "tpad"], [pn])
            V(lambda e: e.tensor_copy(out=kc_[0:64, b, :], in_=pvx[0:64, 0:512]), [pn], [kcn])

        def compute(c, heads):
            n = 512 if c < 8 else 32
            kc_, kcn = kics[c % 2]
            for h in heads:
                pj = h % 2
                for b in range(4):
                    T(lambda e, h=h, b=b, pj=pj, n=n: e.matmul(bank(pj)[32 * b:32 * b + 32, 0:n], lhsT=qiT[:, h, 32 * b:32 * b + 32], rhs=kc_[:, b, 0:n],
                                                             start=True, stop=True, tile_position=(0, 32 * b)), ["qiT", kcn, "idx_pad"], ["ps%d" % pj])
                idx_epilogue(pj, h, c * 512, n)

        for b in range(4):
            load_a(0, b)
            load_b(0, b)
        for c in range(9):
            nxt = c + 1
            if nxt < 8:
                load_a(nxt, 0); load_a(nxt, 1)
                compute(c, [0, 1])
                load_b(nxt, 0); load_a(nxt, 2)
                compute(c, [2, 3])
                load_b(nxt, 1); load_a(nxt, 3)
                compute(c, [4, 5])
                load_b(nxt, 2)
                compute(c, [6, 7])
                load_b(nxt, 3)
            else:
                if nxt == 8:
                    kc_, kcn = kics[0]
                    for b in range(4):
                        V(lambda e, b=b, kc_=kc_: e.tensor_copy(out=kc_[0:64, b, 0:32], in_=nkidx[:, 32 * b:32 * b + 32]), ["xo"], [kcn])
                compute(c, list(range(8)))
        V(lambda e: e.memset(bis[:, 6:7], 0.0), [], ["kic2", "cache_f0", "cache_b0"] + ["KT%d" % k for k in range(16)])
        pend = [None]

        def push(u):
            u[0]()
            if pend[0] is not None:
                pend[0][1]()
            pend[0] = u

        def load_block(b, blk):
            kk = blk % 2
            k0 = blk * 128
            cf = "cache_f%d" % kk; cb = "cache_b%d" % kk; pb = "ps%d" % (6 + kk)
            D([(cache_f[:, kk, 0:128], c_ckv[b, k0:k0 + 128, :]), (cache_f[:, kk, 128:160], c_kpe[b, k0:k0 + 128, :]),
               (cache_f[:, kk, 160:288], c_k[b, k0:k0 + 128, :]), (cache_f[:, kk, 288:416], c_v[b, k0:k0 + 128, :])], [], [cf])
            G(lambda e: e.tensor_copy(out=cache_b[:, kk, 0:288], in_=cache_f[:, kk, 0:288]), [cf], [cb])
            A(lambda e: e.activation(out=KA_ckv[:, blk, 0:128], in_=cache_f[:, kk, 0:128], func=AF.Copy), [cf], ["KA_ckv%d" % blk])
            V(lambda e: e.tensor_copy(out=KA_v[:, blk, :, 0:64], in_=cache_f[:, kk, 288:416].rearrange("p (g d) -> p g d", g=2)),
              [cf], ["KA_v%d" % blk])
            pv7 = bankb(6 + kk)
            o = 0
            T(lambda e: e.transpose(pv7[:, o:o + 128], cache_b[:, kk, 0:128], ident_b[:]), [cb, "ident_b"], [pb])
            T(lambda e: e.transpose(pv7[:, o + 128:o + 256], cache_b[:, kk, 128:256], ident_b[:]), [cb, "ident_b", "tpad"], [pb])
            for g in range(2):
                T(lambda e, g=g: e.transpose(pv7[:, o + 256 + g * 128:o + 384 + g * 128], cache_b[:, kk, 160 + g * 64:288 + g * 64], ident_b[:]),
                  [cb, "ident_b", "tpad"], [pb])
            cs = slice(blk * 128, (blk + 1) * 128)
            V(lambda e: e.tensor_copy(out=KT_ckv[:, cs], in_=pv7[:, o:o + 128]), [pb], ["KT%d" % blk])
            V(lambda e: e.tensor_copy(out=KT_kpe[0:32, cs], in_=pv7[0:32, o + 128:o + 256]), [pb], ["KT%d" % blk])
            V(lambda e: e.tensor_copy(out=KT_kb[0:64, :, cs], in_=pv7[0:64, o + 256:o + 512].rearrange("p (g k) -> p g k", g=2)),
              [pb], ["KT%d" % blk])

        def new_keys(b):
            cs = slice(4096, 4128)
            V(lambda e: e.tensor_copy(out=KT_ckv[:, cs], in_=nckv[:, 32 * b:32 * b + 32]), ["xo"], ["KT32"])
            V(lambda e: e.tensor_copy(out=KT_kpe[0:32, cs], in_=nkpe[:, 32 * b:32 * b + 32]), ["xo"], ["KT32"])
            for g in range(2):
                V(lambda e, g=g: e.tensor_copy(out=KT_kb[0:64, g, cs], in_=nkb[g][:, 32 * b:32 * b + 32]), ["xo"], ["KT32"])
            D([(KA_ckv[0:32, 32, 0:128], ckv_bt[32 * b:32 * b + 32, :])], ["ckv_bt"], ["KA_ckv32"])
            D([(KA_v[0:32, 32, :, 0:64], v_bt[32 * b:32 * b + 32, :].rearrange("p (g d) -> p g d", g=2))], ["v_bt"], ["KA_v32"])

        on_b = olat_b[:, 0:2, :]
        obn = olat_b[:, 2, :].rearrange("p (g d) -> p g d", g=2)
        olatT_S = CFG_S["olatT"][0]

        def att(b, blk, which="both"):
            nk = 128 if blk < 32 else 32
            bias = 1 if blk == 31 else (0 if blk == 32 else None)
            kcol = blk * 128
            q0 = 32 * b
            ob = 2 + (b % 2)
            kreads = ["KT%d" % blk, "KA_ckv%d" % blk, "KA_v%d" % blk]
            first = (blk == 0)
            if which != "dsa":
                pj = state["sps"]; state["sps"] ^= 1
                pi = state["p"]; state["p"] ^= 1
                sv = bank(pj)[0:nk, 0:256]
                pt = Pt[pi]

                def qk(sv=sv, pj=pj, pi=pi, pt=pt):
                    T(lambda e: e.matmul(sv, lhsT=KT_ckv[:, kcol:kcol + nk], rhs=qlatT[:, :, q0:q0 + 32], start=True, stop=False),
                      kreads + ["qlatT"], ["ps%d" % pj])
                    T(lambda e: e.matmul(sv, lhsT=KT_kpe[:, kcol:kcol + nk], rhs=qpeT[:, :, q0:q0 + 32], start=False, stop=True),
                      kreads + ["qpeT", "kpe_pad"], ["ps%d" % pj])
                    A(lambda e: e.activation(out=pt[0:nk, 0:256], in_=sv, func=AF.Exp, scale=MLA_SCALE), ["ps%d" % pj], ["P%d" % pi])

                def pv(pi=pi, pt=pt):
                    for hg in range(2):
                        T(lambda e, hg=hg: e.matmul(bank(ob)[:, hg * 129:(hg + 1) * 129], lhsT=pt[0:nk, hg * 128:(hg + 1) * 128], rhs=KA_ckv[0:nk, blk, :],
                                                    start=(first and hg == 0), stop=False, skip_group_check=True),
                          ["P%d" % pi, "KA_ones"] + kreads, ["ps%d" % ob])
                push((qk, pv))
            for g in (range(2) if which != "mla" else []):
                pj = state["sps"]; state["sps"] ^= 1
                pi = state["p"]; state["p"] ^= 1
                sv = bank(pj)[0:nk, 0:128]
                pt = Pt[pi]

                def qk(sv=sv, g=g, pj=pj, pi=pi, pt=pt):
                    T(lambda e: e.matmul(sv, lhsT=KT_kb[:, g, kcol:kcol + nk], rhs=qbT[:, 4 * g:4 * g + 4, q0:q0 + 32], start=True, stop=False),
                      kreads + ["qbT", "qbT_aug", "KT_ones"], ["ps%d" % pj])
                    T(lambda e: e.matmul(sv, lhsT=mb[:, kcol:kcol + nk], rhs=Rb[:, b, :], start=False, stop=(bias is None)),
                      ["mb", "Rb"], ["ps%d" % pj])
                    if bias is not None:
                        T(lambda e: e.matmul(sv, lhsT=ident_b[0:nk, 0:nk], rhs=Bt[0:nk, bias, 4 * g:4 * g + 4, 0:32], start=False, stop=True),
                          ["Bt", "ident_b"], ["ps%d" % pj])
                    A(lambda e: e.activation(out=pt[0:nk, 0:128], in_=sv, func=AF.Exp), ["ps%d" % pj], ["P%d" % pi])

                def pv(g=g, pi=pi, pt=pt):
                    T(lambda e: e.matmul(bank(ob)[:, 258 + g * 65:258 + (g + 1) * 65], lhsT=pt[0:nk, 0:128], rhs=KA_v[0:nk, blk, g, :],
                                        start=False, stop=False, skip_group_check=True),
                      ["P%d" % pi, "KA_ones"] + kreads, ["ps%d" % ob])
                push((qk, pv))

        def finalize(b):
            ob = 2 + (b % 2)
            obn_ = "ps%d" % ob
            V(lambda e: e.reciprocal(out=rc[:, 0:2], in_=bank(ob)[:, 0:258].rearrange("p (h c) -> p h c", h=2)[:, :, 128]), [obn_], ["rc"])
            V(lambda e: e.reciprocal(out=rc[:, 2:4], in_=bank(ob)[:, 258:388].rearrange("p (h c) -> p h c", h=2)[:, :, 64]), [obn_], ["rc"])
            for hg in range(2):
                V(lambda e, hg=hg: e.tensor_scalar(out=on_b[:, hg, :], in0=bank(ob)[:, hg * 129:hg * 129 + 128], scalar1=rc[:, hg:hg + 1], scalar2=None, op0=ALU.mult),
                  [obn_, "rc"], ["olat_b"])
            for g in range(2):
                V(lambda e, g=g: e.tensor_scalar(out=obn[:, g, :], in0=bank(ob)[:, 258 + g * 65:258 + g * 65 + 64], scalar1=rc[:, 2 + g:3 + g], scalar2=None, op0=ALU.mult),
                  [obn_, "rc"], ["olat_b"])
            pv4 = bankb(4)
            for hg in range(2):
                T(lambda e, hg=hg: e.transpose(pv4[:, hg * 128:(hg + 1) * 128], on_b[:, hg, :], ident_b[:]), ["olat_b", "ident_b"], ["ps4"])
            for hg in range(2):
                V(lambda e, hg=hg: e.tensor_copy(out=olatT_S[:, 4 * hg:4 * hg + 4, 32 * b:32 * b + 32],
                                                 in_=pv4[:, hg * 128:(hg + 1) * 128].rearrange("p (h q) -> p h q", h=4)), ["ps4"], ["hT"])
            dst = bank(5)[32 * b:32 * b + 32, :].rearrange("p (g h d) -> p g h d", g=2, h=4)
            for hl in range(4):
                for g in range(2):
                    T(lambda e, hl=hl, g=g: e.matmul(dst[:, g, hl, :], lhsT=ident_b[:, hl * 32:(hl + 1) * 32], rhs=obn[:, g, :], start=True, stop=True,
                                                     tile_position=(0, 32 * b), skip_group_check=True), ["olat_b", "ident_b"], ["ps5"])

        def flush():
            if pend[0] is not None:
                pend[0][1]()
                pend[0] = None

        LAG = 3

        def do_load(b, blk):
            if blk < 32:
                load_block(b, blk)
            else:
                new_keys(b)

        def b0_mla():
            for blk in range(33):
                do_load(0, blk)
                if blk >= LAG:
                    att(0, blk - LAG, "mla")
            for blk in range(33 - LAG, 33):
                att(0, blk, "mla")
            flush()

        def sample_bisect():
            for _ in bisect(4128, True):
                pass
        emit_merged(capture(sample_bisect), capture(b0_mla))
        loads = [(b, blk) for b in range(1, 4) for blk in range(33)]
        order = []
        for blk in range(33):
            order.append(("dsa0", 0, blk))
            if blk >= 1:
                order.append(("load",) + loads[blk - 1])
            if blk >= 1 + LAG:
                order.append(("att",) + loads[blk - 1 - LAG])
        for n in range(32, len(loads)):
            order.append(("load",) + loads[n])
            order.append(("att",) + loads[n - LAG])
        for n in range(len(loads) - LAG, len(loads)):
            order.append(("att",) + loads[n])
        for it in order:
            if it[0] == "load":
                do_load(it[1], it[2])
            elif it[0] == "dsa0":
                att(0, it[2], "dsa")
                if it[2] == 32:
                    flush()
                    finalize(0)
            else:
                att(it[1], it[2])
                if it[2] == 32:
                    flush()
                    finalize(it[1])
        combine(y_s, 0, sample=True)

    V(lambda e: e.memset(bis[:, 7:8], 0.0), [],
      ["sc", "mb", "mbj_d", "mbj_a", "cache_f0", "cache_f1", "cache_b0", "cache_b1"] + ALIAS_BUFS +
      ["KT%d" % k for k in range(16, 33)] + ["KA_ckv%d" % k for k in range(16, 33)] + ["KA_v%d" % k for k in range(16, 33)])
    V(lambda e: e.tensor_copy(out=qbT1[64:65, :, :], in_=cfs[64:65, :].unsqueeze(2).to_broadcast([1, 8, 128])), ["smallc"], ["qbT_aug1"])
    V(lambda e: e.tensor_copy(out=qbT2[64:65, :, :], in_=cfs[64:65, :].unsqueeze(2).to_broadcast([1, 8, 128])), ["smallc"], ["qbT_aug2"])
    V(lambda e: e.memset(qpeT1[32:64, :, :], 0.0), [], ["kpe_pad1"])
    V(lambda e: e.memset(qpeT1[64:128, :, :], 0.0), [], ["kpe_pad1"])

    def stage_a(i):
        cur.update(cfg_p(i))
        proj_tile(xp[i * 128:(i + 1) * 128, :], cos_p[:, i, :], sin_p[:, i, :], o_p, i * 128, i, False)
        sc_, scn = cur["sc"]
        N = 128 * (i + 1)
        for c in range((N + 511) // 512):
            n = min(512, N - c * 512)
            for h in range(8):
                pj = 5 + h % 2
                T(lambda e, h=h, pj=pj, c=c, n=n: e.matmul(bank(pj)[:, 0:n], lhsT=qiT[:, h, :], rhs=KT_kidx[:, c * 512:c * 512 + n], start=True, stop=True),
                  ["qiT", "idx_pad"] + ["KT%d" % bb for bb in range(c * 4, min(c * 4 + 4, i + 1))], ["ps%d" % pj])
                idx_epilogue(pj, h, c * 512, n)
        V(lambda e, i=i: e.memset(sc_[0:64, i * 128 + 64:i * 128 + 128], -1e30), [scn], [scn])

    def do_bisect(i):
        cur.update(cfg_p(i))
        for _ in bisect(128 * (i + 1), i >= 2):
            pass

    def do_mla(i):
        cur.update(cfg_p(i))
        mu = []
        for kb in range(i + 1):
            mu += attend_block(0, 128, kb * 128, 128, kb, kb == 0, None, kb == i, False, None, None,
                               ["KT%d" % kb, "KA_ckv%d" % kb, "KA_v%d" % kb], which="mla")
        for _ in run_units(mu):
            pass
        mla_finish()

    def do_dsa_combine(i):
        cur.update(cfg_p(i))
        du = []
        for kb in range(i + 1):
            diag = (kb == i)
            need_mask = (i >= 2) or diag
            bias = 0 if diag else (1 if kb == i - 1 else None)
            du += attend_block(0, 128, kb * 128, 128, kb, kb == 0, None, diag, need_mask, bias, None,
                               ["KT%d" % kb, "KA_ckv%d" % kb, "KA_v%d" % kb], which="dsa")
        for _ in run_units(du):
            pass
        combine(y_p, i * 128)

    def emit_merged_k(lists):
        lists = [l for l in lists if l]
        pos = [0] * len(lists)
        while True:
            best = None
            for k, l in enumerate(lists):
                if pos[k] < len(l):
                    f = pos[k] / float(len(l))
                    if best is None or f < best[0]:
                        best = (f, k)
            if best is None:
                break
            k = best[1]
            it = lists[k][pos[k]]; pos[k] += 1
            if it[0] == "op":
                S.op(it[1], it[2], it[3], it[4])
            else:
                S.dma(it[4], it[1], it[2], it[3])

    if NT:
        stage_a(0)
        do_bisect(0)
        do_mla(0)
        if NT > 1:
            stage_a(1)
    for j in range(NT):
        def main_chain(j=j):
            do_dsa_combine(j)
            if j + 1 < NT:
                do_mla(j + 1)
        l_main = capture(main_chain)
        l_bis = capture(lambda: do_bisect(j + 1)) if j + 1 < NT else []
        l_proj = capture(lambda: stage_a(j + 2)) if j + 2 < NT else []
        emit_merged_k([l_main, l_bis, l_proj])

    outs = [B(x) for x in ["xo", "ckv_f", "kpe_f", "kv_f"]]
    S.final_waits("sync", outs)
    print('total ops', S.nop, {e: S.count[e] for e in ENGS})
    S.build(st)
    st.close()
    return nc


def _t5_bucket(rel):
    rel = np.asarray(rel, np.int64)
    nb, me = 16, 8
    ret = (rel > 0).astype(np.int64) * nb
    n = np.abs(rel)
    nf = np.maximum(n, 1).astype(np.float32)
    large = me + (np.log(nf / np.float32(me)) / np.float32(math.log(128 / 8)) * np.float32(nb - me)).astype(np.int32)
    large = np.minimum(large, nb - 1)
    return ret + np.where(n < me, n, large)


def _rope_tables(pos):
    half = 16
    inv = (np.float32(10000.0) ** (-np.arange(half, dtype=np.float32) / np.float32(half))).astype(np.float32)
    ang = pos.astype(np.float32)[:, None] * inv[None, :]
    return np.cos(ang).astype(np.float32), np.sin(ang).astype(np.float32)


_CACHE = {}


def make_in_maps(x_prompt, x_sample, cache_mla_ckv, cache_mla_kpe, cache_dsa_k, cache_dsa_v, cache_dsa_kidx,
                 norm_g, w_in, mla_q_norm_g, mla_kv_norm_g, mla_w_uq, mla_w_uk, mla_w_uv, rel_bias, w_out, final_norm_g):
    f = lambda a: np.ascontiguousarray(np.asarray(a, dtype=np.float32))
    x_prompt, x_sample = f(x_prompt), f(x_sample)
    rel_bias = f(rel_bias)
    cos_p, sin_p = _rope_tables(np.arange(2048))
    cos_s, sin_s = _rope_tables(4096 + (np.arange(128) % 32))
    kl = np.arange(128)[:, None]; ql = np.arange(128)[None, :]
    tb = np.stack([rel_bias[_t5_bucket(kl - ql)], rel_bias[_t5_bucket(kl - 128 - ql)]])
    tb = f(tb.transpose(0, 1, 3, 2).reshape(2, 128, 1024))
    shared = {
        "normg": f(f(norm_g).reshape(8, 128).T), "w_in": f(w_in[0]), "qng": f(f(mla_q_norm_g).reshape(2, 128).T),
        "kvng_bc": f(np.broadcast_to(f(mla_kv_norm_g).reshape(1, 128), (128, 128))),
        "w_uq": f(f(mla_w_uq)[0].reshape(256, 768)), "w_uk": f(f(mla_w_uk)[0].reshape(128, 512)),
        "w_uv": f(f(mla_w_uv)[0].reshape(128, 512)), "w_out": f(f(w_out)[0]),
        "fng_bc": f(np.broadcast_to(f(final_norm_g).reshape(1, 1024), (128, 1024))),
        "cfar_bc": f(np.broadcast_to(rel_bias[15:16, :], (128, 8))), "tbias": tb,
        "cos_p": cos_p, "sin_p": sin_p, "cos_s": cos_s, "sin_s": sin_s,
    }
    in_maps = []
    for c in range(8):
        m = dict(shared)
        m["xp"] = x_prompt[c]
        m["xs"] = x_sample[4 * c:4 * c + 4].reshape(128, 1024)
        m["c_ckv"] = f(cache_mla_ckv[0, 4 * c:4 * c + 4]); m["c_kpe"] = f(cache_mla_kpe[0, 4 * c:4 * c + 4])
        m["c_k"] = f(cache_dsa_k[0, 4 * c:4 * c + 4]).reshape(4, 4096, 128)
        m["c_v"] = f(cache_dsa_v[0, 4 * c:4 * c + 4]).reshape(4, 4096, 128)
        m["c_kidx"] = f(cache_dsa_kidx[0, 4 * c:4 * c + 4])
        in_maps.append(m)
    return in_maps


def kernel(**inputs):
    if "nc" not in _CACHE:
        _CACHE["nc"] = build_program()
    nc = _CACHE["nc"]
    in_maps = make_in_maps(**inputs)
    res = run_bass_kernel_spmd(nc, in_maps, core_ids=list(range(8)))
    R = res.results
    cat = lambda k: np.stack([R[c][k] for c in range(8)])
    y_p = cat("y_p")
    y_s = cat("y_s").reshape(32, 32, 1024)
    outs = [y_p, y_s]
    for k, shp in [("o_ckv_p", (1, 8, 2048, 128)), ("o_kpe_p", (1, 8, 2048, 32)), ("o_k_p", (1, 8, 2048, 2, 64)),
                   ("o_v_p", (1, 8, 2048, 2, 64)), ("o_kidx_p", (1, 8, 2048, 64))]:
        outs.append(cat(k).reshape(shp))
    for k, shp in [("o_ckv_s", (1, 32, 32, 128)), ("o_kpe_s", (1, 32, 32, 32)), ("o_k_s", (1, 32, 32, 2, 64)),
                   ("o_v_s", (1, 32, 32, 2, 64)), ("o_kidx_s", (1, 32, 32, 64))]:
        outs.append(cat(k).reshape(shp))
    return tuple(np.ascontiguousarray(o.astype(np.float32)) for o in outs)
```

```python
import math
from contextlib import ExitStack

import numpy as np
import concourse.bass as bass
import concourse.mybir as mybir
from concourse.bass_utils import run_bass_kernel_spmd

F32 = mybir.dt.float32
BF16 = mybir.dt.bfloat16
ALU = mybir.AluOpType
AF = mybir.ActivationFunctionType

ENGS = ("tensor", "vector", "scalar", "gpsimd", "sync")
EPOCH = 12000
import os
LIMIT = int(os.environ.get('KLIMIT', '100000000'))

CQ, CKV, KPE, GA, QB, KB, VB, KIDX, WIDX, QIDX, GB = 0, 256, 384, 416, 928, 1440, 1568, 1696, 1760, 1768, 2280
GROUPS = [(0, 416), (416, 928), (928, 1440), (1440, 1768), (1768, 2280), (2280, 2792)]
MLA_SCALE = 96.0 ** -0.5
NEG = -30000.0
NITER = 24


class Buf:
    __slots__ = ("name", "w", "r", "const")

    def __init__(self, name):
        self.name = name
        self.w = None
        self.r = {}
        self.const = False


class Sched:
    def __init__(self, nc):
        self.nc = nc
        self.ops = {e: [] for e in ENGS}
        self.count = {e: 0 for e in ENGS}
        self.seen = {e: {} for e in ENGS}
        self.snap = {e: [] for e in ENGS}
        self.semkeys = []
        self.dmacount = {}

    def _deps(self, reads, writes, eng=None):
        deps = {}

        def add(ev, raw):
            if ev is None:
                return
            k, v = ev
            if not raw and eng is not None and k[0] == eng:
                return
            if deps.get(k, 0) < v:
                deps[k] = v
        for b in reads:
            add(b.w, True)
        for b in writes:
            add(b.w, False)
            for k, v in b.r.items():
                add((k, v), False)
        return deps

    def _waits(self, eng, deps):
        waits = []
        for k, v in deps.items():
            if eng == "tensor" and k[0] == "tensor":
                continue
            if self.seen[eng].get(k, 0) >= v:
                continue
            self.seen[eng][k] = v
            waits.append((k, v))
            if k[0] in ENGS:
                snap = self.snap[k[0]][k[1] * EPOCH + v - 1]
                se = self.seen[eng]
                for kk, vv in snap.items():
                    if se.get(kk, 0) < vv:
                        se[kk] = vv
        return waits

    def _mark(self, ev, reads, writes):
        k, v = ev
        for b in reads:
            if not b.const and b.r.get(k, 0) < v:
                b.r[k] = v
        for b in writes:
            b.w = ev
            b.r = {}

    def op(self, eng, fn, reads=(), writes=()):
        self.nop = getattr(self, "nop", 0) + 1
        if os.environ.get('KTRACE'):
            import sys as _s
            print("OP", self.nop, eng, _s._getframe(2).f_lineno, _s._getframe(3).f_lineno)
        if self.nop > LIMIT:
            return
        deps = self._deps(reads, writes, eng)
        waits = self._waits(eng, deps)
        self.count[eng] += 1
        n = self.count[eng]
        key = (eng, (n - 1) // EPOCH)
        if key not in self.semkeys:
            self.semkeys.append(key)
        val = (n - 1) % EPOCH + 1
        self.ops[eng].append((waits, fn, key, 1))
        self.snap[eng].append(dict(self.seen[eng]))
        self._mark((key, val), reads, writes)

    def dma(self, q, pairs, reads=(), writes=()):
        self.nop = getattr(self, "nop", 0) + 1
        if self.nop > LIMIT:
            return
        owner = (writes[0] if writes else reads[0])
        key = ("dma", owner.name)
        if key not in self.dmacount:
            self.dmacount[key] = 0
            self.semkeys.append(key)
        deps = self._deps(reads, writes)
        waits = self._waits(q, deps)
        first = True
        for (o, i) in pairs:
            self.dmacount[key] += 16
            self.ops[q].append((waits if first else [],
                                (lambda e, o=o, i=i: e.dma_start(out=o, in_=i)), key, 16))
            first = False
        self._mark((key, self.dmacount[key]), reads, writes)

    def final_waits(self, eng, bufs):
        deps = self._deps((), bufs)
        waits = self._waits(eng, deps)
        self.ops[eng].append((waits, None, None, 0))

    def build(self, st):
        nc = self.nc
        sems = {}
        for k in self.semkeys:
            sems[k] = st.enter_context(nc.semaphore("s_" + "_".join(str(x) for x in k)))
        block = st.enter_context(nc.Block())

        def mk(engname):
            def body(e):
                for (waits, fn, key, inc) in self.ops[engname]:
                    for (k, v) in waits:
                        e.wait_ge(sems[k], v)
                    if fn is not None:
                        fn(e).then_inc(sems[key], inc)
            return body

        for engname in ENGS:
            if self.ops[engname]:
                getattr(block, engname)(mk(engname))


def build_program(NT=16, SAMPLE=True):
    nc = bass.Bass("TRN2", target_bir_lowering=False)
    st = ExitStack()

    def din(name, shape):
        return nc.dram_tensor(name, shape, F32, kind="ExternalInput").ap()

    def dout(name, shape):
        return nc.dram_tensor(name, shape, F32, kind="ExternalOutput").ap()

    xp = din("xp", [2048, 1024]); xs = din("xs", [128, 1024])
    c_ckv = din("c_ckv", [4, 4096, 128]); c_kpe = din("c_kpe", [4, 4096, 32])
    c_k = din("c_k", [4, 4096, 128]); c_v = din("c_v", [4, 4096, 128]); c_kidx = din("c_kidx", [4, 4096, 64])
    d_normg = din("normg", [128, 8]); d_win = din("w_in", [1024, 2792]); d_qng = din("qng", [128, 2])
    d_kvng = din("kvng_bc", [128, 128]); d_wuq = din("w_uq", [256, 768]); d_wuk = din("w_uk", [128, 512])
    d_wuv = din("w_uv", [128, 512]); d_wout = din("w_out", [1024, 1024]); d_fng = din("fng_bc", [128, 1024])
    d_cfar = din("cfar_bc", [128, 8]); d_tb = din("tbias", [2, 128, 1024])
    d_cosp = din("cos_p", [2048, 16]); d_sinp = din("sin_p", [2048, 16])
    d_coss = din("cos_s", [128, 16]); d_sins = din("sin_s", [128, 16])
    y_p = dout("y_p", [2048, 1024]); y_s = dout("y_s", [128, 1024])
    o_p = {"ckv": dout("o_ckv_p", [2048, 128]), "kpe": dout("o_kpe_p", [2048, 32]), "k": dout("o_k_p", [2048, 128]),
           "v": dout("o_v_p", [2048, 128]), "kidx": dout("o_kidx_p", [2048, 64])}
    o_s = {"ckv": dout("o_ckv_s", [128, 128]), "kpe": dout("o_kpe_s", [128, 32]), "k": dout("o_k_s", [128, 128]),
           "v": dout("o_v_s", [128, 128]), "kidx": dout("o_kidx_s", [128, 64])}

    S = Sched(nc)
    bufs = {}

    def B(name):
        if name not in bufs:
            bufs[name] = Buf(name)
        return bufs[name]

    def sb(name, shape, dt):
        return st.enter_context(nc.sbuf_tensor("sb_" + name, shape, dt))

    w_in_b = sb("w_in_b", [128, 8, 2792], BF16)
    wcomb_b = sb("wcomb_b", [128, 2, 8, 128], BF16)
    wuqpe_b = sb("wuqpe_b", [128, 2, 256], BF16)
    wuv_b = sb("wuv_b", [128, 512], BF16)
    wout_b = sb("wout_b", [128, 8, 1024], BF16)
    ident_b = sb("ident_b", [128, 128], BF16)
    ident_f = sb("ident_f", [128, 128], F32)
    I4 = sb("I4", [128, 512], BF16)
    Rb = sb("Rb", [128, 4, 128], BF16)
    normg_t = sb("normg_t", [128, 8], F32)
    qng_t = sb("qng_t", [128, 2], F32)
    kvng_bc = sb("kvng_bc", [128, 128], F32)
    fng_bc = sb("fng_bc", [128, 1024], F32)
    cfs = sb("cfs", [128, 8], F32)
    cos_p = sb("cos_pt", [128, 16, 16], F32); sin_p = sb("sin_pt", [128, 16, 16], F32)
    cos_s = sb("cos_st", [128, 16], F32); sin_s = sb("sin_st", [128, 16], F32)
    Bt = sb("Bt", [128, 2, 8, 128], BF16)
    cst = sb("cst", [128, 16], F32)
    KT_ckv = sb("KT_ckv", [128, 4224], BF16)
    KT_kpe = sb("KT_kpe", [128, 4224], BF16)
    KT_kb = sb("KT_kb", [65, 2, 4224], BF16)
    KT_kidx = sb("KT_kidx", [128, 2048], BF16)
    KA_ckv = sb("KA_ckv", [128, 33, 129], BF16)
    KA_v = sb("KA_v", [128, 33, 2, 65], BF16)
    xt = sb("xt", [128, 1024], F32)
    xo = sb("xo", [128, 1024], F32)
    mix = sb("mix", [128, 1024], BF16)
    hT = sb("hT", [128, 8, 128], BF16)
    cq_b = sb("cq_b", [128, 256], BF16)
    cqT = sb("cqT", [128, 2, 128], BF16)
    qlatT = sb("qlatT", [128, 8, 128], BF16)
    rt = sb("rt", [128, 6, 8, 16], F32)
    qpe_b = sb("qpe_b", [128, 352], BF16)
    qpeT = sb("qpeT", [128, 8, 128], BF16)
    ckv_f = sb("ckv_f", [128, 128], F32)
    ckv_bt = sb("ckv_bt", [128, 128], BF16)
    kpe_f = sb("kpe_f", [128, 32], F32)
    kpe_bt = sb("kpe_bt", [128, 128], BF16)
    kv_f = sb("kv_f", [128, 328], F32)
    kb_bt = sb("kb_bt", [128, 256], BF16)
    v_bt = sb("v_bt", [128, 128], BF16)
    qbT = sb("qbT", [65, 8, 128], BF16)
    qb_bt = sb("qb_bt", [128, 576], BF16)
    qi_bt = sb("qi_bt", [128, 576], BF16)
    qiT = sb("qiT", [128, 8, 128], BF16)
    sga = sb("sga", [128, 512], BF16)
    sgb = sb("sgb", [128, 512], BF16)
    wabs = sb("wabs", [128, 8], F32)
    wsgn = sb("wsgn", [128, 8], F32)
    stt = sb("stt", [128, 16], F32)
    sc = sb("sc", [128, 4128], F32)
    mb = sb("mb", [128, 4128], BF16)
    rr = [sb("rr0", [128, 512], F32), sb("rr1", [128, 512], F32)]
    bis = sb("bis", [128, 8], F32)
    Pt = [sb("P0", [128, 512], BF16), sb("P1", [128, 512], BF16)]
    rc = sb("rc", [128, 16], F32)
    olat_b = sb("olat_b", [128, 8, 128], BF16)
    cache_f = sb("cache_f", [128, 2, 416], F32)
    cache_b = sb("cache_b", [128, 2, 352], BF16)
    kic = sb("kic", [128, 4, 512], BF16)
    kif = sb("kif", [128, 4, 64], F32)
    kib = sb("kib", [128, 5, 64], BF16)
    print('SBUF bytes remaining', nc.sbuf_bytes_remaining)
    PSX = [st.enter_context(nc.psum_tensor("psx%d" % j, [128, 1024], F32)) for j in range(4)]

    sc0 = sc[:, 0:2048]; sc1 = sc[:, 2048:4096]
    mb0 = mb[:, 0:2048]; mb1 = mb[:, 2048:4096]
    xt1 = KT_ckv[:, 2048:4096].bitcast(F32)
    xt2 = KT_kpe[:, 2048:4096].bitcast(F32)
    kaf = KA_ckv[:, 16:32, :].rearrange("p a b -> p (a b)")
    qlatT1 = kaf[:, 0:1024].rearrange("p (a b) -> p a b", a=8)
    qpeT1 = kaf[:, 1024:2048].rearrange("p (a b) -> p a b", a=8)
    kvf = KA_v[:, 16:33, :, :].rearrange("p a g d -> p (a g d)")
    olatT_p = kvf[:, 0:1024].rearrange("p (a b) -> p a b", a=8)
    mixT_p = kvf[:, 1024:2048].rearrange("p (a b) -> p a b", a=8)
    qbT1 = KT_kb[0:65, 0, 2048:3072].rearrange("p (a b) -> p a b", a=8)
    cff = cache_f[:].rearrange("p a b -> p (a b)").bitcast(BF16)
    hb_p = cff[:, 0:1024]
    sga1 = cff[:, 1024:1536]
    sgb1 = cache_b[:].rearrange("p a b -> p (a b)")[:, 0:512]
    qbT2 = KT_kb[0:65, 1, 2048:3072].rearrange("p (a b) -> p a b", a=8)
    sga2 = kif[:].rearrange("p a b -> p (a b)").bitcast(BF16)
    sgb2 = Rb[:].rearrange("p a b -> p (a b)")
    ALIAS_BUFS = ["sc1", "mb1", "mbj_d1", "mbj_a1", "xt1", "xt2", "qlatT1", "qpeT1", "kpe_pad1", "olatT", "mixT", "qbT1", "qbT_aug1",
                  "hb", "sga1", "sgb1", "qbT2", "qbT_aug2", "sga2", "sgb2", "kif", "Rb"]
    cur = {}
    OD_S = [(5 + h // 4, (h % 4) * 65) for h in range(8)]
    OD_P = [(2 + h // 4, (h % 4) * 65) for h in range(8)]
    CFG_S = dict(xt=(xt[:], "xt"), hb=(mix[:], "mix"), olatT=(hT[:], "hT"), mixT=(qlatT[:], "qlatT"),
                 qbT=(qbT[:], "qbT", "qbT_aug"), sga=(sga[:], "sga"), sgb=(sgb[:], "sgb"),
                 sc=(sc[:], "sc"), mb=(mb[:], "mb", "mbj_d", "mbj_a"), qlatT=(qlatT[:], "qlatT"), qpeT=(qpeT[:], "qpeT", "kpe_pad"), OD=OD_S)

    def cfg_p(t):
        p = t % 2
        return dict(xt=[(xt[:], "xt"), (xt1, "xt1"), (xt2, "xt2")][t % 3], hb=(hb_p, "hb"), olatT=(olatT_p, "olatT"), mixT=(mixT_p, "mixT"),
                    qbT=[(qbT[:], "qbT", "qbT_aug"), (qbT1, "qbT1", "qbT_aug1"), (qbT2, "qbT2", "qbT_aug2")][t % 3],
                    sga=[(sga[:], "sga"), (sga1, "sga1"), (sga2, "sga2")][t % 3], sgb=[(sgb[:], "sgb"), (sgb1, "sgb1"), (sgb2, "sgb2")][t % 3],
                    sc=[(sc0, "sc"), (sc1, "sc1")][p], mb=[(mb0, "mb", "mbj_d", "mbj_a"), (mb1, "mb1", "mbj_d1", "mbj_a1")][p],
                    qlatT=[(qlatT[:], "qlatT"), (qlatT1, "qlatT1")][p], qpeT=[(qpeT[:], "qpeT", "kpe_pad"), (qpeT1, "qpeT1", "kpe_pad1")][p],
                    OD=OD_P)

    def bank(j):
        return PSX[j // 2][:, (j % 2) * 512:(j % 2 + 1) * 512]

    def bankb(j):
        return bank(j).bitcast(BF16)

    def PB(j):
        return B("ps%d" % j)

    capl = [None]

    def _op(eng, fn, reads, writes):
        r = [B(x) for x in reads]; w = [B(x) for x in writes]
        if capl[0] is not None:
            capl[0].append(("op", eng, fn, r, w))
        else:
            S.op(eng, fn, r, w)

    def V(fn, reads, writes):
        _op("vector", fn, reads, writes)

    def A(fn, reads, writes):
        _op("scalar", fn, reads, writes)

    def G(fn, reads, writes):
        _op("gpsimd", fn, reads, writes)

    def T(fn, reads, writes):
        _op("tensor", fn, reads, writes)

    def D(pairs, reads, writes, q="sync"):
        r = [B(x) for x in reads]; w = [B(x) for x in writes]
        if capl[0] is not None:
            capl[0].append(("dma", pairs, r, w, q))
        else:
            S.dma(q, pairs, r, w)

    def capture(fn):
        capl[0] = []
        fn()
        l = capl[0]
        capl[0] = None
        return l

    def emit_merged(l1, l2):
        a = b = 0
        n1, n2 = len(l1), len(l2)
        while a < n1 or b < n2:
            if a < n1 and (b >= n2 or a * n2 <= b * n1):
                it = l1[a]; a += 1
            else:
                it = l2[b]; b += 1
            if it[0] == "op":
                S.op(it[1], it[2], it[3], it[4])
            else:
                S.dma(it[4], it[1], it[2], it[3])

    G(lambda e: e.memset(ident_b[:], 1.0), [], ["ident_b"])
    G(lambda e: e.affine_select(out=ident_b[:], in_=ident_b[:], pattern=[[-1, 128]], compare_op=ALU.is_equal,
                                fill=0.0, base=0, channel_multiplier=1), ["ident_b"], ["ident_b"])
    G(lambda e: e.memset(ident_f[:], 1.0), [], ["ident_f"])
    G(lambda e: e.affine_select(out=ident_f[:], in_=ident_f[:], pattern=[[-1, 128]], compare_op=ALU.is_equal,
                                fill=0.0, base=0, channel_multiplier=1), ["ident_f"], ["ident_f"])
    for j in range(4):
        V(lambda e, j=j: e.tensor_copy(out=I4[:, j * 128:(j + 1) * 128], in_=ident_b[:]), ["ident_b"], ["I4"])
    V(lambda e: e.tensor_tensor(out=rr[0][:, 0:32], in0=ident_f[:, 0:32], in1=ident_f[:, 32:64], op=ALU.add), ["ident_f"], ["rr0"])
    V(lambda e: e.tensor_tensor(out=rr[0][:, 0:32], in0=rr[0][:, 0:32], in1=ident_f[:, 64:96], op=ALU.add), ["ident_f", "rr0"], ["rr0"])
    V(lambda e: e.tensor_tensor(out=rr[0][:, 0:32], in0=rr[0][:, 0:32], in1=ident_f[:, 96:128], op=ALU.add), ["ident_f", "rr0"], ["rr0"])
    for b in range(4):
        V(lambda e, b=b: e.reduce_sum(out=rc[:, b:b + 1], in_=ident_f[:, 32 * b:32 * b + 32], axis=mybir.AxisListType.X), ["ident_f"], ["rc"])
        for h in range(4):
            V(lambda e, b=b, h=h: e.tensor_scalar(out=Rb[:, b, h * 32:(h + 1) * 32], in0=rr[0][:, 0:32], scalar1=rc[:, b:b + 1],
                                                  scalar2=None, op0=ALU.mult), ["rr0", "rc"], ["Rb"])
    V(lambda e: e.memset(cst[:, 0:1], -0.5), [], ["cst"])
    V(lambda e: e.memset(cst[:, 1:2], 1e-6), [], ["cst"])
    D([(normg_t[:], d_normg), (qng_t[:], d_qng), (kvng_bc[:], d_kvng), (fng_bc[:], d_fng), (cfs[:], d_cfar),
       (cos_s[:], d_coss), (sin_s[:], d_sins),
       (cos_p[:], d_cosp.rearrange("(t p) r -> p t r", p=128)), (sin_p[:], d_sinp.rearrange("(t p) r -> p t r", p=128))],
      [], ["smallc"])
    V(lambda e: e.tensor_copy(out=qbT[64:65, :, :], in_=cfs[64:65, :].unsqueeze(2).to_broadcast([1, 8, 128])), ["smallc"], ["qbT_aug"])
    wst = [(xt, "xt"), (xo, "xo"), (sc[:, 0:1024], "scst0"), (sc[:, 1024:2048], "scst1"), (sc[:, 2048:3072], "scst2"), (sc[:, 3072:4096], "scst3")]
    k = 0
    for kc in range(8):
        for (c0, c1) in [(0, 1024), (1024, 2048), (2048, 2792)]:
            t_, tn = wst[k % 6]; k += 1
            n = c1 - c0
            D([(t_[:, 0:n], d_win[kc * 128:(kc + 1) * 128, c0:c1])], [], [tn])
            segs = []
            for (s0, s1, d0, sc_) in [(0, 928, 0, 1.0), (928, 1440, 928, 0.125), (1440, 1696, 1440, 1.0),
                                      (2208, 2280, 1696, 1.0), (1696, 2208, 1768, 0.125), (2280, 2792, 2280, 1.0)]:
                a0, a1 = max(s0, c0), min(s1, c1)
                if a0 < a1:
                    segs.append((a0, a1, d0 + (a0 - s0), sc_))
            for (a0, a1, dd, sc_) in segs:
                eng = V if (k % 2) else G
                eng(lambda e, t_=t_, a0=a0, a1=a1, dd=dd, sc_=sc_, kc=kc, c0=c0:
                    e.tensor_scalar(out=w_in_b[:, kc, dd:dd + (a1 - a0)], in0=t_[:, a0 - c0:a1 - c0],
                                    scalar1=normg_t[:, kc:kc + 1], scalar2=sc_, op0=ALU.mult, op1=ALU.mult),
                    [tn, "smallc"], ["w_in_b"])
    for ec in range(8):
        t_, tn = wst[k % 6]; k += 1
        D([(t_[:], d_wout[ec * 128:(ec + 1) * 128, :])], [], [tn])
        (V if ec % 2 else G)(lambda e, t_=t_, ec=ec: e.tensor_copy(out=wout_b[:, ec, :], in_=t_[:]), [tn], ["wout_b"])
    D([(xt[:, 0:512], d_wuv)], [], ["xt"])
    V(lambda e: e.tensor_copy(out=wuv_b[:], in_=xt[:, 0:512]), ["xt"], ["wuv_b"])
    D([(xo[:, 0:768], d_wuq[0:128, :]), (xo[:, 768:1024], d_wuk[:, 0:256])], [], ["xo"])
    D([(xt[:, 0:768], d_wuq[128:256, :]), (xt[:, 768:1024], d_wuk[:, 256:512])], [], ["xt"])
    wq = [xo, xt]; wqn = ["xo", "xt"]
    for jc in range(2):
        V(lambda e, jc=jc: e.tensor_scalar(out=wq[jc][:, 0:768], in0=wq[jc][:, 0:768], scalar1=qng_t[:, jc:jc + 1], scalar2=None,
                                           op0=ALU.mult), [wqn[jc], "smallc"], [wqn[jc]])
        V(lambda e, jc=jc: e.tensor_copy(out=wuqpe_b[:, jc, :].rearrange("p (h r) -> p h r", h=8),
                                         in_=wq[jc][:, 0:768].rearrange("p (h e) -> p h e", h=8)[:, :, 64:96]),
          [wqn[jc]], ["wuqpe_b"])
    uqT = sc[0:64, 0:2048].rearrange("p (h j) -> p h j", h=8)
    ukT = sc[0:64, 2048:3072].rearrange("p (h c) -> p h c", h=8)
    for h in range(8):
        for jc in range(2):
            T(lambda e, h=h, jc=jc: e.matmul(bank(0)[0:64, 0:128], lhsT=wq[jc][:, h * 96:h * 96 + 64], rhs=ident_f[:], start=True, stop=True),
              [wqn[jc], "ident_f"], ["ps0"])
            V(lambda e, h=h, jc=jc: e.tensor_copy(out=uqT[:, h, jc * 128:(jc + 1) * 128], in_=bank(0)[0:64, 0:128]), ["ps0"], ["sc", "scst0", "scst1", "scst2", "scst3"])
        src, srcn, off = (xo, "xo", 768 + h * 64) if h < 4 else (xt, "xt", 768 + (h - 4) * 64)
        T(lambda e, src=src, off=off: e.matmul(bank(1)[0:64, 0:128], lhsT=src[:, off:off + 64], rhs=ident_f[:], start=True, stop=True),
          [srcn, "ident_f"], ["ps1"])
        V(lambda e, h=h: e.tensor_copy(out=ukT[:, h, :], in_=bank(1)[0:64, 0:128]), ["ps1"], ["sc", "scst0", "scst1", "scst2", "scst3"])
    for h in range(8):
        for jc in range(2):
            pj = 2 + (h * 2 + jc) % 2
            T(lambda e, h=h, jc=jc, pj=pj: e.matmul(bank(pj)[:, 0:128], lhsT=uqT[:, h, jc * 128:(jc + 1) * 128], rhs=ukT[:, h, :],
                                                   start=True, stop=True), ["sc"], ["ps%d" % pj])
            A(lambda e, h=h, jc=jc, pj=pj: e.activation(out=wcomb_b[:, jc, h, :], in_=bank(pj)[:, 0:128], func=AF.Copy),
              ["ps%d" % pj], ["wcomb_b"])

    V(lambda e: e.memset(KA_ckv[:, :, 128:129], 1.0), [], ["KA_ones"])
    G(lambda e: e.memset(KA_v[:, :, :, 64:65], 1.0), [], ["KA_ones"])
    V(lambda e: e.memset(KT_kb[64:65, :, :], 1.0), [], ["KT_ones"])
    V(lambda e: e.memset(qpe_b[:, 256:352], 0.0), [], ["tpad"])
    V(lambda e: e.memset(kpe_bt[:, 32:128], 0.0), [], ["tpad"])
    V(lambda e: e.memset(kb_bt[:, 192:256], 0.0), [], ["tpad"])
    V(lambda e: e.memset(qb_bt[:, 512:576], 0.0), [], ["tpad"])
    V(lambda e: e.memset(qi_bt[:, 512:576], 0.0), [], ["tpad"])
    G(lambda e: e.memset(cache_b[:, :, 288:352], 0.0), [], ["tpad"])
    V(lambda e: e.memset(kib[:, 4, :], 0.0), [], ["tpad"])
    G(lambda e: e.memset(KT_kidx[64:128, :], 0.0), [], ["idx_pad"])
    V(lambda e: e.memset(qiT[64:128, :, :], 0.0), [], ["idx_pad"])
    G(lambda e: e.memset(kic[64:128, :, :], 0.0), [], ["idx_pad"])
    V(lambda e: e.memset(KT_kpe[32:64, :], 0.0), [], ["kpe_pad"])
    G(lambda e: e.memset(KT_kpe[64:128, :], 0.0), [], ["kpe_pad"])
    V(lambda e: e.memset(qpeT[32:64, :, :], 0.0), [], ["kpe_pad"])
    G(lambda e: e.memset(qpeT[64:128, :, :], 0.0), [], ["kpe_pad"])
    for typ in range(2):
        D([(xo[:], d_tb[typ])], [], ["xo"])
        for h in range(8):
            V(lambda e, typ=typ, h=h: e.tensor_scalar(out=Bt[:, typ, h, :], in0=xo[:, h * 128:(h + 1) * 128], scalar1=cfs[:, h:h + 1],
                                                      scalar2=None, op0=ALU.subtract), ["xo", "smallc"], ["Bt"])

    def rope(src, cos, sin, nh, outb, outf, rd, wr):
        cb = cos.unsqueeze(1).to_broadcast([128, nh, 16]); sbb = sin.unsqueeze(1).to_broadcast([128, nh, 16])
        x1 = src[:, :, 0:16]; x2 = src[:, :, 16:32]
        t = [rt[:, i, 0:nh, :] for i in range(6)]
        V(lambda e: e.tensor_tensor(out=t[0], in0=x1, in1=cb, op=ALU.mult), rd, ["rt"])
        V(lambda e: e.tensor_tensor(out=t[1], in0=x2, in1=sbb, op=ALU.mult), rd, ["rt"])
        V(lambda e: e.tensor_tensor(out=t[2], in0=x1, in1=sbb, op=ALU.mult), rd, ["rt"])
        V(lambda e: e.tensor_tensor(out=t[3], in0=x2, in1=cb, op=ALU.mult), rd, ["rt"])
        tgt = outf if outf is not None else outb
        V(lambda e: e.tensor_tensor(out=tgt[:, :, 0:16], in0=t[0], in1=t[1], op=ALU.subtract), ["rt"], wr)
        V(lambda e: e.tensor_tensor(out=tgt[:, :, 16:32], in0=t[2], in1=t[3], op=ALU.add), ["rt"], wr)
        if outf is not None:
            V(lambda e: e.tensor_copy(out=outb, in_=outf), wr, wr + ["rope_b"])

    def rstd_from(ssq_ap, out_ap, n, rd, wr):
        V(lambda e: e.tensor_scalar(out=out_ap, in0=ssq_ap, scalar1=1.0 / n, scalar2=1e-6, op0=ALU.mult, op1=ALU.add), rd, wr)
        G(lambda e: e.tensor_tensor(out=out_ap, in0=out_ap, in1=cst[:, 0:1], op=ALU.pow), wr + ["cst"], wr)

    def transposes(items, evac_out, evac_reads_extra=()):
        pass

    def proj_tile(xsrc, cos, sin, odst, row0, blk, sample):
        xt_, xtn = cur["xt"]; hb_, hbn = cur["hb"]
        qbT_, qbTn, _ = cur["qbT"]; sga_, sgan = cur["sga"]; sgb_, sgbn = cur["sgb"]
        D([(xt_, xsrc)], [], [xtn])
        A(lambda e: e.activation(out=hb_, in_=xt_, func=AF.Square, accum_out=stt[:, 0:1]), [xtn], [hbn, "stt0"])
        rstd_from(stt[:, 0:1], stt[:, 1:2], 1024.0, ["stt0"], ["stt1"])
        V(lambda e: e.tensor_scalar(out=hb_, in0=xt_, scalar1=stt[:, 1:2], scalar2=None, op0=ALU.mult), [xtn, "stt1"], [hbn])
        hv = bankb(7).rearrange("p (a b) -> p a b", a=8)
        for kc in range(8):
            T(lambda e, kc=kc: e.transpose(hv[:, kc, :], hb_[:, kc * 128:(kc + 1) * 128], ident_b[:]), [hbn, "ident_b"], ["ps7"])
        A(lambda e: e.activation(out=hT[:].rearrange("p a b -> p (a b)"), in_=bankb(7), func=AF.Copy), ["ps7"], ["hT"])
        def group(gi, pj):
            c0, c1 = GROUPS[gi]
            for kc in range(8):
                T(lambda e, kc=kc, c0=c0, c1=c1, pj=pj: e.matmul(bank(pj)[:, 0:c1 - c0], lhsT=hT[:, kc, :], rhs=w_in_b[:, kc, c0:c1],
                                                               start=(kc == 0), stop=(kc == 7)), ["hT", "w_in_b"], ["ps%d" % pj])
        group(0, 5)
        pa = bank(5)
        A(lambda e: e.activation(out=hb_[:, 0:256], in_=pa[:, 0:256], func=AF.Square, accum_out=stt[:, 2:3]), ["ps5"], [hbn, "stt2"])
        A(lambda e: e.activation(out=hb_[:, 256:384], in_=pa[:, 256:384], func=AF.Square, accum_out=stt[:, 3:4]), ["ps5"], [hbn, "stt3"])
        rstd_from(stt[:, 2:3], stt[:, 4:5], 256.0, ["stt2"], ["stt4"])
        rstd_from(stt[:, 3:4], stt[:, 5:6], 128.0, ["stt3"], ["stt5"])
        V(lambda e: e.tensor_scalar(out=cq_b[:], in0=pa[:, 0:256], scalar1=stt[:, 4:5], scalar2=None, op0=ALU.mult), ["ps5", "stt4"], ["cq_b"])
        V(lambda e: e.scalar_tensor_tensor(out=ckv_f[:], in0=pa[:, 256:384], scalar=stt[:, 5:6], in1=kvng_bc[:], op0=ALU.mult, op1=ALU.mult),
          ["ps5", "stt5", "smallc"], ["ckv_f"])
        rope(pa[:, 384:416].rearrange("p (h r) -> p h r", h=1), cos, sin, 1, kpe_bt[:, 0:32].rearrange("p (h r) -> p h r", h=1),
             kpe_f[:].rearrange("p (h r) -> p h r", h=1), ["ps5", "smallc"], ["kpe_f"])
        D([(odst["ckv"][row0:row0 + 128, :], ckv_f[:])], ["ckv_f"], [], q="gpsimd")
        D([(odst["kpe"][row0:row0 + 128, :], kpe_f[:])], ["kpe_f"], [], q="gpsimd")
        G(lambda e: e.tensor_copy(out=ckv_bt[:], in_=ckv_f[:]), ["ckv_f"], ["ckv_bt"])
        if not sample:
            G(lambda e: e.tensor_copy(out=KA_ckv[:, blk, 0:128], in_=ckv_f[:]), ["ckv_f"], ["KA_ckv%d" % blk])
        group(1, 6)
        A(lambda e: e.activation(out=rr[0][:], in_=bank(6), func=AF.Tanh, scale=0.5), ["ps6"], ["rr0"])
        V(lambda e: e.scalar_tensor_tensor(out=sga_, in0=rr[0][:], scalar=1.0, in1=bank(6), op0=ALU.add, op1=ALU.mult), ["rr0", "ps6"], [sgan])
        group(2, 5)
        V(lambda e: e.tensor_copy(out=qb_bt[:, 0:512], in_=bank(5)), ["ps5"], ["qb_bt"])
        group(3, 6)
        A(lambda e: e.activation(out=kv_f[:], in_=bank(6)[:, 0:328], func=AF.Copy), ["ps6"], ["kv_f"])
        D([(odst["k"][row0:row0 + 128, :], kv_f[:, 0:128])], ["kv_f"], [], q="gpsimd")
        D([(odst["v"][row0:row0 + 128, :], kv_f[:, 128:256])], ["kv_f"], [], q="gpsimd")
        D([(odst["kidx"][row0:row0 + 128, :], kv_f[:, 256:320])], ["kv_f"], [], q="gpsimd")
        G(lambda e: e.tensor_copy(out=kb_bt[:, 0:128], in_=kv_f[:, 0:128]), ["kv_f"], ["kb_bt"])
        G(lambda e: e.tensor_copy(out=kb_bt[:, 128:192], in_=kv_f[:, 256:320]), ["kv_f"], ["kb_bt"])
        if sample:
            G(lambda e: e.tensor_copy(out=v_bt[:], in_=kv_f[:, 128:256]), ["kv_f"], ["v_bt"])
        else:
            G(lambda e: e.tensor_copy(out=KA_v[:, blk, :, 0:64], in_=kv_f[:, 128:256].rearrange("p (g d) -> p g d", g=2)),
              ["kv_f"], ["KA_v%d" % blk])
        A(lambda e: e.activation(out=wabs[:], in_=kv_f[:, 320:328], func=AF.Abs, scale=8.0 ** -0.5), ["kv_f"], ["wabs"])
        V(lambda e: e.tensor_scalar(out=wsgn[:], in0=kv_f[:, 320:328], scalar1=0.0, scalar2=2.0, op0=ALU.is_ge, op1=ALU.mult),
          ["kv_f"], ["wsgn"])
        V(lambda e: e.tensor_scalar(out=wsgn[:], in0=wsgn[:], scalar1=-1.0, scalar2=None, op0=ALU.add), ["wsgn"], ["wsgn"])
        group(4, 5)
        V(lambda e: e.tensor_copy(out=qi_bt[:, 0:512], in_=bank(5)), ["ps5"], ["qi_bt"])
        group(5, 6)
        A(lambda e: e.activation(out=rr[1][:], in_=bank(6), func=AF.Tanh, scale=0.5), ["ps6"], ["rr1"])
        V(lambda e: e.scalar_tensor_tensor(out=sgb_, in0=rr[1][:], scalar=1.0, in1=bank(6), op0=ALU.add, op1=ALU.mult), ["rr1", "ps6"], [sgbn])
        pv7 = bankb(7)
        for jc in range(2):
            T(lambda e, jc=jc: e.transpose(pv7[:, jc * 128:(jc + 1) * 128], cq_b[:, jc * 128:(jc + 1) * 128], ident_b[:]), ["cq_b", "ident_b"], ["ps7"])
        T(lambda e: e.transpose(pv7[:, 256:384], ckv_bt[:], ident_b[:]), ["ckv_bt", "ident_b"], ["ps7"])
        T(lambda e: e.transpose(pv7[:, 384:512], kpe_bt[:], ident_b[:]), ["rope_b", "kpe_f", "ident_b", "tpad"], ["ps7"])
        for g in range(2):
            T(lambda e, g=g: e.transpose(pv7[:, 512 + g * 128:640 + g * 128], kb_bt[:, g * 64:g * 64 + 128], ident_b[:]), ["kb_bt", "ident_b", "tpad"], ["ps7"])
        T(lambda e: e.transpose(pv7[:, 768:896], kb_bt[:, 128:256], ident_b[:]), ["kb_bt", "ident_b", "tpad"], ["ps7"])
        V(lambda e: e.tensor_copy(out=cqT[:].rearrange("p a b -> p (a b)"), in_=pv7[:, 0:256]), ["ps7"], ["cqT"])
        if sample:
            kdst = dict(ckv=xo[:, 0:128].bitcast(BF16)[:, 0:128], kpe=xo[0:32, 128:256].bitcast(BF16)[:, 0:128],
                        kb=[xo[0:64, 256:384].bitcast(BF16)[:, 0:128], xo[0:64, 384:512].bitcast(BF16)[:, 0:128]],
                        kidx=xo[0:64, 512:640].bitcast(BF16)[:, 0:128])
            kw = ["xo"]
        else:
            kdst = dict(ckv=KT_ckv[:, blk * 128:(blk + 1) * 128], kpe=KT_kpe[0:32, blk * 128:(blk + 1) * 128],
                        kb=[KT_kb[0:64, g, blk * 128:(blk + 1) * 128] for g in range(2)],
                        kidx=KT_kidx[0:64, blk * 128:(blk + 1) * 128])
            kw = ["KT%d" % blk]
        V(lambda e: e.tensor_copy(out=kdst["ckv"], in_=pv7[:, 256:384]), ["ps7"], kw)
        V(lambda e: e.tensor_copy(out=kdst["kpe"], in_=pv7[0:32, 384:512]), ["ps7"], kw)
        for g in range(2):
            V(lambda e, g=g: e.tensor_copy(out=kdst["kb"][g], in_=pv7[0:64, 512 + g * 128:640 + g * 128]), ["ps7"], kw)
        V(lambda e: e.tensor_copy(out=kdst["kidx"], in_=pv7[0:64, 768:896]), ["ps7"], kw)
        qlatT_, qlatTn = cur["qlatT"]; qpeT_, qpeTn, _ = cur["qpeT"]
        for h in range(8):
            pj = 5 + h // 4
            for jc in range(2):
                T(lambda e, h=h, jc=jc, pj=pj: e.matmul(bank(pj)[:, (h % 4) * 128:(h % 4 + 1) * 128], lhsT=wcomb_b[:, jc, h, :], rhs=cqT[:, jc, :],
                                                       start=(jc == 0), stop=(jc == 1)), ["wcomb_b", "cqT"], ["ps%d" % pj])
        A(lambda e: e.activation(out=qlatT_[:, 0:4, :].rearrange("p a b -> p (a b)"), in_=bank(5), func=AF.Copy), ["ps5"], [qlatTn])
        V(lambda e: e.tensor_copy(out=qlatT_[:, 4:8, :].rearrange("p a b -> p (a b)"), in_=bank(6)), ["ps6"], [qlatTn])
        for jc in range(2):
            T(lambda e, jc=jc: e.matmul(bank(7)[:, 0:256], lhsT=cqT[:, jc, :], rhs=wuqpe_b[:, jc, :], start=(jc == 0), stop=(jc == 1)),
              ["cqT", "wuqpe_b"], ["ps7"])
        rope(bank(7)[:, 0:256].rearrange("p (h r) -> p h r", h=8), cos, sin, 8, qpe_b[:, 0:256].rearrange("p (h r) -> p h r", h=8), None, ["ps7", "smallc"], ["qpe_b"])
        pv6 = bankb(5)
        for h in range(8):
            T(lambda e, h=h: e.transpose(pv6[:, h * 128:(h + 1) * 128], qpe_b[:, h * 32:h * 32 + 128], ident_b[:]), ["qpe_b", "ident_b", "tpad"], ["ps5"])
        V(lambda e: e.tensor_copy(out=qpeT_[0:32, :, :].rearrange("p a b -> p (a b)"), in_=pv6[0:32, :]), ["ps5"], [qpeTn])
        pv5 = bankb(6)
        for h in range(8):
            T(lambda e, h=h: e.transpose(pv5[:, h * 128:(h + 1) * 128], qb_bt[:, h * 64:h * 64 + 128], ident_b[:]), ["qb_bt", "ident_b", "tpad"], ["ps6"])
        A(lambda e: e.activation(out=qbT_[0:64, :, :].rearrange("p a b -> p (a b)"), in_=pv5[0:64, :], func=AF.Copy), ["ps6"], [qbTn])
        for h in range(8):
            T(lambda e, h=h: e.transpose(pv7[:, h * 128:(h + 1) * 128], qi_bt[:, h * 64:h * 64 + 128], ident_b[:]), ["qi_bt", "ident_b", "tpad"], ["ps7"])
        V(lambda e: e.tensor_copy(out=qiT[0:64, :, :].rearrange("p a b -> p (a b)"), in_=pv7[0:64, :]), ["ps7"], ["qiT"])

    state = {"r": 0, "p": 0, "sps": 0}

    def idx_epilogue(pj, h, c0, n):
        j = state["r"]; state["r"] ^= 1
        sc_, scn = cur["sc"]
        A(lambda e: e.activation(out=rr[j][:, 0:n], in_=bank(pj)[:, 0:n], func=AF.Relu, scale=wabs[:, h:h + 1]), ["ps%d" % pj, "wabs"], ["rr%d" % j])
        if h == 0:
            V(lambda e: e.tensor_scalar(out=sc_[:, c0:c0 + n], in0=rr[j][:, 0:n], scalar1=wsgn[:, 0:1], scalar2=None, op0=ALU.mult),
              ["rr%d" % j, "wsgn"], [scn])
        else:
            V(lambda e: e.scalar_tensor_tensor(out=sc_[:, c0:c0 + n], in0=rr[j][:, 0:n], scalar=wsgn[:, h:h + 1], in1=sc_[:, c0:c0 + n],
                                               op0=ALU.mult, op1=ALU.add), ["rr%d" % j, "wsgn", scn], [scn])

    def bisect(N, do):
        sc_, scn = cur["sc"]; mb_, mbn, mjd, mja = cur["mb"]
        if not do:
            V(lambda e: e.memset(bis[:, 3:4], -1e9), [], ["bis_thr"])
        else:
            M = int(round(0.55 * N / 32.0)) * 32
            Nd = N - M
            V(lambda e: e.memset(bis[:, 0:1], 0.0), [], ["bis_t"])
            g = 32.0
            for it in range(NITER):
                last = (it == NITER - 1)
                V(lambda e: e.tensor_scalar(out=mb_[:, 0:Nd], in0=sc_[:, 0:Nd], scalar1=bis[:, 0:1], scalar2=None, op0=ALU.is_ge, op1=ALU.add,
                                            accum_out=bis[:, 1:2]), [scn, "bis_t"], [mjd, "bis_c"])
                A(lambda e: e.activation(out=mb_[:, Nd:N], in_=sc_[:, Nd:N], func=AF.Sign, scale=-1.0, bias=bis[:, 0:1], accum_out=bis[:, 4:5]),
                  [scn, "bis_t"], [mja, "bis_s"])
                V(lambda e: e.scalar_tensor_tensor(out=bis[:, 5:6], in0=bis[:, 1:2], scalar=2.0, in1=bis[:, 4:5], op0=ALU.mult, op1=ALU.subtract),
                  ["bis_c", "bis_s"], ["bis_u"])
                cthr = 511.0 - M
                if not last:
                    hh = g / 2
                    V(lambda e, hh=hh: e.tensor_scalar(out=bis[:, 2:3], in0=bis[:, 5:6], scalar1=cthr, scalar2=2 * hh, op0=ALU.is_ge, op1=ALU.mult),
                      ["bis_u"], ["bis_d"])
                    V(lambda e, hh=hh: e.scalar_tensor_tensor(out=bis[:, 0:1], in0=bis[:, 2:3], scalar=-hh, in1=bis[:, 0:1], op0=ALU.add, op1=ALU.add),
                      ["bis_d", "bis_t"], ["bis_t"])
                    g = hh
                else:
                    V(lambda e, g=g: e.tensor_scalar(out=bis[:, 2:3], in0=bis[:, 5:6], scalar1=cthr, scalar2=g, op0=ALU.is_ge, op1=ALU.mult),
                      ["bis_u"], ["bis_d"])
                    V(lambda e, g=g: e.scalar_tensor_tensor(out=bis[:, 3:4], in0=bis[:, 2:3], scalar=-g, in1=bis[:, 0:1], op0=ALU.add, op1=ALU.add),
                      ["bis_d", "bis_t"], ["bis_thr"])
                yield
        V(lambda e: e.tensor_scalar(out=mb_[:, 0:N], in0=sc_[:, 0:N], scalar1=bis[:, 3:4], scalar2=NEG, op0=ALU.is_lt, op1=ALU.mult),
          [scn, "bis_thr"], [mbn, mjd, mja])
        yield

    def interleave(g1, n1, g2, n2):
        a = b = 0
        d1 = d2 = False
        while not (d1 and d2):
            if not d1 and (d2 or a * n2 <= b * n1):
                try:
                    next(g1); a += 1
                except StopIteration:
                    d1 = True
            elif not d2:
                try:
                    next(g2); b += 1
                except StopIteration:
                    d2 = True

    OM = [(2 + h // 3, (h % 3) * 129) for h in range(8)]

    def attend_block(q0, nq, kcol, nk, kblk, first, tp, mla_mask, dsa_mask, dsa_bias, Rsel, kreads, which="both"):
        units = []
        qlatT_, qlatTn = cur["qlatT"]; qpeT_, qpeTn, qpadn = cur["qpeT"]
        mb_, mbn = cur["mb"][0], cur["mb"][1]
        OD = cur["OD"]
        hgs = [(0, 4), (4, 8)] if nq == 128 else [(0, 8)]
        if which == "dsa":
            hgs = []
        for (h0, h1) in hgs:
            nh = h1 - h0
            pj = state["sps"]; state["sps"] ^= 1
            pi = state["p"]; state["p"] ^= 1
            n = nh * nq
            sv = bank(pj)[0:nk, 0:n]
            pt = Pt[pi]

            def qk(sv=sv, h0=h0, h1=h1, pj=pj, pi=pi, pt=pt, n=n, nh=nh):
                T(lambda e: e.matmul(sv, lhsT=KT_ckv[:, kcol:kcol + nk], rhs=qlatT_[:, h0:h1, q0:q0 + nq], start=True, stop=False),
                  kreads + [qlatTn], ["ps%d" % pj])
                T(lambda e: e.matmul(sv, lhsT=KT_kpe[:, kcol:kcol + nk], rhs=qpeT_[:, h0:h1, q0:q0 + nq], start=False, stop=True),
                  kreads + [qpeTn, qpadn, "kpe_pad"], ["ps%d" % pj])
                A(lambda e: e.activation(out=pt[0:nk, 0:n], in_=sv, func=AF.Exp, scale=MLA_SCALE), ["ps%d" % pj], ["P%d" % pi])
                if mla_mask:
                    G(lambda e: e.memset(pt[64:128, 0:nh * 128].rearrange("p (h q) -> p h q", h=nh)[:, :, 0:64], 0.0), ["P%d" % pi], ["P%d" % pi])

            def pv(h0=h0, h1=h1, pi=pi, pt=pt):
                for h in range(h0, h1):
                    bj, co = OM[h]
                    st_flag = first and (h % 3 == 0)
                    T(lambda e, h=h, bj=bj, co=co, st_flag=st_flag: e.matmul(
                        bank(bj)[q0:q0 + nq, co:co + 129], lhsT=pt[0:nk, (h - h0) * nq:(h - h0 + 1) * nq], rhs=KA_ckv[0:nk, kblk, :],
                        start=st_flag, stop=False, tile_position=tp, skip_group_check=True),
                      ["P%d" % pi, "KA_ones"] + kreads, ["ps%d" % bj])
            units.append((qk, pv))
        for g in (range(2) if which != "mla" else []):
            pj = state["sps"]; state["sps"] ^= 1
            pi = state["p"]; state["p"] ^= 1
            n = 4 * nq
            sv = bank(pj)[0:nk, 0:n]
            pt = Pt[pi]
            more = dsa_mask or (dsa_bias is not None)

            qbT_, qbTn, qbTan = cur["qbT"]

            def qk(sv=sv, g=g, pj=pj, pi=pi, pt=pt, n=n, more=more, qbT_=qbT_, qbTn=qbTn, qbTan=qbTan):
                T(lambda e: e.matmul(sv, lhsT=KT_kb[:, g, kcol:kcol + nk], rhs=qbT_[:, 4 * g:4 * g + 4, q0:q0 + nq], start=True, stop=not more),
                  kreads + [qbTn, qbTan, "KT_ones"], ["ps%d" % pj])
                if dsa_mask:
                    rhs_m = I4[:, 0:n] if Rsel is None else Rb[:, Rsel, :]
                    T(lambda e: e.matmul(sv, lhsT=mb_[:, kcol:kcol + nk], rhs=rhs_m, start=False, stop=(dsa_bias is None)),
                      [mbn, "I4", "Rb"], ["ps%d" % pj])
                if dsa_bias is not None:
                    T(lambda e: e.matmul(sv, lhsT=ident_b[0:nk, 0:nk], rhs=Bt[0:nk, dsa_bias, 4 * g:4 * g + 4, 0:nq], start=False, stop=True),
                      ["Bt", "ident_b"], ["ps%d" % pj])
                A(lambda e: e.activation(out=pt[0:nk, 0:n], in_=sv, func=AF.Exp), ["ps%d" % pj], ["P%d" % pi])

            def pv(g=g, pi=pi, pt=pt):
                for hl in range(4):
                    h = 4 * g + hl
                    bj, co = OD[h]
                    st_flag = first and (hl == 0)
                    T(lambda e, hl=hl, bj=bj, co=co, st_flag=st_flag: e.matmul(
                        bank(bj)[q0:q0 + nq, co:co + 65], lhsT=pt[0:nk, hl * nq:(hl + 1) * nq], rhs=KA_v[0:nk, kblk, g, :],
                        start=st_flag, stop=False, tile_position=tp, skip_group_check=True),
                      ["P%d" % pi, "KA_ones"] + kreads, ["ps%d" % bj])
            units.append((qk, pv))
        return units

    def run_units(units):
        if not units:
            return
        units[0][0]()
        for n in range(len(units)):
            if n + 1 < len(units):
                units[n + 1][0]()
            units[n][1]()
            yield

    def mla_finish():
        for j, nh in [(0, 3), (1, 3), (2, 2)]:
            V(lambda e, j=j, nh=nh: e.reciprocal(out=rc[:, 3 * j:3 * j + nh], in_=bank(2 + j)[:, 0:nh * 129].rearrange("p (h c) -> p h c", h=nh)[:, :, 128]),
              ["ps%d" % (2 + j)], ["rc"])
        for h in range(8):
            bj, co = OM[h]
            V(lambda e, h=h, bj=bj, co=co: e.tensor_scalar(out=olat_b[:, h, :], in0=bank(bj)[:, co:co + 128], scalar1=rc[:, h:h + 1], scalar2=None, op0=ALU.mult),
              ["ps%d" % bj, "rc"], ["olat_b"])

    def combine(xdst, row0, sample=False):
        xt_, xtn = cur["xt"]; olatT, olatTn = cur["olatT"]; mixT, mixTn = cur["mixT"]
        sga_, sgan = cur["sga"]; sgb_, sgbn = cur["sgb"]
        OD = cur["OD"]; odb = OD[0][0]
        pv7 = bankb(0).rearrange("p (a b) -> p a b", a=8)
        if sample:
            V(lambda e: e.scalar_tensor_tensor(out=mix[:, 512:1024], in0=bank(5), scalar=0.5, in1=sgb_, op0=ALU.mult, op1=ALU.mult), ["ps5", sgbn], ["mix"])
        else:
            for j in range(2):
                V(lambda e, j=j: e.reciprocal(out=rc[:, 8 + 4 * j:12 + 4 * j], in_=bank(odb + j)[:, 0:260].rearrange("p (h c) -> p h c", h=4)[:, :, 64]),
                  ["ps%d" % (odb + j)], ["rcd"])
            V(lambda e: e.tensor_scalar(out=rc[:, 8:16], in0=rc[:, 8:16], scalar1=0.5, scalar2=None, op0=ALU.mult), ["rcd"], ["rcd"])
            for h in range(8):
                bj, co = OD[h]
                V(lambda e, h=h, bj=bj, co=co: e.scalar_tensor_tensor(out=mix[:, 512 + h * 64:576 + h * 64], in0=bank(bj)[:, co:co + 64], scalar=rc[:, 8 + h:9 + h],
                                                                      in1=sgb_[:, h * 64:(h + 1) * 64], op0=ALU.mult, op1=ALU.mult),
                  ["ps%d" % bj, "rcd", sgbn], ["mix"])
            pv7 = bankb(0).rearrange("p (a b) -> p a b", a=8)
            for h in range(8):
                T(lambda e, h=h: e.transpose(pv7[:, h, :], olat_b[:, h, :], ident_b[:]), ["olat_b", "ident_b"], ["ps0"])
            A(lambda e: e.activation(out=olatT[:].rearrange("p a b -> p (a b)"), in_=bankb(0), func=AF.Copy), ["ps0"], [olatTn])
        for h in range(8):
            T(lambda e, h=h: e.matmul(bank(1)[:, h * 64:(h + 1) * 64], lhsT=olatT[:, h, :], rhs=wuv_b[:, h * 64:(h + 1) * 64], start=True, stop=True),
              [olatTn, "wuv_b"], ["ps1"])
        V(lambda e: e.scalar_tensor_tensor(out=mix[:, 0:512], in0=bank(1), scalar=0.5, in1=sga_, op0=ALU.mult, op1=ALU.mult), ["ps1", sgan], ["mix"])
        for ec in range(8):
            T(lambda e, ec=ec: e.transpose(pv7[:, ec, :], mix[:, ec * 128:(ec + 1) * 128], ident_b[:]), ["mix", "ident_b"], ["ps0"])
        A(lambda e: e.activation(out=mixT[:].rearrange("p a b -> p (a b)"), in_=bankb(0), func=AF.Copy), ["ps0"], [mixTn])
        for nn in range(2):
            for ec in range(8):
                T(lambda e, nn=nn, ec=ec: e.matmul(bank(nn), lhsT=mixT[:, ec, :], rhs=wout_b[:, ec, nn * 512:(nn + 1) * 512], start=(ec == 0), stop=(ec == 7)),
                  [mixTn, "wout_b"], ["ps%d" % nn])
        V(lambda e: e.tensor_tensor(out=xo[:], in0=PSX[0][:], in1=xt_, op=ALU.add), ["ps0", "ps1", xtn], ["xo"])
        A(lambda e: e.activation(out=mix[:], in_=xo[:], func=AF.Square, accum_out=stt[:, 6:7]), ["xo"], ["mix", "stt6"])
        rstd_from(stt[:, 6:7], stt[:, 7:8], 1024.0, ["stt6"], ["stt7"])
        V(lambda e: e.scalar_tensor_tensor(out=xo[:], in0=xo[:], scalar=stt[:, 7:8], in1=fng_bc[:], op0=ALU.mult, op1=ALU.mult),
          ["xo", "stt7", "smallc"], ["xo"])
        D([(xdst[row0:row0 + 128, :], xo[:])], ["xo"], [], q="gpsimd")

    if SAMPLE:
        cur.update(CFG_S)
        proj_tile(xs, cos_s[:], sin_s[:], o_s, 0, None, True)
        nckv = xo[:, 0:128].bitcast(BF16)[:, 0:128]; nkpe = xo[0:32, 128:256].bitcast(BF16)[:, 0:128]
        nkb = [xo[0:64, 256:384].bitcast(BF16)[:, 0:128], xo[0:64, 384:512].bitcast(BF16)[:, 0:128]]
        nkidx = xo[0:64, 512:640].bitcast(BF16)[:, 0:128]
        kifs = [(kif[:], "kif"), (cache_f[:, 0, 0:256].rearrange("p (a b) -> p a b", a=4), "cache_f0")]
        kibs = [(kib[:], "kib"), (cache_b[:, 0, 0:320].rearrange("p (a b) -> p a b", a=5), "cache_b0")]
        kics = [(kic[:], "kic"), (KT_ckv[:, 0:2048].rearrange("p (a b) -> p a b", a=4), "kic2")]
        V(lambda e: e.memset(KT_ckv[64:128, 0:2048], 0.0), [], ["kic2"])

        def load_a(c, b):
            sl = (c * 4 + b) % 2
            kf, kfn = kifs[sl]; kb_, kbn = kibs[sl]
            D([(kf, c_kidx[b, c * 512:(c + 1) * 512, :].rearrange("(k p) d -> p k d", p=128))], [], [kfn])
            G(lambda e: e.tensor_copy(out=kb_[:, 0:4, :], in_=kf), [kfn], [kbn])

        def load_b(c, b):
            sl = (c * 4 + b) % 2
            kb_, kbn = kibs[sl]; kc_, kcn = kics[c % 2]
            pvx = bankb(6 + sl); pn = "ps%d" % (6 + sl)
            for kk in range(4):
                T(lambda e, kk=kk: e.transpose(pvx[:, kk * 128:(kk + 1) * 128], kb_[:, kk:kk + 2, :].rearrange("p a b -> p (a b)"), ident_b[:]),
                  [kbn, "ident_b", "tpad"], [pn])
            V(lambda e: e.tensor_copy(out=kc_[0:64, b, :], in_=pvx[0:64, 0:512]), [pn], [kcn])

        def compute(c, heads):
            n = 512 if c < 8 else 32
            kc_, kcn = kics[c % 2]
            for h in heads:
                pj = h % 2
                for b in range(4):
                    T(lambda e, h=h, b=b, pj=pj, n=n: e.matmul(bank(pj)[32 * b:32 * b + 32, 0:n], lhsT=qiT[:, h, 32 * b:32 * b + 32], rhs=kc_[:, b, 0:n],
                                                             start=True, stop=True, tile_position=(0, 32 * b)), ["qiT", kcn, "idx_pad"], ["ps%d" % pj])
                idx_epilogue(pj, h, c * 512, n)

        for b in range(4):
            load_a(0, b)
            load_b(0, b)
        for c in range(9):
            nxt = c + 1
            if nxt < 8:
                load_a(nxt, 0); load_a(nxt, 1)
                compute(c, [0, 1])
                load_b(nxt, 0); load_a(nxt, 2)
                compute(c, [2, 3])
                load_b(nxt, 1); load_a(nxt, 3)
                compute(c, [4, 5])
                load_b(nxt, 2)
                compute(c, [6, 7])
                load_b(nxt, 3)
            else:
                if nxt == 8:
                    kc_, kcn = kics[0]
                    for b in range(4):
                        V(lambda e, b=b, kc_=kc_: e.tensor_copy(out=kc_[0:64, b, 0:32], in_=nkidx[:, 32 * b:32 * b + 32]), ["xo"], [kcn])
                compute(c, list(range(8)))
        V(lambda e: e.memset(bis[:, 6:7], 0.0), [], ["kic2", "cache_f0", "cache_b0"] + ["KT%d" % k for k in range(16)])
        pend = [None]

        def push(u):
            u[0]()
            if pend[0] is not None:
                pend[0][1]()
            pend[0] = u

        def load_block(b, blk):
            kk = blk % 2
            k0 = blk * 128
            cf = "cache_f%d" % kk; cb = "cache_b%d" % kk; pb = "ps%d" % (6 + kk)
            D([(cache_f[:, kk, 0:128], c_ckv[b, k0:k0 + 128, :]), (cache_f[:, kk, 128:160], c_kpe[b, k0:k0 + 128, :]),
               (cache_f[:, kk, 160:288], c_k[b, k0:k0 + 128, :]), (cache_f[:, kk, 288:416], c_v[b, k0:k0 + 128, :])], [], [cf])
            G(lambda e: e.tensor_copy(out=cache_b[:, kk, 0:288], in_=cache_f[:, kk, 0:288]), [cf], [cb])
            A(lambda e: e.activation(out=KA_ckv[:, blk, 0:128], in_=cache_f[:, kk, 0:128], func=AF.Copy), [cf], ["KA_ckv%d" % blk])
            V(lambda e: e.tensor_copy(out=KA_v[:, blk, :, 0:64], in_=cache_f[:, kk, 288:416].rearrange("p (g d) -> p g d", g=2)),
              [cf], ["KA_v%d" % blk])
            pv7 = bankb(6 + kk)
            o = 0
            T(lambda e: e.transpose(pv7[:, o:o + 128], cache_b[:, kk, 0:128], ident_b[:]), [cb, "ident_b"], [pb])
            T(lambda e: e.transpose(pv7[:, o + 128:o + 256], cache_b[:, kk, 128:256], ident_b[:]), [cb, "ident_b", "tpad"], [pb])
            for g in range(2):
                T(lambda e, g=g: e.transpose(pv7[:, o + 256 + g * 128:o + 384 + g * 128], cache_b[:, kk, 160 + g * 64:288 + g * 64], ident_b[:]),
                  [cb, "ident_b", "tpad"], [pb])
            cs = slice(blk * 128, (blk + 1) * 128)
            V(lambda e: e.tensor_copy(out=KT_ckv[:, cs], in_=pv7[:, o:o + 128]), [pb], ["KT%d" % blk])
            V(lambda e: e.tensor_copy(out=KT_kpe[0:32, cs], in_=pv7[0:32, o + 128:o + 256]), [pb], ["KT%d" % blk])
            V(lambda e: e.tensor_copy(out=KT_kb[0:64, :, cs], in_=pv7[0:64, o + 256:o + 512].rearrange("p (g k) -> p g k", g=2)),
              [pb], ["KT%d" % blk])

        def new_keys(b):
            cs = slice(4096, 4128)
            V(lambda e: e.tensor_copy(out=KT_ckv[:, cs], in_=nckv[:, 32 * b:32 * b + 32]), ["xo"], ["KT32"])
            V(lambda e: e.tensor_copy(out=KT_kpe[0:32, cs], in_=nkpe[:, 32 * b:32 * b + 32]), ["xo"], ["KT32"])
            for g in range(2):
                V(lambda e, g=g: e.tensor_copy(out=KT_kb[0:64, g, cs], in_=nkb[g][:, 32 * b:32 * b + 32]), ["xo"], ["KT32"])
            D([(KA_ckv[0:32, 32, 0:128], ckv_bt[32 * b:32 * b + 32, :])], ["ckv_bt"], ["KA_ckv32"])
            D([(KA_v[0:32, 32, :, 0:64], v_bt[32 * b:32 * b + 32, :].rearrange("p (g d) -> p g d", g=2))], ["v_bt"], ["KA_v32"])

        on_b = olat_b[:, 0:2, :]
        obn = olat_b[:, 2, :].rearrange("p (g d) -> p g d", g=2)
        olatT_S = CFG_S["olatT"][0]

        def att(b, blk, which="both"):
            nk = 128 if blk < 32 else 32
            bias = 1 if blk == 31 else (0 if blk == 32 else None)
            kcol = blk * 128
            q0 = 32 * b
            ob = 2 + (b % 2)
            kreads = ["KT%d" % blk, "KA_ckv%d" % blk, "KA_v%d" % blk]
            first = (blk == 0)
            if which != "dsa":
                pj = state["sps"]; state["sps"] ^= 1
                pi = state["p"]; state["p"] ^= 1
                sv = bank(pj)[0:nk, 0:256]
                pt = Pt[pi]

                def qk(sv=sv, pj=pj, pi=pi, pt=pt):
                    T(lambda e: e.matmul(sv, lhsT=KT_ckv[:, kcol:kcol + nk], rhs=qlatT[:, :, q0:q0 + 32], start=True, stop=False),
                      kreads + ["qlatT"], ["ps%d" % pj])
                    T(lambda e: e.matmul(sv, lhsT=KT_kpe[:, kcol:kcol + nk], rhs=qpeT[:, :, q0:q0 + 32], start=False, stop=True),
                      kreads + ["qpeT", "kpe_pad"], ["ps%d" % pj])
                    A(lambda e: e.activation(out=pt[0:nk, 0:256], in_=sv, func=AF.Exp, scale=MLA_SCALE), ["ps%d" % pj], ["P%d" % pi])

                def pv(pi=pi, pt=pt):
                    for hg in range(2):
                        T(lambda e, hg=hg: e.matmul(bank(ob)[:, hg * 129:(hg + 1) * 129], lhsT=pt[0:nk, hg * 128:(hg + 1) * 128], rhs=KA_ckv[0:nk, blk, :],
                                                    start=(first and hg == 0), stop=False, skip_group_check=True),
                          ["P%d" % pi, "KA_ones"] + kreads, ["ps%d" % ob])
                push((qk, pv))
            for g in (range(2) if which != "mla" else []):
                pj = state["sps"]; state["sps"] ^= 1
                pi = state["p"]; state["p"] ^= 1
                sv = bank(pj)[0:nk, 0:128]
                pt = Pt[pi]

                def qk(sv=sv, g=g, pj=pj, pi=pi, pt=pt):
                    T(lambda e: e.matmul(sv, lhsT=KT_kb[:, g, kcol:kcol + nk], rhs=qbT[:, 4 * g:4 * g + 4, q0:q0 + 32], start=True, stop=False),
                      kreads + ["qbT", "qbT_aug", "KT_ones"], ["ps%d" % pj])
                    T(lambda e: e.matmul(sv, lhsT=mb[:, kcol:kcol + nk], rhs=Rb[:, b, :], start=False, stop=(bias is None)),
                      ["mb", "Rb"], ["ps%d" % pj])
                    if bias is not None:
                        T(lambda e: e.matmul(sv, lhsT=ident_b[0:nk, 0:nk], rhs=Bt[0:nk, bias, 4 * g:4 * g + 4, 0:32], start=False, stop=True),
                          ["Bt", "ident_b"], ["ps%d" % pj])
                    A(lambda e: e.activation(out=pt[0:nk, 0:128], in_=sv, func=AF.Exp), ["ps%d" % pj], ["P%d" % pi])

                def pv(g=g, pi=pi, pt=pt):
                    T(lambda e: e.matmul(bank(ob)[:, 258 + g * 65:258 + (g + 1) * 65], lhsT=pt[0:nk, 0:128], rhs=KA_v[0:nk, blk, g, :],
                                        start=False, stop=False, skip_group_check=True),
                      ["P%d" % pi, "KA_ones"] + kreads, ["ps%d" % ob])
                push((qk, pv))

        def finalize(b):
            ob = 2 + (b % 2)
            obn_ = "ps%d" % ob
            V(lambda e: e.reciprocal(out=rc[:, 0:2], in_=bank(ob)[:, 0:258].rearrange("p (h c) -> p h c", h=2)[:, :, 128]), [obn_], ["rc"])
            V(lambda e: e.reciprocal(out=rc[:, 2:4], in_=bank(ob)[:, 258:388].rearrange("p (h c) -> p h c", h=2)[:, :, 64]), [obn_], ["rc"])
            for hg in range(2):
                V(lambda e, hg=hg: e.tensor_scalar(out=on_b[:, hg, :], in0=bank(ob)[:, hg * 129:hg * 129 + 128], scalar1=rc[:, hg:hg + 1], scalar2=None, op0=ALU.mult),
                  [obn_, "rc"], ["olat_b"])
            for g in range(2):
                V(lambda e, g=g: e.tensor_scalar(out=obn[:, g, :], in0=bank(ob)[:, 258 + g * 65:258 + g * 65 + 64], scalar1=rc[:, 2 + g:3 + g], scalar2=None, op0=ALU.mult),
                  [obn_, "rc"], ["olat_b"])
            pv4 = bankb(4)
            for hg in range(2):
                T(lambda e, hg=hg: e.transpose(pv4[:, hg * 128:(hg + 1) * 128], on_b[:, hg, :], ident_b[:]), ["olat_b", "ident_b"], ["ps4"])
            for hg in range(2):
                V(lambda e, hg=hg: e.tensor_copy(out=olatT_S[:, 4 * hg:4 * hg + 4, 32 * b:32 * b + 32],
                                                 in_=pv4[:, hg * 128:(hg + 1) * 128].rearrange("p (h q) -> p h q", h=4)), ["ps4"], ["hT"])
            dst = bank(5)[32 * b:32 * b + 32, :].rearrange("p (g h d) -> p g h d", g=2, h=4)
            for hl in range(4):
                for g in range(2):
                    T(lambda e, hl=hl, g=g: e.matmul(dst[:, g, hl, :], lhsT=ident_b[:, hl * 32:(hl + 1) * 32], rhs=obn[:, g, :], start=True, stop=True,
                                                     tile_position=(0, 32 * b), skip_group_check=True), ["olat_b", "ident_b"], ["ps5"])

        def flush():
            if pend[0] is not None:
                pend[0][1]()
                pend[0] = None

        LAG = 3

        def do_load(b, blk):
            if blk < 32:
                load_block(b, blk)
            else:
                new_keys(b)

        def b0_mla():
            for blk in range(33):
                do_load(0, blk)
                if blk >= LAG:
                    att(0, blk - LAG, "mla")
            for blk in range(33 - LAG, 33):
                att(0, blk, "mla")
            flush()

        def sample_bisect():
            for _ in bisect(4128, True):
                pass
        emit_merged(capture(sample_bisect), capture(b0_mla))
        loads = [(b, blk) for b in range(1, 4) for blk in range(33)]
        order = []
        for blk in range(33):
            order.append(("dsa0", 0, blk))
            if blk >= 1:
                order.append(("load",) + loads[blk - 1])
            if blk >= 1 + LAG:
                order.append(("att",) + loads[blk - 1 - LAG])
        for n in range(32, len(loads)):
            order.append(("load",) + loads[n])
            order.append(("att",) + loads[n - LAG])
        for n in range(len(loads) - LAG, len(loads)):
            order.append(("att",) + loads[n])
        for it in order:
            if it[0] == "load":
                do_load(it[1], it[2])
            elif it[0] == "dsa0":
                att(0, it[2], "dsa")
                if it[2] == 32:
                    flush()
                    finalize(0)
            else:
                att(it[1], it[2])
                if it[2] == 32:
                    flush()
                    finalize(it[1])
        combine(y_s, 0, sample=True)

    V(lambda e: e.memset(bis[:, 7:8], 0.0), [],
      ["sc", "mb", "mbj_d", "mbj_a", "cache_f0", "cache_f1", "cache_b0", "cache_b1"] + ALIAS_BUFS +
      ["KT%d" % k for k in range(16, 33)] + ["KA_ckv%d" % k for k in range(16, 33)] + ["KA_v%d" % k for k in range(16, 33)])
    V(lambda e: e.tensor_copy(out=qbT1[64:65, :, :], in_=cfs[64:65, :].unsqueeze(2).to_broadcast([1, 8, 128])), ["smallc"], ["qbT_aug1"])
    V(lambda e: e.tensor_copy(out=qbT2[64:65, :, :], in_=cfs[64:65, :].unsqueeze(2).to_broadcast([1, 8, 128])), ["smallc"], ["qbT_aug2"])
    V(lambda e: e.memset(qpeT1[32:64, :, :], 0.0), [], ["kpe_pad1"])
    V(lambda e: e.memset(qpeT1[64:128, :, :], 0.0), [], ["kpe_pad1"])

    def stage_a(i):
        cur.update(cfg_p(i))
        proj_tile(xp[i * 128:(i + 1) * 128, :], cos_p[:, i, :], sin_p[:, i, :], o_p, i * 128, i, False)
        sc_, scn = cur["sc"]
        N = 128 * (i + 1)
        for c in range((N + 511) // 512):
            n = min(512, N - c * 512)
            for h in range(8):
                pj = 5 + h % 2
                T(lambda e, h=h, pj=pj, c=c, n=n: e.matmul(bank(pj)[:, 0:n], lhsT=qiT[:, h, :], rhs=KT_kidx[:, c * 512:c * 512 + n], start=True, stop=True),
                  ["qiT", "idx_pad"] + ["KT%d" % bb for bb in range(c * 4, min(c * 4 + 4, i + 1))], ["ps%d" % pj])
                idx_epilogue(pj, h, c * 512, n)
        V(lambda e, i=i: e.memset(sc_[0:64, i * 128 + 64:i * 128 + 128], -1e30), [scn], [scn])

    def do_bisect(i):
        cur.update(cfg_p(i))
        for _ in bisect(128 * (i + 1), i >= 2):
            pass

    def do_mla(i):
        cur.update(cfg_p(i))
        mu = []
        for kb in range(i + 1):
            mu += attend_block(0, 128, kb * 128, 128, kb, kb == 0, None, kb == i, False, None, None,
                               ["KT%d" % kb, "KA_ckv%d" % kb, "KA_v%d" % kb], which="mla")
        for _ in run_units(mu):
            pass
        mla_finish()

    def do_dsa_combine(i):
        cur.update(cfg_p(i))
        du = []
        for kb in range(i + 1):
            diag = (kb == i)
            need_mask = (i >= 2) or diag
            bias = 0 if diag else (1 if kb == i - 1 else None)
            du += attend_block(0, 128, kb * 128, 128, kb, kb == 0, None, diag, need_mask, bias, None,
                               ["KT%d" % kb, "KA_ckv%d" % kb, "KA_v%d" % kb], which="dsa")
        for _ in run_units(du):
            pass
        combine(y_p, i * 128)

    def emit_merged_k(lists):
        lists = [l for l in lists if l]
        pos = [0] * len(lists)
        while True:
            best = None
            for k, l in enumerate(lists):
                if pos[k] < len(l):
                    f = pos[k] / float(len(l))
                    if best is None or f < best[0]:
                        best = (f, k)
            if best is None:
                break
            k = best[1]
            it = lists[k][pos[k]]; pos[k] += 1
            if it[0] == "op":
                S.op(it[1], it[2], it[3], it[4])
            else:
                S.dma(it[4], it[1], it[2], it[3])

    if NT:
        stage_a(0)
        do_bisect(0)
        do_mla(0)
        if NT > 1:
            stage_a(1)
    for j in range(NT):
        def main_chain(j=j):
            do_dsa_combine(j)
            if j + 1 < NT:
                do_mla(j + 1)
        l_main = capture(main_chain)
        l_bis = capture(lambda: do_bisect(j + 1)) if j + 1 < NT else []
        l_proj = capture(lambda: stage_a(j + 2)) if j + 2 < NT else []
        emit_merged_k([l_main, l_bis, l_proj])

    outs = [B(x) for x in ["xo", "ckv_f", "kpe_f", "kv_f"]]
    S.final_waits("sync", outs)
    print('total ops', S.nop, {e: S.count[e] for e in ENGS})
    S.build(st)
    st.close()
    return nc


def _t5_bucket(rel):
    rel = np.asarray(rel, np.int64)
    nb, me = 16, 8
    ret = (rel > 0).astype(np.int64) * nb
    n = np.abs(rel)
    nf = np.maximum(n, 1).astype(np.float32)
    large = me + (np.log(nf / np.float32(me)) / np.float32(math.log(128 / 8)) * np.float32(nb - me)).astype(np.int32)
    large = np.minimum(large, nb - 1)
    return ret + np.where(n < me, n, large)


def _rope_tables(pos):
    half = 16
    inv = (np.float32(10000.0) ** (-np.arange(half, dtype=np.float32) / np.float32(half))).astype(np.float32)
    ang = pos.astype(np.float32)[:, None] * inv[None, :]
    return np.cos(ang).astype(np.float32), np.sin(ang).astype(np.float32)


_CACHE = {}


def make_in_maps(x_prompt, x_sample, cache_mla_ckv, cache_mla_kpe, cache_dsa_k, cache_dsa_v, cache_dsa_kidx,
                 norm_g, w_in, mla_q_norm_g, mla_kv_norm_g, mla_w_uq, mla_w_uk, mla_w_uv, rel_bias, w_out, final_norm_g):
    f = lambda a: np.ascontiguousarray(np.asarray(a, dtype=np.float32))
    x_prompt, x_sample = f(x_prompt), f(x_sample)
    rel_bias = f(rel_bias)
    cos_p, sin_p = _rope_tables(np.arange(2048))
    cos_s, sin_s = _rope_tables(4096 + (np.arange(128) % 32))
    kl = np.arange(128)[:, None]; ql = np.arange(128)[None, :]
    tb = np.stack([rel_bias[_t5_bucket(kl - ql)], rel_bias[_t5_bucket(kl - 128 - ql)]])
    tb = f(tb.transpose(0, 1, 3, 2).reshape(2, 128, 1024))
    shared = {
        "normg": f(f(norm_g).reshape(8, 128).T), "w_in": f(w_in[0]), "qng": f(f(mla_q_norm_g).reshape(2, 128).T),
        "kvng_bc": f(np.broadcast_to(f(mla_kv_norm_g).reshape(1, 128), (128, 128))),
        "w_uq": f(f(mla_w_uq)[0].reshape(256, 768)), "w_uk": f(f(mla_w_uk)[0].reshape(128, 512)),
        "w_uv": f(f(mla_w_uv)[0].reshape(128, 512)), "w_out": f(f(w_out)[0]),
        "fng_bc": f(np.broadcast_to(f(final_norm_g).reshape(1, 1024), (128, 1024))),
        "cfar_bc": f(np.broadcast_to(rel_bias[15:16, :], (128, 8))), "tbias": tb,
        "cos_p": cos_p, "sin_p": sin_p, "cos_s": cos_s, "sin_s": sin_s,
    }
    in_maps = []
    for c in range(8):
        m = dict(shared)
        m["xp"] = x_prompt[c]
        m["xs"] = x_sample[4 * c:4 * c + 4].reshape(128, 1024)
        m["c_ckv"] = f(cache_mla_ckv[0, 4 * c:4 * c + 4]); m["c_kpe"] = f(cache_mla_kpe[0, 4 * c:4 * c + 4])
        m["c_k"] = f(cache_dsa_k[0, 4 * c:4 * c + 4]).reshape(4, 4096, 128)
        m["c_v"] = f(cache_dsa_v[0, 4 * c:4 * c + 4]).reshape(4, 4096, 128)
        m["c_kidx"] = f(cache_dsa_kidx[0, 4 * c:4 * c + 4])
        in_maps.append(m)
    return in_maps


def kernel(**inputs):
    if "nc" not in _CACHE:
        _CACHE["nc"] = build_program()
    nc = _CACHE["nc"]
    in_maps = make_in_maps(**inputs)
    res = run_bass_kernel_spmd(nc, in_maps, core_ids=list(range(8)))
    R = res.results
    cat = lambda k: np.stack([R[c][k] for c in range(8)])
    y_p = cat("y_p")
    y_s = cat("y_s").reshape(32, 32, 1024)
    outs = [y_p, y_s]
    for k, shp in [("o_ckv_p", (1, 8, 2048, 128)), ("o_kpe_p", (1, 8, 2048, 32)), ("o_k_p", (1, 8, 2048, 2, 64)),
                   ("o_v_p", (1, 8, 2048, 2, 64)), ("o_kidx_p", (1, 8, 2048, 64))]:
        outs.append(cat(k).reshape(shp))
    for k, shp in [("o_ckv_s", (1, 32, 32, 128)), ("o_kpe_s", (1, 32, 32, 32)), ("o_k_s", (1, 32, 32, 2, 64)),
                   ("o_v_s", (1, 32, 32, 2, 64)), ("o_kidx_s", (1, 32, 32, 64))]:
        outs.append(cat(k).reshape(shp))
    return tuple(np.ascontiguousarray(o.astype(np.float32)) for o in outs)
```
